# Optimizing a Trainium2 kernel written in Bass

```python
import math
import jax
import jax.numpy as jnp
from jax import lax
import numpy as np


D_MODEL = 1024
BATCH = 8
SEQ = 4096
DEPTH = 4

GRID_W = 64
CTX_LEN = 256
N_MIXERS = 3
EXPAND = 2
D_INNER = EXPAND * D_MODEL
NORM_EPS = 1e-6

RWKV_HEAD = 64
RWKV_HEADS = D_INNER // RWKV_HEAD
RWKV_DECAY_LORA = 64
RWKV_ICLR_LORA = 64
RWKV_VRES_LORA = 32
RWKV_N_MU = 6
RWKV_GN_EPS = 64e-5

HYENA_EMB = 33
HYENA_FILTER_WIDTH = 64
HYENA_SIN_W = 1.0
HYENA_FAST_DECAY = 0.3
HYENA_SLOW_DECAY = 1.5
HYENA_TARGET = 1e-2

HGRN_KEY = 128
HGRN_HEADS = D_INNER // HGRN_KEY
HGRN_VAL = D_INNER // HGRN_HEADS
HGRN_CHUNK = 32

kernel_name = 'hybrid_rwkv7_hyena_hgrn2_prefix_backbone'


def rms_norm(x, g, eps=NORM_EPS):
    xf = x.astype(jnp.float32)
    y = xf * lax.rsqrt(jnp.mean(xf * xf, axis=-1, keepdims=True) + eps)
    return (y * g.astype(jnp.float32)).astype(x.dtype)


def adaln(cond, w, b):
    return jnp.split(jax.nn.silu(cond) @ w + b, 3, axis=-1)


def split_heads(a, n_heads):
    return a.reshape(a.shape[:-1] + (n_heads, a.shape[-1] // n_heads))


def merge_heads(a):
    return a.reshape(a.shape[:-2] + (a.shape[-2] * a.shape[-1],))


def grid_shift(h):
    b, t, d = h.shape
    rows = t // GRID_W
    g = h.reshape(b, rows, GRID_W, d)
    q = d // 4
    left = jnp.pad(g[:, :, :-1, :q], ((0, 0), (0, 0), (1, 0), (0, 0)))
    right = jnp.pad(g[:, :, 1:, q:2 * q], ((0, 0), (0, 0), (0, 1), (0, 0)))
    up = jnp.pad(g[:, :-1, :, 2 * q:3 * q], ((0, 0), (1, 0), (0, 0), (0, 0)))
    down = jnp.pad(g[:, 1:, :, 3 * q:], ((0, 0), (0, 1), (0, 0), (0, 0)))
    return jnp.concatenate([left, right, up, down], axis=-1).reshape(b, t, d)


def seq_shift(h):
    half = h.shape[-1] // 2
    prev = jnp.pad(h[:, :-1, :half], ((0, 0), (1, 0), (0, 0)))
    nxt = jnp.pad(h[:, 1:, half:], ((0, 0), (0, 1), (0, 0)))
    return jnp.concatenate([prev, nxt], axis=-1)


def rwkv7_col_groups(vres):
    widths = [D_INNER] * 4 + [RWKV_DECAY_LORA] * 2 + [RWKV_ICLR_LORA] * 2
    groups = [0, 1, 2, 3, 4, 4, 5, 5]
    if vres:
        widths.append(RWKV_VRES_LORA)
        groups.append(2)
    return np.repeat(np.asarray(groups), np.asarray(widths))


def rwkv7_prepare(h, h_shift, p, v_first):
    e = D_INNER
    vres = 'v0' in p
    mu_cols = p['mu'][rwkv7_col_groups(vres)].T
    proj = h @ p['w_in'] + (h_shift - h) @ (p['w_in'] * mu_cols)
    r, k, v, gate = (proj[..., j * e:(j + 1) * e] for j in range(4))
    lo = proj[..., 4 * e:]
    nd, na = 2 * RWKV_DECAY_LORA, 2 * RWKV_ICLR_LORA
    w_lo = split_heads(lo[..., :nd], 2)
    a_lo = split_heads(lo[..., nd:nd + na], 2)
    if vres:
        v = v + (v_first - v) * jax.nn.sigmoid(p['v0'] + lo[..., nd + na:] @ p['v2'])
    else:
        v_first = v
    kk = split_heads(k * p['k_k'], RWKV_HEADS).astype(jnp.float32)
    kk = kk / jnp.maximum(jnp.sqrt(jnp.sum(kk * kk, axis=-1, keepdims=True)), 1e-12)
    return (split_heads(r, RWKV_HEADS), k, split_heads(v, RWKV_HEADS), gate, kk, w_lo, a_lo, v_first)


def rwkv7_direction(k, w_lo, a_lo, p, z):
    w_raw = -jax.nn.softplus(-(p['w0'][z] + jnp.tanh(w_lo[..., z, :]) @ p['w2'][z])) - 0.5
    decay = jnp.exp(-jnp.exp(w_raw.astype(jnp.float32)))
    a = jax.nn.sigmoid(p['a0'][z] + a_lo[..., z, :] @ p['a2'][z])
    k_dir = k * (1.0 + (a - 1.0) * p['k_a'])
    return (split_heads(decay, RWKV_HEADS), split_heads(k_dir, RWKV_HEADS), split_heads(a, RWKV_HEADS))


def rwkv7_scan(r, decay, k, v, kk, a, s0, reverse):
    xs = tuple(jnp.moveaxis(t.astype(jnp.float32), 1, 0) for t in (r, decay, k, v, -kk, kk * a))

    def step(s, inp):
        r_t, w_t, k_t, v_t, a_t, b_t = inp
        sa = jnp.einsum('bhvk,bhk->bhv', s, a_t)
        s = s * w_t[:, :, None, :] + sa[..., None] * b_t[:, :, None, :] + v_t[..., None] * k_t[:, :, None, :]
        return s, jnp.einsum('bhvk,bhk->bhv', s, r_t)

    s_last, ys = lax.scan(step, s0, xs, reverse=reverse)
    return jnp.moveaxis(ys, 0, 1), s_last


def rwkv7_readout(y, bonus, gate, p, dtype):
    mean = jnp.mean(y, axis=-1, keepdims=True)
    var = jnp.mean(jnp.square(y - mean), axis=-1, keepdims=True)
    yn = merge_heads((y - mean) * lax.rsqrt(var + RWKV_GN_EPS))
    yn = yn * p['ln_w'].astype(jnp.float32) + p['ln_b'].astype(jnp.float32)
    return ((yn + merge_heads(bonus)) * jax.nn.silu(gate.astype(jnp.float32))).astype(dtype)


def rwkv7_mixer(hx, hc, p, v_first_x, v_first_c, ctx_out):
    rx, kx, vx, gx, kkx, wlx, alx, v_first_x = rwkv7_prepare(hx, grid_shift(hx), p, v_first_x)
    rc, kc, vc, gc, kkc, wlc, alc, v_first_c = rwkv7_prepare(hc, seq_shift(hc), p, v_first_c)
    s0 = jnp.zeros((hx.shape[0], RWKV_HEADS, RWKV_HEAD, RWKV_HEAD), jnp.float32)
    r_k = p['r_k'].astype(jnp.float32)
    yx = bx = yc = bc = 0.0
    for z in range(2):
        dc, kdc, ac = rwkv7_direction(kc, wlc, alc, p, z)
        dx, kdx, ax = rwkv7_direction(kx, wlx, alx, p, z)
        yc_z, s_ctx = rwkv7_scan(rc, dc, kdc, vc, kkc, ac, s0, z == 1)
        yx_z, _ = rwkv7_scan(rx, dx, kdx, vx, kkx, ax, s_ctx, z == 1)
        yx = yx + yx_z
        bx = bx + jnp.sum(rx * kdx * r_k, axis=-1, keepdims=True) * vx
        if ctx_out:
            yc = yc + yc_z
            bc = bc + jnp.sum(rc * kdc * r_k, axis=-1, keepdims=True) * vc
    ux = rwkv7_readout(yx, bx, gx, p, hx.dtype)
    uc = rwkv7_readout(yc, bc, gc, p, hc.dtype) if ctx_out else None
    return ux, uc, v_first_x, v_first_c


def hyena_filters(length, p):
    f32 = jnp.float32
    t = jnp.linspace(0.0, 1.0, length, dtype=f32)[:, None]
    bands = (HYENA_EMB - 1) // 2
    freqs = jnp.linspace(1e-4, bands - 1, bands, dtype=f32)[None, :]
    ang = (2.0 * math.pi / length) * jnp.arange(length, dtype=f32)[:, None] * freqs
    z = jnp.concatenate([t, jnp.cos(ang), -jnp.sin(ang)], axis=-1)
    sf = p['sin_freq'].astype(f32)
    hdn = jnp.sin(sf * (z @ p['f_w1'].astype(f32) + p['f_b1'].astype(f32)))
    hdn = jnp.sin(sf * (hdn @ p['f_w2'].astype(f32) + p['f_b2'].astype(f32)))
    hdn = jnp.sin(sf * (hdn @ p['f_w3'].astype(f32) + p['f_b3'].astype(f32)))
    filt = hdn @ p['f_w4'].astype(f32)
    deltas = jnp.abs(jnp.linspace(math.log(HYENA_TARGET) / HYENA_SLOW_DECAY,
                                  math.log(HYENA_TARGET) / HYENA_FAST_DECAY, D_INNER, dtype=f32))
    window = jnp.exp(-t * deltas)
    return filt.reshape(length, 2, D_INNER) * window[:, None, :]


def bidir_long_conv(u, filt, skip):
    length = u.shape[1]
    hf, hb = filt[:, 0], filt[:, 1]
    k = jnp.concatenate([hf[:1] + hb[:1], hf[1:], jnp.zeros_like(hf[:1]), hb[:0:-1]], axis=0)
    n = 2 * length
    uf32 = u.astype(jnp.float32)
    uf = jnp.fft.rfft(uf32, n=n, axis=1)
    kf = jnp.fft.rfft(k, n=n, axis=0)
    y = jnp.fft.irfft(uf * kf[None], n=n, axis=1)[:, :length]
    return (y + uf32 * skip.astype(jnp.float32)).astype(u.dtype)


def centred_dwconv3(u, w, b):
    c = u.shape[-1]
    y = lax.conv_general_dilated(u, w[:, None, :].astype(u.dtype), window_strides=(1,), padding=((1, 1),),
                                 dimension_numbers=('NWC', 'WIO', 'NWC'), feature_group_count=c)
    return y + b


def hyena_sequence(h, p):
    e = D_INNER
    proj = h @ p['w_in']
    s = centred_dwconv3(proj[..., :3 * e], p['conv_w'], p['conv_b'])
    x0, x1, v = s[..., :e], s[..., e:2 * e], s[..., 2 * e:]
    v = bidir_long_conv(v * x1, hyena_filters(h.shape[1], p), p['filter_bias'])
    return v * x0 * jax.nn.silu(proj[..., 3 * e:])


def hyena_mixer(hx, hc, p, ctx_out):
    ux = hyena_sequence(hx, p)
    uc = hyena_sequence(hc, p) if ctx_out else None
    return ux, uc


def hgrn2_prepare(h, p, lb):
    e = D_INNER
    proj = h @ p['w_in']
    q = split_heads(jax.nn.silu(proj[..., :e]), HGRN_HEADS)
    i = split_heads(proj[..., 3 * e:4 * e], HGRN_HEADS)
    gate = proj[..., 4 * e:]
    dirs = []
    for z in range(2):
        f = lb + (1.0 - lb) * jax.nn.sigmoid(proj[..., (1 + z) * e:(2 + z) * e].astype(jnp.float32))
        dirs.append((split_heads(1.0 - f, HGRN_HEADS), split_heads(jnp.log(f), HGRN_HEADS)))
    return q, i, gate, dirs


def gla_chunkwise(q, k, v, log_f, s0):
    b, t, h, _ = q.shape
    dv = v.shape[-1]
    n = t // HGRN_CHUNK

    def chunks(a):
        return a.astype(jnp.float32).reshape(b, n, HGRN_CHUNK, h, a.shape[-1]).transpose(1, 0, 3, 2, 4)

    causal = jnp.asarray(np.tril(np.ones((HGRN_CHUNK, HGRN_CHUNK), dtype=bool)))[:, :, None]

    def step(s, inp):
        qc, kc, vc, gc = inp
        cum = jnp.cumsum(gc, axis=2)
        last = cum[:, :, -1:, :]
        o = jnp.einsum('bhck,bhkv->bhcv', qc * jnp.exp(cum), s)
        rel = jnp.exp(jnp.where(causal, cum[:, :, :, None, :] - cum[:, :, None, :, :], -jnp.inf))
        att = jnp.einsum('bhtk,bhsk,bhtsk->bhts', qc, kc, rel)
        o = o + jnp.einsum('bhts,bhsv->bhtv', att, vc)
        s = jnp.exp(last[:, :, 0, :, None]) * s + jnp.einsum('bhsk,bhsv->bhkv', kc * jnp.exp(last - cum), vc)
        return s, o

    s_last, o = lax.scan(step, s0, (chunks(q), chunks(k), chunks(v), chunks(log_f)))
    return o.transpose(1, 0, 3, 2, 4).reshape(b, t, h, dv), s_last


def gla_direction(q, k, v, log_f, s0, reverse):
    if reverse:
        o, s = gla_chunkwise(jnp.flip(q, 1), jnp.flip(k, 1), jnp.flip(v, 1), jnp.flip(log_f, 1), s0)
        return jnp.flip(o, 1), s
    return gla_chunkwise(q, k, v, log_f, s0)


def hgrn2_readout(o, gate, g_norm, dtype):
    o = o * lax.rsqrt(jnp.mean(o * o, axis=-1, keepdims=True) + NORM_EPS) * g_norm.astype(jnp.float32)
    return (merge_heads(o) * jax.nn.silu(gate.astype(jnp.float32))).astype(dtype)


def hgrn2_mixer(hx, hc, p, lb, ctx_out):
    qx, ix, gx, dirs_x = hgrn2_prepare(hx, p, lb)
    qc, ic, gc, dirs_c = hgrn2_prepare(hc, p, lb)
    s0 = jnp.zeros((hx.shape[0], HGRN_HEADS, HGRN_KEY, HGRN_VAL), jnp.float32)
    ox = oc = 0.0
    for z in range(2):
        oc_z, s_ctx = gla_direction(qc, dirs_c[z][0], ic, dirs_c[z][1], s0, z == 1)
        ox_z, _ = gla_direction(qx, dirs_x[z][0], ix, dirs_x[z][1], s_ctx, z == 1)
        ox = ox + ox_z
        if ctx_out:
            oc = oc + oc_z
    ux = hgrn2_readout(ox, gx, p['g_norm'], hx.dtype)
    uc = hgrn2_readout(oc, gc, p['g_norm'], hc.dtype) if ctx_out else None
    return ux, uc


def setup_inputs(seed: int = 0) -> dict:
    key = jax.random.key(seed)
    keys = iter(jax.random.split(key, 96))
    f32 = jnp.float32
    e, d = D_INNER, D_MODEL

    def normal(shape, scale):
        return jax.random.normal(next(keys), shape, f32) * scale

    def near_one(shape):
        return 1.0 + normal(shape, 0.05)

    def rwkv(prefix, vres):
        width = 4 * e + 2 * RWKV_DECAY_LORA + 2 * RWKV_ICLR_LORA + (RWKV_VRES_LORA if vres else 0)
        out = {
            prefix + 'w_in': normal((d, width), d ** -0.5),
            prefix + 'mu': jax.random.uniform(next(keys), (RWKV_N_MU, d), f32),
            prefix + 'w0': jnp.linspace(-6.5, -1.5, e, dtype=f32)[None, :] + normal((2, e), 0.1),
            prefix + 'w2': normal((2, RWKV_DECAY_LORA, e), 0.1),
            prefix + 'a0': normal((2, e), 0.1),
            prefix + 'a2': normal((2, RWKV_ICLR_LORA, e), 0.1),
            prefix + 'k_k': 0.85 + normal((e,), 0.05),
            prefix + 'k_a': near_one((e,)),
            prefix + 'r_k': normal((RWKV_HEADS, RWKV_HEAD), 0.1),
            prefix + 'ln_w': near_one((e,)),
            prefix + 'ln_b': normal((e,), 0.02),
        }
        if vres:
            out[prefix + 'v0'] = near_one((e,))
            out[prefix + 'v2'] = normal((RWKV_VRES_LORA, e), 0.1)
        return out

    inputs = {
        'x': normal((BATCH, SEQ, d), 1.0),
        'c': normal((BATCH, d), 1.0),
        'ctx': normal((BATCH, CTX_LEN, d), 1.0),
        'c_ctx': normal((d,), 1.0),
        'ada_w': normal((DEPTH, d, 3 * d), 0.5 * d ** -0.5),
        'ada_b': normal((DEPTH, 3 * d), 0.02),
        'norm_pre': near_one((DEPTH, d)),
        'norm_post': near_one((DEPTH, d)),
        'w_out': normal((DEPTH, e, d), e ** -0.5),
    }
    inputs.update(rwkv('l0_', False))
    inputs.update({
        'l1_w_in': normal((d, 4 * e), d ** -0.5),
        'l1_conv_w': normal((3, 3 * e), 3 ** -0.5),
        'l1_conv_b': normal((3 * e,), 0.02),
        'l1_f_w1': normal((HYENA_EMB, HYENA_FILTER_WIDTH), HYENA_EMB ** -0.5),
        'l1_f_b1': normal((HYENA_FILTER_WIDTH,), 0.1),
        'l1_f_w2': normal((HYENA_FILTER_WIDTH, HYENA_FILTER_WIDTH), HYENA_FILTER_WIDTH ** -0.5),
        'l1_f_b2': normal((HYENA_FILTER_WIDTH,), 0.1),
        'l1_f_w3': normal((HYENA_FILTER_WIDTH, HYENA_FILTER_WIDTH), HYENA_FILTER_WIDTH ** -0.5),
        'l1_f_b3': normal((HYENA_FILTER_WIDTH,), 0.1),
        'l1_f_w4': normal((HYENA_FILTER_WIDTH, 2 * e), HYENA_FILTER_WIDTH ** -0.5),
        'l1_sin_freq': HYENA_SIN_W * near_one((HYENA_FILTER_WIDTH,)),
        'l1_filter_bias': normal((e,), 1.0),
    })
    inputs.update({
        'l2_w_in': normal((d, 5 * e), d ** -0.5),
        'l2_g_norm': near_one((HGRN_VAL,)),
        'hgrn_lb_logits': normal((DEPTH, e), 1.0),
    })
    inputs.update(rwkv('l3_', True))
    return inputs


def reference(x, c, ctx, c_ctx, ada_w, ada_b, norm_pre, norm_post, w_out,
              l0_w_in, l0_mu, l0_w0, l0_w2, l0_a0, l0_a2, l0_k_k, l0_k_a, l0_r_k, l0_ln_w, l0_ln_b,
              l1_w_in, l1_conv_w, l1_conv_b, l1_f_w1, l1_f_b1, l1_f_w2, l1_f_b2, l1_f_w3, l1_f_b3,
              l1_f_w4, l1_sin_freq, l1_filter_bias,
              l2_w_in, l2_g_norm, hgrn_lb_logits,
              l3_w_in, l3_mu, l3_w0, l3_w2, l3_a0, l3_a2, l3_k_k, l3_k_a, l3_r_k, l3_ln_w, l3_ln_b,
              l3_v0, l3_v2):
    rwkv0 = dict(w_in=l0_w_in, mu=l0_mu, w0=l0_w0, w2=l0_w2, a0=l0_a0, a2=l0_a2, k_k=l0_k_k,
                 k_a=l0_k_a, r_k=l0_r_k, ln_w=l0_ln_w, ln_b=l0_ln_b)
    hyena1 = dict(w_in=l1_w_in, conv_w=l1_conv_w, conv_b=l1_conv_b, f_w1=l1_f_w1, f_b1=l1_f_b1,
                  f_w2=l1_f_w2, f_b2=l1_f_b2, f_w3=l1_f_w3, f_b3=l1_f_b3, f_w4=l1_f_w4,
                  sin_freq=l1_sin_freq, filter_bias=l1_filter_bias)
    hgrn2 = dict(w_in=l2_w_in, g_norm=l2_g_norm)
    rwkv3 = dict(w_in=l3_w_in, mu=l3_mu, w0=l3_w0, w2=l3_w2, a0=l3_a0, a2=l3_a2, k_k=l3_k_k,
                 k_a=l3_k_a, r_k=l3_r_k, ln_w=l3_ln_w, ln_b=l3_ln_b, v0=l3_v0, v2=l3_v2)
    layer_params = [rwkv0, hyena1, hgrn2, rwkv3]

    lb_soft = jax.nn.softmax(hgrn_lb_logits.astype(jnp.float32), axis=0)
    lower_bounds = jnp.cumsum(lb_soft, axis=0) - lb_soft[0]

    v_first_x = None
    v_first_c = None
    for l in range(DEPTH):
        kind = l % N_MIXERS
        p = layer_params[l]
        ctx_out = l < DEPTH - 1
        sh_x, sc_x, gt_x = adaln(c[:, None, :], ada_w[l], ada_b[l])
        sh_c, sc_c, gt_c = adaln(c_ctx[None, None, :], ada_w[l], ada_b[l])
        hx = rms_norm(x, norm_pre[l]) * (1.0 + sc_x) + sh_x
        hc = rms_norm(ctx, norm_pre[l]) * (1.0 + sc_c) + sh_c
        if kind == 0:
            ux, uc, v_first_x, v_first_c = rwkv7_mixer(hx, hc, p, v_first_x, v_first_c, ctx_out)
        elif kind == 1:
            ux, uc = hyena_mixer(hx, hc, p, ctx_out)
        else:
            ux, uc = hgrn2_mixer(hx, hc, p, lower_bounds[l], ctx_out)
        x = x + rms_norm(ux @ w_out[l], norm_post[l]) * gt_x
        if ctx_out:
            ctx = ctx + rms_norm(uc @ w_out[l], norm_post[l]) * gt_c
    return x
```

```python
import math
import os
CUT = int(os.environ.get('KCUT', '99'))
KHP = int(os.environ.get('KHP', '16'))
KDBG = int(os.environ.get('KDBG', '0'))
from contextlib import ExitStack
import numpy as np
import concourse.bass as bass
import concourse.mybir as mybir
from concourse.bass_utils import run_bass_kernel_spmd

F32 = mybir.dt.float32
BF16 = mybir.dt.bfloat16
I32 = mybir.dt.int32
AF = mybir.ActivationFunctionType
ALU = mybir.AluOpType

D = 1024
E = 2048
LX = 4096
LC = 256
T = LX + LC
NC8 = 8
EPS = 1e-6
TILES = [(0, 256)] + [(256 + 512 * i, 512) for i in range(8)]


class StopBuild(Exception):
    pass


class Buf:
    def __init__(self, t, name):
        self.t = t
        self.name = name
        self.w = {}
        self.r = {}

    def __getitem__(self, idx):
        return V(self, self.t[idx])


class V:
    def __init__(self, buf, ap):
        self.buf = buf
        self.ap = ap

    def __getitem__(self, idx):
        return V(self.buf, self.ap[idx])

    def re(self, pat, **kw):
        return V(self.buf, self.ap.rearrange(pat, **kw))

    def bc(self, shape):
        return V(self.buf, self.ap.to_broadcast(shape))


class Prog:
    NDMA = 40

    def __init__(self, nc):
        self.nc = nc
        self.stack = ExitStack()
        self.eng = {'pe': nc.tensor, 'dve': nc.vector, 'act': nc.scalar, 'pool': nc.gpsimd, 'sp': nc.sync}
        self.sem = {k: self.stack.enter_context(nc.semaphore("s_" + k)) for k in ('pe', 'dve', 'act', 'pool')}
        self.cnt = {k: 0 for k in self.sem}
        self.dsem = [self.stack.enter_context(nc.semaphore("d%d" % i)) for i in range(self.NDMA)]
        self.dval = [0] * self.NDMA
        self.dnext = 0
        self.seen = {e: {} for e in self.eng}
        self.nins = 0
        self.uid = 0

    def sb(self, shape, dt, stack=None, name=None):
        self.uid += 1
        name = (name or "t") + "_%d" % self.uid
        t = (stack or self.stack).enter_context(self.nc.sbuf_tensor(name, list(shape), dt))
        return Buf(t, name)

    def ps(self, shape, dt=F32, stack=None, name=None):
        self.uid += 1
        name = (name or "p") + "_%d" % self.uid
        t = (stack or self.stack).enter_context(self.nc.psum_tensor(name, list(shape), dt))
        return Buf(t, name)

    def dram(self, name, shape, dt, kind=None):
        if kind:
            t = self.nc.dram_tensor(name, list(shape), dt, kind=kind).ap()
        else:
            t = self.nc.dram_tensor(name, list(shape), dt).ap()
        return Buf(t, name)

    def _wait(self, e, key, val):
        if val <= 0 or (e == 'pe' and key == 'pe'):
            return
        if self.seen[e].get(key, 0) >= val:
            return
        sem = self.sem[key] if isinstance(key, str) else self.dsem[key]
        self.eng[e].wait_ge(sem, val)
        self.nins += 1
        self.seen[e][key] = val

    def _deps(self, e, reads, writes, partial):
        for b in reads:
            for k, v in b.w.items():
                self._wait(e, k, v)
        for b in writes:
            if not partial:
                for k, v in b.w.items():
                    self._wait(e, k, v)
            for k, v in b.r.items():
                self._wait(e, k, v)

    def _mark(self, key, val, reads, writes, partial):
        for b in reads:
            if b.r.get(key, 0) < val:
                b.r[key] = val
        for b in writes:
            if partial:
                b.w[key] = val
            else:
                b.w = {key: val}
                b.r = {}

    limit = None

    def op(self, e, fn, reads, writes, partial=False):
        if self.limit is not None:
            if self.limit <= 0:
                raise StopBuild()
            self.limit -= 1
        reads = [v.buf for v in reads if isinstance(v, V)]
        writes = [v.buf for v in writes]
        self._deps(e, reads, writes, partial)
        ins = fn(self.eng[e])
        self.cnt[e] += 1
        ins.then_inc(self.sem[e], 1)
        self.nins += 1
        self._mark(e, self.cnt[e], reads, writes, partial)

    def dma(self, q, out, in_, partial=True):
        self._deps(q, [in_.buf], [out.buf], partial)
        i = self.dnext
        self.dnext = (i + 1) % self.NDMA
        self._wait(q, i, self.dval[i])
        self.dval[i] += 16
        self.eng[q].dma_start(out=out.ap, in_=in_.ap).then_inc(self.dsem[i], 16)
        self.nins += 1
        self._mark(i, self.dval[i], [in_.buf], [out.buf], partial)

    def barrier(self):
        for e in self.eng:
            for k in self.sem:
                self._wait(e, k, self.cnt[k])
            for i in range(self.NDMA):
                self._wait(e, i, self.dval[i])

    def mm(self, out, lhsT, rhs, start=True, stop=True):
        self.op('pe', lambda e: e.matmul(out.ap, lhsT.ap, rhs.ap, start=start, stop=stop), [lhsT, rhs], [out], partial=True)

    def act(self, out, in_, func=AF.Identity, bias=None, scale=None, partial=False, eng='act'):
        kw = {}
        rd = [in_]
        if bias is not None:
            kw['bias'] = bias.ap if isinstance(bias, V) else bias
            rd.append(bias)
        if scale is not None:
            kw['scale'] = scale.ap if isinstance(scale, V) else scale
            rd.append(scale)
        self.op('act', lambda e: e.activation(out=out.ap, in_=in_.ap, func=func, **kw), rd, [out], partial)

    def tt(self, out, a, b, op, partial=False, eng='dve'):
        self.op(eng, lambda e: e.tensor_tensor(out=out.ap, in0=a.ap, in1=b.ap, op=op), [a, b], [out], partial)

    def ts(self, out, a, s1, s2, op0, op1=None, partial=False, eng='dve'):
        g = lambda s: s.ap if isinstance(s, V) else s
        if op1 is None:
            self.op(eng, lambda e: e.tensor_scalar(out=out.ap, in0=a.ap, scalar1=g(s1), scalar2=None, op0=op0), [a, s1], [out], partial)
        else:
            self.op(eng, lambda e: e.tensor_scalar(out=out.ap, in0=a.ap, scalar1=g(s1), scalar2=g(s2), op0=op0, op1=op1), [a, s1, s2], [out], partial)

    def stt(self, out, a, s, b, op0, op1, partial=False):
        g = s.ap if isinstance(s, V) else s
        self.op('dve', lambda e: e.scalar_tensor_tensor(out=out.ap, in0=a.ap, scalar=g, in1=b.ap, op0=op0, op1=op1), [a, s, b], [out], partial)

    def copy(self, out, in_, partial=False, eng='dve'):
        if eng == 'act':
            self.act(out, in_, AF.Identity, partial=partial)
        else:
            self.op(eng, lambda e: e.tensor_copy(out=out.ap, in_=in_.ap), [in_], [out], partial)

    def memset(self, out, val, eng='dve', partial=False):
        self.op(eng, lambda e: e.memset(out.ap, val), [], [out], partial)

    def rsqrt(self, out, in_, addc, partial=False):
        self.ts(out, in_, addc, None, ALU.add, partial=partial)
        self.act(out, out, AF.Ln, partial=partial)
        self.act(out, out, AF.Exp, scale=-0.5, partial=partial)

    def scan(self, out, d0, d1, init, op0, op1, partial=False):
        g = init.ap if isinstance(init, V) else init
        self.op('dve', lambda e: e.tensor_tensor_scan(out=out.ap, data0=d0.ap, data1=d1.ap, initial=g, op0=op0, op1=op1), [d0, d1, init], [out], partial)


def col_layout(v):
    v = np.asarray(v, np.float32)
    return np.ascontiguousarray(v.reshape(-1, 128).T)


def hyena_consts():
    c = {}
    f32 = np.float32
    for nm, L in (('x', LX), ('c', LC)):
        t = np.linspace(0.0, 1.0, L, dtype=f32)[:, None]
        freqs = np.linspace(1e-4, 15, 16, dtype=f32)[None, :]
        ang = (f32(2.0 * math.pi / L) * np.arange(L, dtype=f32)[:, None]) * freqs
        z = np.concatenate([t, np.cos(ang), -np.sin(ang)], axis=-1).astype(f32)
        c['zT' + nm] = np.ascontiguousarray(z.T)
        c['tn' + nm] = np.ascontiguousarray(-t[:, 0].reshape(L // 128, 128).T)
    deltas = np.abs(np.linspace(math.log(1e-2) / 1.5, math.log(1e-2) / 0.3, E, dtype=f32))
    c['deltab'] = np.ascontiguousarray(np.tile(deltas[None, :], (128, 1)).astype(f32))
    c['jrow'] = np.ascontiguousarray(np.tile(np.arange(128, dtype=f32)[None, :], (128, 1)))
    c['tcol'] = np.ascontiguousarray((np.arange(32, dtype=f32)[None, :] * 128 + np.arange(128, dtype=f32)[:, None]))
    alt = np.where(np.arange(128) % 2 == 0, 1.0, -1.0).astype(f32)
    c['altf'] = np.ascontiguousarray(np.stack([alt, alt], axis=1))
    c['altrf'] = np.ascontiguousarray(alt.reshape(1, 128))
    return c


def make_consts():
    c = {}
    idn = np.eye(128, dtype=np.float32)
    s = np.arange(128)
    blk = (s[:, None] // 64) == (s[None, :] // 64)
    sl = s % 64
    m = np.zeros((2, 128, 256), np.float32)
    m[0, :, :128] = blk & (sl[:, None] < sl[None, :])
    m[0, :, 128:] = blk & (sl[:, None] <= sl[None, :])
    m[1, :, :128] = blk & (sl[:, None] > sl[None, :])
    m[1, :, 128:] = blk & (sl[:, None] >= sl[None, :])
    c['masks'] = m
    mt = np.zeros((2, 128, 128), np.float32)
    mt[0] = m[0, :, :128].T
    mt[1] = m[1, :, :128].T
    c['maskt'] = mt
    c['idn'] = idn
    c['blk'] = blk.astype(np.float32)
    return c


class Ctx:
    pass


def adaln_phase(P, G, l):
    with ExitStack() as st:
        wt = [P.sb([128, 8, 512], F32, st, "adaw") for _ in range(2)]
        mod = P.sb([128, 24, 2], F32, st, "mod")
        for nb in range(6):
            w = wt[nb % 2]
            P.dma('sp', w[:, :, :], G.ada_w[l, :, nb * 512:(nb + 1) * 512].re("(c p) n -> p c n", p=128), partial=False)
            pt = G.psum[nb % 2]
            for j in range(4):
                ob = nb * 4 + j
                for kc in range(8):
                    P.mm(pt[:, j * 2:j * 2 + 2], w[:, kc, j * 128:(j + 1) * 128], G.scT[:, kc, :], start=(kc == 0), stop=(kc == 7))
            P.tt(mod[:, nb * 4:nb * 4 + 4, :], pt[:, 0:8].re("p (a b) -> p a b", b=2),
                 G.adab[:, l * 24 + nb * 4:l * 24 + nb * 4 + 4].re("p (a o) -> p a o", o=1).bc([128, 4, 2]), ALU.add, partial=True)
        P.ts(G.A[:, :, :], mod[:, 8:16, :], 1.0, 32.0, ALU.add, ALU.mult)
        P.tt(G.A[:, :, :], G.A[:, :, :], G.npre[:, l * 8:(l + 1) * 8].re("p (a o) -> p a o", o=1).bc([128, 8, 2]), ALU.mult)
        P.copy(G.B[:, :, :], mod[:, 0:8, :])
        P.ts(G.Gt[:, :, :], mod[:, 16:24, :], 32.0, None, ALU.mult)
        P.tt(G.Gt[:, :, :], G.Gt[:, :, :], G.npost[:, l * 8:(l + 1) * 8].re("p (a o) -> p a o", o=1).bc([128, 8, 2]), ALU.mult)
        P.barrier()


def norm_phase(P, G, hT):
    with ExitStack() as st:
        xt = [P.sb([128, 8, 512], F32, st, "xt") for _ in range(2)]
        sq = P.sb([128, 8, 512], F32, st, "sq")
        rs = P.sb([128, 512], F32, st, "rs")
        tmp = [P.sb([128, 512], F32, st, "tmp") for _ in range(2)]
        for ti, (n0, nt) in enumerate(TILES):
            j = 1 if n0 == 0 else 0
            x = xt[ti % 2]
            P.dma('sp', x[:, :, :nt], G.xs[:, n0:n0 + nt].re("(c p) n -> p c n", p=128), partial=False)
            P.act(sq[:, :, :nt], x[:, :, :nt], AF.Square)
            pt = G.psum[ti % 2]
            for c in range(8):
                P.mm(pt[:, :nt], G.ones[:, :], sq[:, c, :nt], start=(c == 0), stop=(c == 7))
            P.rsqrt(rs[:, :nt], pt[:, :nt], 1024.0 * EPS)
            for c in range(8):
                t = tmp[c % 2]
                P.tt(t[:, :nt], x[:, c, :nt], rs[:, :nt], ALU.mult)
                P.act(hT[:, c, n0:n0 + nt], t[:, :nt], AF.Identity, bias=G.B[:, c, j:j + 1], scale=G.A[:, c, j:j + 1], partial=True)
        P.barrier()


def outproj_phase(P, G, l, uT):
    with ExitStack() as st:
        wo = P.sb([128, 16, 1024], BF16, st, "wo")
        P.dma('pool', wo[:, :, :], G.w_out[l].re("(c p) n -> p c n", p=128), partial=False)
        ut = [P.sb([128, 16, 512], BF16, st, "ut") for _ in range(2)]
        o = P.sb([128, 8, 512], F32, st, "o")
        sq = P.sb([128, 8, 512], F32, st, "sq")
        rs = P.sb([128, 512], F32, st, "rs")
        xt = [P.sb([128, 8, 512], F32, st, "xt") for _ in range(2)]
        for ti, (n0, nt) in enumerate(TILES):
            j = 1 if n0 == 0 else 0
            u = ut[ti % 2]
            x = xt[ti % 2]
            P.dma('sp', u[:, :, :nt], uT[:, n0:n0 + nt].re("(c p) n -> p c n", p=128), partial=False)
            P.dma('sp', x[:, :, :nt], G.xs[:, n0:n0 + nt].re("(c p) n -> p c n", p=128), partial=False)
            for ob in range(8):
                pt = G.psum[ob % 4]
                for ec in range(16):
                    P.mm(pt[:, :nt], wo[:, ec, ob * 128:(ob + 1) * 128], u[:, ec, :nt], start=(ec == 0), stop=(ec == 15))
                P.copy(o[:, ob, :nt], pt[:, :nt], partial=True, eng=('act' if ob % 2 else 'dve'))
            P.act(sq[:, :, :nt], o[:, :, :nt], AF.Square)
            pt = G.psum[4 + ti % 2]
            for c in range(8):
                P.mm(pt[:, :nt], G.ones[:, :], sq[:, c, :nt], start=(c == 0), stop=(c == 7))
            P.rsqrt(rs[:, :nt], pt[:, :nt], 1024.0 * EPS)
            for c in range(8):
                P.tt(o[:, c, :nt], o[:, c, :nt], rs[:, :nt], ALU.mult, partial=True)
                P.stt(x[:, c, :nt], o[:, c, :nt], G.Gt[:, c, j:j + 1], x[:, c, :nt], ALU.mult, ALU.add, partial=True)
            P.dma('sp', G.xs[:, n0:n0 + nt].re("(c p) n -> p c n", p=128), x[:, :, :nt])
        P.barrier()


def load_w_block(P, dst, w_dram, c0, ncols):
    P.dma('pool', dst[:, :, :ncols], w_dram[:, c0:c0 + ncols].re("(c p) n -> p c n", p=128), partial=False)


def proj_fm(P, G, src, w_dram, c0, ncols, dst_dram, r0, wbufs, stg, cnt, post=None):
    for cb in range(0, ncols, 512):
        nb = min(512, ncols - cb)
        w = wbufs[cnt[0] % 2]
        cnt[0] += 1
        load_w_block(P, w, w_dram, c0 + cb, nb)
        for (n0, nt) in TILES:
            for eb in range(0, nb, 128):
                ne = min(128, nb - eb)
                k = cnt[1] % 4
                cnt[1] += 1
                pt = G.psum[k]
                for c in range(8):
                    P.mm(pt[:ne, :nt], w[:, c, eb:eb + ne], src[:, c, n0:n0 + nt], start=(c == 0), stop=(c == 7))
                s = stg[k]
                if post is None:
                    P.copy(s[:ne, :nt], pt[:ne, :nt], eng=('act' if k % 2 else 'dve'))
                else:
                    post(s[:ne, :nt], pt[:ne, :nt], r0 + cb + eb, ne)
                P.dma('sp', dst_dram[r0 + cb + eb:r0 + cb + eb + ne, n0:n0 + nt], s[:ne, :nt])


def proj_tm(P, G, src, w_dram, c0, ncols, dst_dram, wbufs, stg, cnt, func=None):
    for cb in range(0, ncols, 512):
        nb = min(512, ncols - cb)
        w = wbufs[cnt[0] % 2]
        cnt[0] += 1
        load_w_block(P, w, w_dram, c0 + cb, nb)
        for tb in range(T // 128):
            k = cnt[1] % 4
            cnt[1] += 1
            pt = G.psum[k]
            for c in range(8):
                P.mm(pt[:, :nb], src[:, c, tb * 128:(tb + 1) * 128], w[:, c, :nb], start=(c == 0), stop=(c == 7))
            s = stg[k]
            if func is None:
                P.copy(s[:, :nb], pt[:, :nb], eng=('act' if k % 2 else 'dve'))
            else:
                P.act(s[:, :nb], pt[:, :nb], func)
            P.dma('sp', dst_dram[tb * 128:(tb + 1) * 128, cb:cb + nb], s[:, :nb])


def rwkv_layer(P, G, l, W, vres):
    nc = P.nc
    S = G.scr
    with ExitStack() as st:
        hT = P.sb([128, 8, T], BF16, st, "hT")
        norm_phase(P, G, hT)
        hm = P.sb([128, 8, T], BF16, st, "hm")
        wb = [P.sb([128, 8, 512], BF16, st, "wb") for _ in range(2)]
        stg = [P.sb([128, 512], F32, st, "stg") for _ in range(4)]
        cnt = [0, 0]
        mu = W['mu']
        ncol_base = [0, E, 2 * E, 3 * E, 4 * E, 4 * E + 128]
        for g in range(6):
            for c in range(8):
                mcol = mu[:, g * 8 + c:g * 8 + c + 1]
                ocol = W['om'][:, g * 8 + c:g * 8 + c + 1]
                P.act(hm[:, c, :], hT[:, c, :], AF.Identity, scale=ocol, partial=True)
                if c < 4:
                    P.stt(hm[:, c, 1:LC], hT[:, c, 0:LC - 1], mcol, hm[:, c, 1:LC], ALU.mult, ALU.add, partial=True)
                else:
                    P.stt(hm[:, c, 0:LC - 1], hT[:, c, 1:LC], mcol, hm[:, c, 0:LC - 1], ALU.mult, ALU.add, partial=True)
                hx = hT[:, c, LC:].re("p (r w) -> p r w", w=64)
                mx = hm[:, c, LC:].re("p (r w) -> p r w", w=64)
                if c < 2:
                    P.stt(mx[:, :, 1:64], hx[:, :, 0:63], mcol, mx[:, :, 1:64], ALU.mult, ALU.add, partial=True)
                elif c < 4:
                    P.stt(mx[:, :, 0:63], hx[:, :, 1:64], mcol, mx[:, :, 0:63], ALU.mult, ALU.add, partial=True)
                elif c < 6:
                    P.stt(mx[:, 1:64, :], hx[:, 0:63, :], mcol, mx[:, 1:64, :], ALU.mult, ALU.add, partial=True)
                else:
                    P.stt(mx[:, 0:63, :], hx[:, 1:64, :], mcol, mx[:, 0:63, :], ALU.mult, ALU.add, partial=True)
            c0 = ncol_base[g]
            if g == 0:
                proj_fm(P, G, hm, W['w_in'], c0, E, S['rT'], 0, wb, stg, cnt)
            elif g == 1:
                proj_fm(P, G, hm, W['w_in'], c0, E, S['kT'], 0, wb, stg, cnt)
            elif g == 2:
                proj_tm(P, G, hm, W['w_in'], c0, E, S['vtok'], wb, stg, cnt)
                if vres:
                    proj_fm(P, G, hm, W['w_in'], 4 * E + 256, 32, S['lovT'], 0, wb, stg, cnt)
            elif g == 3:
                proj_tm(P, G, hm, W['w_in'], c0, E, S['gtok'], wb, stg, cnt, func=AF.Silu)
            else:
                proj_fm(P, G, hm, W['w_in'], c0, 128, S['loT'], (g - 4) * 128, wb, stg, cnt)
        P.barrier()
    if CUT <= 1:
        return
    if vres:
        with ExitStack() as st:
            lov = P.sb([33, T], F32, st, "lov")
            v2a = P.sb([33, E], F32, st, "v2a")
            P.memset(lov[:, :], 1.0)
            P.dma('sp', lov[0:32, :], S['lovT'][0:32, :], partial=False)
            P.dma('sp', v2a[0:32, :], W['v2'][:, :])
            P.dma('sp', v2a[32:33, :], W['v0'][:, :])
            vt = [P.sb([128, E], F32, st, "vt") for _ in range(2)]
            vf = [P.sb([128, E], F32, st, "vf") for _ in range(2)]
            sg = P.sb([128, E], F32, st, "sg")
            for tb in range(T // 128):
                v = vt[tb % 2]
                f = vf[tb % 2]
                P.dma('sp', v[:, :], S['vtok'][tb * 128:(tb + 1) * 128, :], partial=False)
                P.dma('sp', f[:, :], S['vfirst'][tb * 128:(tb + 1) * 128, :], partial=False)
                for q in range(4):
                    pt = G.psum[q]
                    P.mm(pt[:, :], lov[:, tb * 128:(tb + 1) * 128], v2a[:, q * 512:(q + 1) * 512])
                    P.act(sg[:, q * 512:(q + 1) * 512], pt[:, :], AF.Sigmoid, partial=True)
                P.tt(f[:, :], f[:, :], v[:, :], ALU.subtract)
                P.tt(f[:, :], f[:, :], sg[:, :], ALU.mult)
                P.tt(v[:, :], v[:, :], f[:, :], ALU.add)
                P.dma('sp', S['vtok'][tb * 128:(tb + 1) * 128, :], v[:, :])
            P.barrier()
    with ExitStack() as st:
        lo = P.sb([128, 2, T], F32, st, "lo")
        P.dma('sp', lo[:, 0, :], S['loT'][0:128, :], partial=False)
        P.dma('sp', lo[:, 1, :], S['loT'][128:256, :], partial=True)
        P.act(lo[:, 0, :], lo[:, 0, :], AF.Tanh, partial=True)
        w2 = P.sb([128, E], F32, st, "w2")
        a2 = P.sb([128, E], F32, st, "a2")
        P.dma('sp', w2[:, :], W['w2'][:, :], partial=False)
        P.dma('sp', a2[:, :], W['a2'][:, :], partial=False)
        sa = [P.sb([128, 512], F32, st, "sa") for _ in range(4)]
        i = 0
        for hp in range(16):
            e0 = hp * 128
            for z in range(2):
                zs = slice(z * 64, z * 64 + 64)
                for (n0, nt) in TILES:
                    pt = G.psum[i % 4]
                    s1 = sa[i % 4]
                    P.mm(pt[:, :nt], a2[zs, e0:e0 + 128], lo[zs, 1, n0:n0 + nt])
                    P.act(s1[:, :nt], pt[:, :nt], AF.Sigmoid, bias=W['a0'][:, z * 16 + hp:z * 16 + hp + 1])
                    P.dma('sp', S['aT'][z * E + e0:z * E + e0 + 128, n0:n0 + nt], s1[:, :nt])
                    i += 1
                    pt = G.psum[i % 4]
                    s2 = sa[i % 4]
                    P.mm(pt[:, :nt], w2[zs, e0:e0 + 128], lo[zs, 0, n0:n0 + nt])
                    P.act(s2[:, :nt], pt[:, :nt], AF.Sigmoid, bias=W['w0'][:, z * 16 + hp:z * 16 + hp + 1])
                    P.ts(s2[:, :nt], s2[:, :nt], -math.exp(-0.5), None, ALU.mult)
                    P.dma('sp', S['lwT'][z * E + e0:z * E + e0 + 128, n0:n0 + nt], s2[:, :nt])
                    i += 1
        P.barrier()
    if CUT <= 2:
        return
    with ExitStack() as st:
        NCH = T // 64
        vst = P.sb([128, NCH, 64], BF16, st, "vst")
        gt_ = P.sb([128, NCH, 64], BF16, st, "gt_")
        ybuf = P.sb([128, NCH, 64], F32, st, "ybuf")
        tmpR = P.sb([128, NCH, 64], F32, st, "tmpR")
        bon = P.sb([128, NCH, 1], F32, st, "bon")
        stat = P.sb([128, NCH, 2], F32, st, "stat")
        uob = P.sb([128, NCH, 128], BF16, st, "uob")
        P.memset(uob[:, :, :], 0.0, eng='pool')
        uT_sb = P.sb([128, T], BF16, st, "uT_sb")
        onesb = P.sb([128, 2], BF16, st, "onesb")
        P.memset(onesb[:, :], 1.0)
        idst = P.sb([128, 64], BF16, st, "idst")
        P.copy(idst[0:64, :], G.idnb[0:64, 0:64], partial=True)
        P.copy(idst[64:128, :], G.idnb[64:128, 64:128], partial=True)
        ones512 = P.sb([128, 256], F32, st, "ones256")
        P.memset(ones512[:, :], 1.0)
        pp = G.psum
        CH = []
        for z in range(2):
            C = Ctx()
            C.wk = [P.sb([128, 256], F32, st, "wk") for _ in range(14)]
            C.AR = P.sb([128, 4, 256], BF16, st, "AR")
            C.Kt = P.sb([128, 4, 128], BF16, st, "Kt")
            C.Bt = P.sb([128, 4, 128], BF16, st, "Bt")
            C.Pb = P.sb([128, 4, 128], BF16, st, "Pb")
            for b_ in (C.AR, C.Kt, C.Bt, C.Pb):
                P.memset(b_[:, :, :], 0.0, eng='pool')
            C.cend = P.sb([128, 4], F32, st, "cend")
            C.Mk = P.sb([128, 256], BF16, st, "Mk")
            C.Mb = P.sb([128, 256], BF16, st, "Mb")
            C.PS = [P.sb([128, 256], BF16, st, "PSb") for _ in range(2)]
            C.PT = [P.sb([128, 128], BF16, st, "PTb") for _ in range(2)]
            C.KtT = P.sb([128, 128], BF16, st, "KtT")
            C.BtT = P.sb([128, 128], BF16, st, "BtT")
            C.Wt = P.sb([128, 64], BF16, st, "Wt")
            C.Ut = P.sb([128, 64], BF16, st, "Ut")
            C.Sf = P.sb([128, 64], F32, st, "Sf")
            C.Sb = [P.sb([128, 64], BF16, st, "Sb") for _ in range(2)]
            C.tmpS = P.sb([128, 64], F32, st, "tmpS")
            C.banks = (pp[3 * z], pp[3 * z + 1], pp[3 * z + 2])
            CH.append(C)
        STILES = [(256 * i, 256) for i in range(T // 256)]

        def chain(hp, z, C):
            e0 = hp * 128
            r, k, a, lw, t1, t2, kk, Gc, Ec, kd, eg, ex, ei, bco = C.wk
            AR, Kt, Bt, Pb, cend = C.AR, C.Kt, C.Bt, C.Pb, C.cend
            Mk, Mb, PS_, PT_, KtT, BtT, Wt, Ut, Sf, Sb, tmpS = C.Mk, C.Mb, C.PS, C.PT, C.KtT, C.BtT, C.Wt, C.Ut, C.Sf, C.Sb, C.tmpS
            b0, b1, b2 = C.banks
            tiles = list(STILES) if z == 0 else [STILES[0]] + list(STILES[:0:-1])
            P.memset(Sf[:, :], 0.0); yield
            P.memset(Sb[0][:, :], 0.0); yield
            sbi = 0
            for (n0, nt) in tiles:
                ncn = nt // 64
                cb = n0 // 64
                P.dma('sp', r[:, :nt], S['rT'][e0:e0 + 128, n0:n0 + nt], partial=False); yield
                P.dma('sp', k[:, :nt], S['kT'][e0:e0 + 128, n0:n0 + nt], partial=False); yield
                P.dma('sp', a[:, :nt], S['aT'][z * E + e0:z * E + e0 + 128, n0:n0 + nt], partial=False); yield
                P.dma('sp', lw[:, :nt], S['lwT'][z * E + e0:z * E + e0 + 128, n0:n0 + nt], partial=False); yield
                P.ts(t1[:, :nt], k[:, :nt], W['k_k'][:, hp:hp + 1], None, ALU.mult); yield
                P.tt(t2[:, :nt], t1[:, :nt], t1[:, :nt], ALU.mult); yield
                P.mm(b1[:, :nt], G.blk[:, :], t2[:, :nt]); yield
                P.ts(kk[:, :nt], b1[:, :nt], 1e-24, None, ALU.add); yield
                P.act(kk[:, :nt], kk[:, :nt], AF.Ln); yield
                P.act(kk[:, :nt], kk[:, :nt], AF.Exp, scale=-0.5); yield
                P.tt(kk[:, :nt], kk[:, :nt], t1[:, :nt], ALU.mult); yield
                P.scan(Gc[:, :nt], ones512[:, :nt], lw[:, :nt], 0.0, ALU.mult, ALU.add); yield
                P.tt(Ec[:, :nt], Gc[:, :nt], lw[:, :nt], ALU.subtract); yield
                v3 = lambda b_: b_[:, :nt].re("p (c s) -> p c s", s=64)
                G3, E3, t13, t23 = v3(Gc), v3(Ec), v3(t1), v3(t2)
                ce3 = cend[:, :ncn].re("p (c o) -> p c o", o=1)
                if z == 0:
                    base = E3[:, :, 0:1].bc([128, ncn, 64])
                    P.tt(t13, G3, base, ALU.subtract); yield
                    P.tt(t23, E3, base, ALU.subtract); yield
                else:
                    base = G3[:, :, 63:64].bc([128, ncn, 64])
                    P.tt(t13, base, E3, ALU.subtract); yield
                    P.tt(t23, base, G3, ALU.subtract); yield
                P.tt(ce3, G3[:, :, 63:64], E3[:, :, 0:1], ALU.subtract); yield
                P.act(cend[:, :ncn], cend[:, :ncn], AF.Exp); yield
                P.ts(kd[:, :nt], a[:, :nt], -1.0, W['k_a'][:, hp:hp + 1], ALU.add, ALU.mult); yield
                P.stt(kd[:, :nt], kd[:, :nt], 1.0, k[:, :nt], ALU.add, ALU.mult); yield
                P.act(eg[:, :nt], t1[:, :nt], AF.Exp); yield
                P.act(ex[:, :nt], t2[:, :nt], AF.Exp); yield
                P.act(ei[:, :nt], t1[:, :nt], AF.Exp, scale=-1.0); yield
                P.tt(bco[:, :nt], kk[:, :nt], a[:, :nt], ALU.mult); yield
                r3, kk3, kd3, eg3, ex3, ei3, bc3 = v3(r), v3(kk), v3(kd), v3(eg), v3(ex), v3(ei), v3(bco)
                for hpar in range(2):
                    ps_ = slice(hpar * 64, hpar * 64 + 64)
                    cs = slice(hpar * 64, hpar * 64 + 64)
                    P.stt(AR[ps_, :ncn, cs], kk3[ps_], -1.0, ex3[ps_], ALU.mult, ALU.mult, partial=True); yield
                    P.tt(AR[ps_, :ncn, 128 + hpar * 64:128 + hpar * 64 + 64], r3[ps_], eg3[ps_], ALU.mult, partial=True); yield
                    P.tt(Kt[ps_, :ncn, cs], kd3[ps_], ei3[ps_], ALU.mult, partial=True); yield
                    P.tt(Bt[ps_, :ncn, cs], bc3[ps_], ei3[ps_], ALU.mult, partial=True); yield
                    P.stt(Pb[ps_, :ncn, cs], r3[ps_], W['r_k'][ps_, hp:hp + 1], kd3[ps_], ALU.mult, ALU.mult, partial=True); yield
                corder = range(ncn) if z == 0 else range(ncn - 1, -1, -1)
                for cl in corder:
                    c = cb + cl
                    ARc, Ktc, Btc = AR[:, cl, :], Kt[:, cl, :], Bt[:, cl, :]
                    P.mm(b0[:, 0:256], Ktc, ARc); yield
                    P.mm(b0[:, 256:512], Btc, ARc); yield
                    P.mm(b1[:, 0:128], AR[:, cl, 0:128], Btc); yield
                    P.tt(Mk[:, :], b0[:, 0:256], G.masks[:, z, :], ALU.mult); yield
                    P.tt(Mb[:, :], b0[:, 256:512], G.masks[:, z, :], ALU.mult); yield
                    P.tt(PT_[0][:, :], b1[:, 0:128], G.maskt[:, z, :], ALU.mult); yield
                    P.mm(b0[:, 0:128], Ktc, G.idnb[:, :]); yield
                    P.mm(b0[:, 128:256], Btc, G.idnb[:, :]); yield
                    P.mm(b2[:, 0:128], PT_[0][:, :], Mb[:, 0:128]); yield
                    P.mm(b2[:, 256:384], Mb[:, 0:128], PT_[0][:, :]); yield
                    P.copy(KtT[:, :], b0[:, 0:128]); yield
                    P.copy(BtT[:, :], b0[:, 128:256]); yield
                    P.copy(PS_[1][:, 0:128], b2[:, 0:128], partial=True, eng='act'); yield
                    P.tt(PS_[1][:, 128:256], Mb[:, 0:128], G.idnb[:, :], ALU.add, partial=True); yield
                    P.copy(PT_[1][:, :], b2[:, 256:384], eng='act'); yield
                    cur = 1
                    for lev in range(1, 6):
                        nxt = 1 - cur
                        if lev < 5:
                            P.mm(b2[:, 0:128], PT_[cur][:, :], PS_[cur][:, 0:128]); yield
                            P.mm(b2[:, 128:256], PT_[cur][:, :], PS_[cur][:, 128:256], start=True, stop=False); yield
                            P.mm(b2[:, 128:256], G.idnb[:, :], PS_[cur][:, 128:256], start=False, stop=True); yield
                            P.mm(b2[:, 256:384], PS_[cur][:, 0:128], PT_[cur][:, :]); yield
                            P.copy(PS_[nxt][:, :], b2[:, 0:256], eng='act'); yield
                            P.copy(PT_[nxt][:, :], b2[:, 256:384], eng='act'); yield
                        else:
                            P.mm(b2[:, 128:256], PT_[cur][:, :], PS_[cur][:, 128:256], start=True, stop=False); yield
                            P.mm(b2[:, 128:256], G.idnb[:, :], PS_[cur][:, 128:256], start=False, stop=True); yield
                            P.copy(PS_[nxt][:, 128:256], b2[:, 128:256], partial=True, eng='act'); yield
                        cur = nxt
                    X = PS_[cur][:, 128:256]
                    S0 = Sb[sbi]
                    P.mm(b2[:, 384:448], AR[:, cl, 0:128], S0[:, :], start=True, stop=False); yield
                    P.mm(b2[:, 384:448], Mk[:, 0:128], vst[:, c, :], start=False, stop=True); yield
                    P.copy(Wt[:, :], b2[:, 384:448], eng='act'); yield
                    P.mm(b2[:, 448:512], X, Wt[:, :]); yield
                    P.copy(Ut[:, :], b2[:, 448:512], eng='act'); yield
                    P.mm(b1[:, 320:384], AR[:, cl, 128:256], S0[:, :], start=True, stop=False); yield
                    P.mm(b1[:, 320:384], Mk[:, 128:256], vst[:, c, :], start=False, stop=False); yield
                    P.mm(b1[:, 320:384], Mb[:, 128:256], Ut[:, :], start=False, stop=True); yield
                    P.mm(b1[:, 384:386], Pb[:, cl, :], onesb[:, :]); yield
                    P.mm(b1[:, 256:320], KtT[:, :], vst[:, c, :], start=True, stop=False); yield
                    P.mm(b1[:, 256:320], BtT[:, :], Ut[:, :], start=False, stop=True); yield
                    P.tt(ybuf[:, c, :], ybuf[:, c, :], b1[:, 320:384], ALU.add, partial=True); yield
                    P.tt(bon[:, c, :], bon[:, c, :], b1[:, 384:385], ALU.add, partial=True); yield
                    P.tt(tmpS[:, :], b1[:, 256:320], Sf[:, :], ALU.add); yield
                    P.ts(Sf[:, :], tmpS[:, :], cend[:, cl:cl + 1], None, ALU.mult); yield
                    sbi = 1 - sbi
                    P.act(Sb[sbi][:, :], tmpS[:, :], AF.Identity, scale=cend[:, cl:cl + 1]); yield

        for hp in range(KHP):
            e0 = hp * 128
            for hpar in range(2):
                for (n0, nt) in TILES:
                    P.dma('pool', vst[hpar * 64:(hpar + 1) * 64, n0 // 64:(n0 + nt) // 64, :],
                          S['vtok'][n0:n0 + nt, e0 + hpar * 64:e0 + hpar * 64 + 64].re("(c s) v -> s c v", s=64), partial=(hpar == 1 or n0 > 0))
                    P.dma('pool', gt_[hpar * 64:(hpar + 1) * 64, n0 // 64:(n0 + nt) // 64, :],
                          S['gtok'][n0:n0 + nt, e0 + hpar * 64:e0 + hpar * 64 + 64].re("(c s) v -> s c v", s=64), partial=(hpar == 1 or n0 > 0))
            P.memset(ybuf[:, :, :], 0.0)
            P.memset(bon[:, :, :], 0.0)
            gens = [chain(hp, z, CH[z]) for z in range(2)]
            while gens:
                for g_ in list(gens):
                    try:
                        next(g_)
                    except StopIteration:
                        gens.remove(g_)
            P.op('dve', lambda e: e.tensor_reduce(out=stat[:, :, 0:1].ap, in_=ybuf[:, :, :].ap, axis=mybir.AxisListType.X, op=ALU.add),
                 [ybuf[:, :, :]], [stat[:, :, 0:1]], partial=True)
            P.ts(stat[:, :, 0:1], stat[:, :, 0:1], 1.0 / 64, None, ALU.mult, partial=True)
            P.tt(ybuf[:, :, :], ybuf[:, :, :], stat[:, :, 0:1].bc([128, NCH, 64]), ALU.subtract)
            P.tt(tmpR[:, :, :], ybuf[:, :, :], ybuf[:, :, :], ALU.mult)
            P.op('dve', lambda e: e.tensor_reduce(out=stat[:, :, 1:2].ap, in_=tmpR[:, :, :].ap, axis=mybir.AxisListType.X, op=ALU.add),
                 [tmpR[:, :, :]], [stat[:, :, 1:2]], partial=True)
            P.ts(stat[:, :, 1:2], stat[:, :, 1:2], 1.0 / 64, None, ALU.mult, partial=True)
            P.rsqrt(stat[:, :, 1:2], stat[:, :, 1:2], 64e-5, partial=True)
            P.tt(ybuf[:, :, :], ybuf[:, :, :], stat[:, :, 1:2].bc([128, NCH, 64]), ALU.mult)
            P.tt(ybuf[:, :, :], ybuf[:, :, :], W['lnwb'][:, hp, :].re("p (o v) -> p o v", o=1).bc([128, NCH, 64]), ALU.mult)
            P.tt(ybuf[:, :, :], ybuf[:, :, :], W['lnbb'][:, hp, :].re("p (o v) -> p o v", o=1).bc([128, NCH, 64]), ALU.add)
            P.tt(tmpR[:, :, :], vst[:, :, :], bon[:, :, :].bc([128, NCH, 64]), ALU.mult)
            P.tt(ybuf[:, :, :], ybuf[:, :, :], tmpR[:, :, :], ALU.add)
            for hpar in range(2):
                ps_ = slice(hpar * 64, hpar * 64 + 64)
                P.tt(uob[ps_, :, hpar * 64:hpar * 64 + 64], ybuf[ps_], gt_[ps_], ALU.mult, partial=True)
            for c in range(NCH):
                pt = pp[6 + c % 2]
                P.mm(pt[:, 0:64], uob[:, c, :], idst[:, :])
                P.copy(uT_sb[:, c * 64:(c + 1) * 64], pt[:, 0:64], partial=True, eng=('act' if c % 2 else 'dve'))
            P.dma('sp', S['uT'][e0:e0 + 128, :], uT_sb[:, :])
        P.barrier()


def hgrn_layer(P, G, l, W):
    S = G.scr
    with ExitStack() as st:
        hT = P.sb([128, 8, T], BF16, st, "hT")
        norm_phase(P, G, hT)
        wb = [P.sb([128, 8, 512], BF16, st, "wb") for _ in range(2)]
        stg = [P.sb([128, 512], F32, st, "stg") for _ in range(4)]
        cnt = [0, 0]
        silu_post = lambda s_, pt_, r0_, ne_: P.act(s_, pt_, AF.Silu)
        proj_fm(P, G, hT, W['w_in'], 0, E, S['rT'], 0, wb, stg, cnt, post=silu_post)
        proj_fm(P, G, hT, W['w_in'], E, 2 * E, S['aT'], 0, wb, stg, cnt)
        proj_tm(P, G, hT, W['w_in'], 3 * E, E, S['vtok'], wb, stg, cnt)
        proj_tm(P, G, hT, W['w_in'], 4 * E, E, S['gtok'], wb, stg, cnt, func=AF.Silu)
        P.barrier()
    if CUT <= 1:
        return
    with ExitStack() as st:
        NSC = T // 128
        lg = P.sb([128, 4, 16], F32, st, "lg")
        P.dma('sp', lg[:, :, :], W['lbl'][:, :, :], partial=False)
        P.act(lg[:, :, :], lg[:, :, :], AF.Exp)
        ssum = P.sb([128, 16], F32, st, "ssum")
        lb = P.sb([128, 16], F32, st, "lb")
        oml = P.sb([128, 16], F32, st, "oml")
        P.tt(ssum[:, :], lg[:, 0, :], lg[:, 1, :], ALU.add)
        P.tt(ssum[:, :], ssum[:, :], lg[:, 2, :], ALU.add)
        P.tt(ssum[:, :], ssum[:, :], lg[:, 3, :], ALU.add)
        lo_, hi_ = 1, l
        P.copy(lb[:, :], lg[:, 1, :])
        for i_ in range(2, l + 1):
            P.tt(lb[:, :], lb[:, :], lg[:, i_, :], ALU.add)
        P.op('dve', lambda e: e.reciprocal(out=ssum[:, :].ap, in_=ssum[:, :].ap), [ssum[:, :]], [ssum[:, :]])
        P.tt(lb[:, :], lb[:, :], ssum[:, :], ALU.mult)
        P.ts(oml[:, :], lb[:, :], -1.0, 1.0, ALU.mult, ALU.add)
        vt = P.sb([128, NSC, 128], BF16, st, "vt")
        gt_ = P.sb([128, NSC, 128], BF16, st, "gt_")
        obuf = P.sb([128, NSC, 128], F32, st, "obuf")
        tmpR = P.sb([128, NSC, 128], F32, st, "tmpR")
        stat = P.sb([128, NSC, 1], F32, st, "stat")
        ub = P.sb([128, NSC, 128], BF16, st, "ub")
        uT_sb = P.sb([128, T], BF16, st, "uT_sb")
        ones512 = P.sb([128, 256], F32, st, "ones256")
        P.memset(ones512[:, :], 1.0)
        pp = G.psum
        CH = []
        for z in range(2):
            C = Ctx()
            C.wk = [P.sb([128, 256], F32, st, "wk") for _ in range(10)]
            C.Qe = P.sb([128, 2, 640], BF16, st, "Qe")
            C.Ke = P.sb([128, 2, 640], BF16, st, "Ke")
            C.Qp = P.sb([128, 2, 128], BF16, st, "Qp")
            C.Kp = P.sb([128, 2, 128], BF16, st, "Kp")
            P.memset(C.Qe[:, :, :], 0.0, eng='pool')
            P.memset(C.Ke[:, :, :], 0.0, eng='pool')
            C.cend = P.sb([128, 8], F32, st, "cend")
            C.At = P.sb([128, 128], BF16, st, "At")
            C.KeT = P.sb([128, 512], BF16, st, "KeT")
            C.Sf = P.sb([128, 128], F32, st, "Sf")
            C.Sb = [P.sb([128, 128], BF16, st, "Sb") for _ in range(2)]
            C.tmpS = P.sb([128, 128], F32, st, "tmpS")
            C.banks = (pp[3 * z], pp[3 * z + 1], pp[3 * z + 2])
            CH.append(C)
        STILES = [(256 * i, 256) for i in range(T // 256)]

        def chain(h, z, C):
            e0 = h * 128
            q, fp, f, lf, Gc, Ec, t1, eg, ei, kc = C.wk
            Qe, Ke, Qp, Kp, cend, At, KeT, Sf, Sb, tmpS = C.Qe, C.Ke, C.Qp, C.Kp, C.cend, C.At, C.KeT, C.Sf, C.Sb, C.tmpS
            bD, bA, bS = C.banks
            tiles = list(STILES) if z == 0 else [STILES[0]] + list(STILES[:0:-1])
            P.memset(Sf[:, :], 0.0); yield
            P.memset(Sb[0][:, :], 0.0); yield
            sbi = 0
            for (n0, nt) in tiles:
                nsc = nt // 128
                ncn = nt // 32
                P.dma('sp', q[:, :nt], S['rT'][e0:e0 + 128, n0:n0 + nt], partial=False); yield
                P.dma('sp', fp[:, :nt], S['aT'][z * E + e0:z * E + e0 + 128, n0:n0 + nt], partial=False); yield
                P.act(f[:, :nt], fp[:, :nt], AF.Sigmoid); yield
                P.ts(f[:, :nt], f[:, :nt], oml[:, h:h + 1], lb[:, h:h + 1], ALU.mult, ALU.add); yield
                P.act(lf[:, :nt], f[:, :nt], AF.Ln); yield
                P.ts(kc[:, :nt], f[:, :nt], -1.0, 1.0, ALU.mult, ALU.add); yield
                P.scan(Gc[:, :nt], ones512[:, :nt], lf[:, :nt], 0.0, ALU.mult, ALU.add); yield
                P.tt(Ec[:, :nt], Gc[:, :nt], lf[:, :nt], ALU.subtract); yield
                v3 = lambda b_: b_[:, :nt].re("p (c s) -> p c s", s=32)
                G3, E3, t13 = v3(Gc), v3(Ec), v3(t1)
                if z == 0:
                    P.tt(t13, G3, E3[:, :, 0:1].bc([128, ncn, 32]), ALU.subtract); yield
                else:
                    P.tt(t13, G3[:, :, 31:32].bc([128, ncn, 32]), E3, ALU.subtract); yield
                ce3 = cend[:, :ncn].re("p (c o) -> p c o", o=1)
                P.tt(ce3, G3[:, :, 31:32], E3[:, :, 0:1], ALU.subtract); yield
                P.act(cend[:, :ncn], cend[:, :ncn], AF.Exp); yield
                P.act(eg[:, :nt], t1[:, :nt], AF.Exp); yield
                P.act(ei[:, :nt], t1[:, :nt], AF.Exp, scale=-1.0); yield
                v128 = lambda b_: b_[:, :nt].re("p (a s) -> p a s", s=128)
                P.tt(Qp[:, :nsc, :], v128(q), v128(eg), ALU.mult); yield
                P.tt(Kp[:, :nsc, :], v128(kc), v128(ei), ALU.mult); yield
                v4 = lambda b_: b_[:, :nt].re("p (a j s) -> p a j s", j=4, s=32)
                P.tt(Qe[:, :nsc, :].re("p a (j x) -> p a j x", x=160)[:, :, :, 0:32], v4(q), v4(eg), ALU.mult, partial=True); yield
                P.tt(Ke[:, :nsc, :].re("p a (j x) -> p a j x", x=160)[:, :, :, 0:32], v4(kc), v4(ei), ALU.mult, partial=True); yield
                sc_order = range(nsc) if z == 0 else range(nsc - 1, -1, -1)
                for a_ in sc_order:
                    g = n0 // 128 + a_
                    P.mm(bD[:, 0:128], Kp[:, a_, :], Qp[:, a_, :]); yield
                    for j in range(4):
                        P.mm(bA[:, j * 128:(j + 1) * 128], Ke[:, a_, j * 128:(j + 1) * 128], G.idnb[:, :]); yield
                    P.tt(At[:, :], bD[:, 0:128], G.mask32[:, z, :], ALU.mult); yield
                    P.copy(KeT[:, :], bA[:, :], eng='act'); yield
                    P.mm(bD[:, 128:256], At[:, :], vt[:, g, :], start=True, stop=False); yield
                    jorder = list(range(4)) if z == 0 else [3, 2, 1, 0]
                    for jj, j in enumerate(jorder):
                        P.mm(bD[:, 128:256], Qe[:, a_, j * 128:(j + 1) * 128], Sb[sbi][:, :], start=False, stop=(jj == 3)); yield
                        ps_ = bS[:, 128 * (jj % 2):128 + 128 * (jj % 2)]
                        P.mm(ps_, KeT[:, j * 128:(j + 1) * 128], vt[:, g, :]); yield
                        P.tt(tmpS[:, :], ps_, Sf[:, :], ALU.add); yield
                        cc = cend[:, a_ * 4 + j:a_ * 4 + j + 1]
                        P.ts(Sf[:, :], tmpS[:, :], cc, None, ALU.mult); yield
                        sbi = 1 - sbi
                        P.act(Sb[sbi][:, :], tmpS[:, :], AF.Identity, scale=cc); yield
                    P.tt(obuf[:, g, :], obuf[:, g, :], bD[:, 128:256], ALU.add, partial=True); yield

        for h in range(KHP):
            e0 = h * 128
            for (n0, nt) in TILES:
                P.dma('pool', vt[:, n0 // 128:(n0 + nt) // 128, :], S['vtok'][n0:n0 + nt, e0:e0 + 128].re("(c s) v -> s c v", s=128), partial=(n0 > 0))
                P.dma('pool', gt_[:, n0 // 128:(n0 + nt) // 128, :], S['gtok'][n0:n0 + nt, e0:e0 + 128].re("(c s) v -> s c v", s=128), partial=(n0 > 0))
            P.memset(obuf[:, :, :], 0.0)
            gens = [chain(h, z, CH[z]) for z in range(2)]
            while gens:
                for g_ in list(gens):
                    try:
                        next(g_)
                    except StopIteration:
                        gens.remove(g_)
            P.tt(tmpR[:, :, :], obuf[:, :, :], obuf[:, :, :], ALU.mult)
            P.op('dve', lambda e: e.tensor_reduce(out=stat[:, :, 0:1].ap, in_=tmpR[:, :, :].ap, axis=mybir.AxisListType.X, op=ALU.add),
                 [tmpR[:, :, :]], [stat[:, :, 0:1]])
            P.ts(stat[:, :, :], stat[:, :, :], 1.0 / 128, None, ALU.mult)
            P.rsqrt(stat[:, :, :], stat[:, :, :], EPS)
            P.tt(obuf[:, :, :], obuf[:, :, :], stat[:, :, 0:1].bc([128, NSC, 128]), ALU.mult)
            P.tt(obuf[:, :, :], obuf[:, :, :], W['gnb'][:, :].re("p (o v) -> p o v", o=1).bc([128, NSC, 128]), ALU.mult)
            P.tt(ub[:, :, :], obuf[:, :, :], gt_[:, :, :], ALU.mult)
            for g in range(NSC):
                pt = pp[6 + g % 2]
                P.mm(pt[:, 0:128], ub[:, g, :], G.idnb[:, :])
                P.copy(uT_sb[:, g * 128:(g + 1) * 128], pt[:, 0:128], partial=True)
            P.dma('sp', S['uT'][e0:e0 + 128, :], uT_sb[:, :])
        P.barrier()


TWO_PI = 2.0 * math.pi


def hyena_tables(P, G, L, cosb, sinb):
    nb = L // 128
    N = 2 * L
    with ExitStack() as st:
        arg = P.sb([128, nb, 128], F32, st, "arg")
        m = P.sb([128, nb, 128], F32, st, "m")
        ob = [P.sb([128, nb, 128], BF16, st, "ob") for _ in range(2)]
        fr = P.sb([128, 128], F32, st, "fr")
        ki = P.sb([128, nb, 128], I32, st, "ki")
        for fb in range(nb):
            P.ts(fr[:, :], G.jrow[:, :], float(128 * fb), None, ALU.add)
            P.tt(arg[:, :, :], G.tcol[:, :nb].re("p (c o) -> p c o", o=1).bc([128, nb, 128]),
                 fr[:, :].re("p (o j) -> p o j", o=1).bc([128, nb, 128]), ALU.mult)
            for kind, (off, dst) in enumerate(((N / 4, cosb), (0.0, sinb))):
                P.ts(m[:, :, :], arg[:, :, :], float(off), 1.0 / N, ALU.add, ALU.mult)
                P.copy(ki[:, :, :], m[:, :, :])
                P.tt(m[:, :, :], m[:, :, :], ki[:, :, :], ALU.subtract)
                P.act(ob[kind][:, :, :], m[:, :, :], AF.Sin, scale=TWO_PI)
                P.dma('sp', dst[fb], ob[kind][:, :, :])
        P.barrier()


def hyena_filters(P, G, W, L, zT_d, tnneg, ksum_d, kdiff_d):
    nb = L // 128
    with ExitStack() as st:
        zT = P.sb([33, L], F32, st, "zT")
        P.dma('sp', zT[:, :], zT_d[:, :], partial=False)
        ha = P.sb([64, L], F32, st, "ha")
        hb_ = P.sb([64, L], F32, st, "hb")
        tmp = P.sb([64, 512], F32, st, "ftmp")
        kif = P.sb([64, 512], I32, st, "kif")
        plan = [(zT, 33, W['f_w1'], 0, ha), (ha, 64, W['f_w2'], 1, hb_), (hb_, 64, W['f_w3'], 2, ha)]
        for (src, kd, wm, bi, dst) in plan:
            for t0 in range(0, L, 512):
                nt = min(512, L - t0)
                pt = G.psum[(t0 // 512) % 2]
                P.mm(pt[0:64, :nt], wm[0:kd, :], src[0:kd, t0:t0 + nt])
                P.ts(tmp[:, :nt], pt[0:64, :nt], W['fb'][:, bi:bi + 1], W['sf'][:, 0:1], ALU.add, ALU.mult)
                P.ts(tmp[:, :nt], tmp[:, :nt], 1.0 / TWO_PI, None, ALU.mult)
                P.copy(kif[:, :nt], tmp[:, :nt])
                P.tt(tmp[:, :nt], tmp[:, :nt], kif[:, :nt], ALU.subtract)
                P.act(dst[:, t0:t0 + nt], tmp[:, :nt], AF.Sin, scale=TWO_PI, partial=True)
        h3 = ha
        w4 = P.sb([64, 2 * E], F32, st, "w4")
        P.dma('sp', w4[:, :], W['f_w4'][:, :], partial=False)
        win = P.sb([128, 512], F32, st, "win")
        hf = P.sb([128, 512], F32, st, "hf")
        hk = P.sb([128, 512], F32, st, "hk")
        sk = [P.sb([128, 512], BF16, st, "sk") for _ in range(2)]
        dk = [P.sb([128, 512], BF16, st, "dk") for _ in range(2)]
        i = 0
        for tb in range(nb):
            for ebk in range(4):
                pf = G.psum[0]
                pb = G.psum[1]
                P.mm(pf[:, :], h3[:, tb * 128:(tb + 1) * 128], w4[:, ebk * 512:(ebk + 1) * 512])
                P.mm(pb[:, :], h3[:, tb * 128:(tb + 1) * 128], w4[:, E + ebk * 512:E + (ebk + 1) * 512])
                P.act(win[:, :], G.deltab[:, ebk * 512:(ebk + 1) * 512], AF.Exp, scale=tnneg[:, tb:tb + 1])
                P.tt(hf[:, :], pf[:, :], win[:, :], ALU.mult)
                P.tt(hk[:, :], pb[:, :], win[:, :], ALU.mult)
                P.tt(sk[i % 2][:, :], hf[:, :], hk[:, :], ALU.add)
                P.tt(dk[i % 2][:, :], hf[:, :], hk[:, :], ALU.subtract)
                P.dma('sp', ksum_d[tb * 128:(tb + 1) * 128, ebk * 512:(ebk + 1) * 512], sk[i % 2][:, :])
                P.dma('sp', kdiff_d[tb * 128:(tb + 1) * 128, ebk * 512:(ebk + 1) * 512], dk[i % 2][:, :])
                i += 1
        P.barrier()


def hyena_dft(P, G, S, L, n_off, cosb, sinb, ksum_d, kdiff_d):
    nb = L // 128
    N = 2 * L
    pp = G.psum
    for unit in range(4):
        c0 = unit * 512
        with ExitStack() as st0:
            KY = P.sb([128, nb, 2, 512], BF16, st0, "KY")
            kn = P.sb([2, 512], F32, st0, "kn")
            ynb = P.sb([2, 512], BF16, st0, "ynb")
            with ExitStack() as st:
                ta = P.sb([128, nb, 512], BF16, st, "ta")
                tb_ = P.sb([128, nb, 512], BF16, st, "tb")
                slab = [[P.sb([128, nb, 128], BF16, st, "slab") for _ in range(2)] for _ in range(2)]
                P.dma('sp', ta[:, :, :], ksum_d[:, c0:c0 + 512].re("(c p) e -> p c e", p=128), partial=False)
                P.dma('sp', tb_[:, :, :], kdiff_d[:, c0:c0 + 512].re("(c p) e -> p c e", p=128), partial=False)
                for fb in range(nb):
                    cs, ss = slab[0][fb % 2], slab[1][fb % 2]
                    P.dma('sp', cs[:, :, :], cosb[fb], partial=False)
                    P.dma('sp', ss[:, :, :], sinb[fb], partial=False)
                    for tc in range(nb):
                        P.mm(pp[0][:, :], cs[:, tc, :], ta[:, tc, :], start=(tc == 0), stop=(tc == nb - 1))
                    for tc in range(nb):
                        P.mm(pp[1][:, :], ss[:, tc, :], tb_[:, tc, :], start=(tc == 0), stop=(tc == nb - 1))
                    P.copy(KY[:, fb, 0, :], pp[0][:, :], partial=True)
                    P.copy(KY[:, fb, 1, :], pp[1][:, :], partial=True)
                for tc in range(nb):
                    P.mm(pp[4][0:2, :], G.altb[:, :], ta[:, tc, :], start=(tc == 0), stop=(tc == nb - 1))
                P.copy(kn[:, :], pp[4][0:2, :])
                P.barrier()
            with ExitStack() as st:
                ta = P.sb([128, nb, 512], BF16, st, "ta")
                slab = [[P.sb([128, nb, 128], BF16, st, "slab") for _ in range(2)] for _ in range(2)]
                t = [P.sb([128, 512], F32, st, "t") for _ in range(4)]
                P.dma('sp', ta[:, :, :], S['utok'][n_off:n_off + L, c0:c0 + 512].re("(c p) e -> p c e", p=128), partial=False)
                for fb in range(nb):
                    cs, ss = slab[0][fb % 2], slab[1][fb % 2]
                    P.dma('sp', cs[:, :, :], cosb[fb], partial=False)
                    P.dma('sp', ss[:, :, :], sinb[fb], partial=False)
                    for tc in range(nb):
                        P.mm(pp[2][:, :], cs[:, tc, :], ta[:, tc, :], start=(tc == 0), stop=(tc == nb - 1))
                    for tc in range(nb):
                        P.mm(pp[3][:, :], ss[:, tc, :], ta[:, tc, :], start=(tc == 0), stop=(tc == nb - 1))
                    P.tt(t[0][:, :], pp[2][:, :], KY[:, fb, 0, :], ALU.mult)
                    P.tt(t[1][:, :], pp[3][:, :], KY[:, fb, 1, :], ALU.mult)
                    P.tt(t[2][:, :], pp[2][:, :], KY[:, fb, 1, :], ALU.mult)
                    P.tt(t[3][:, :], pp[3][:, :], KY[:, fb, 0, :], ALU.mult)
                    P.tt(KY[:, fb, 0, :], t[0][:, :], t[1][:, :], ALU.subtract, partial=True)
                    P.tt(KY[:, fb, 1, :], t[2][:, :], t[3][:, :], ALU.add, partial=True)
                    if fb == 0:
                        P.ts(KY[0:1, 0, 0, :], KY[0:1, 0, 0, :], 0.5, None, ALU.mult, partial=True)
                for tc in range(nb):
                    P.mm(pp[4][0:2, :], G.altb[:, :], ta[:, tc, :], start=(tc == 0), stop=(tc == nb - 1))
                P.stt(ynb[:, :], pp[4][0:2, :], 0.5, kn[:, :], ALU.mult, ALU.mult)
                P.barrier()
            with ExitStack() as st:
                slab = [[P.sb([128, nb, 128], BF16, st, "slab") for _ in range(2)] for _ in range(2)]
                yst = [P.sb([128, 4, 128], F32, st, "yst") for _ in range(2)]
                k = 0
                for tbk in range(nb):
                    cs, ss = slab[0][tbk % 2], slab[1][tbk % 2]
                    P.dma('sp', cs[:, :, :], cosb[tbk], partial=False)
                    P.dma('sp', ss[:, :, :], sinb[tbk], partial=False)
                    ys = yst[tbk % 2]
                    for eb in range(4):
                        po = pp[5 + k % 3]
                        k += 1
                        for fc in range(nb):
                            P.mm(po[:, 0:128], KY[:, fc, 0, eb * 128:(eb + 1) * 128], cs[:, fc, :], start=(fc == 0), stop=False)
                            P.mm(po[:, 0:128], KY[:, fc, 1, eb * 128:(eb + 1) * 128], ss[:, fc, :], start=False, stop=False)
                        P.mm(po[:, 0:128], ynb[0:1, eb * 128:(eb + 1) * 128], G.altrow[0:1, :], start=False, stop=True)
                        P.ts(ys[:, eb, :], po[:, 0:128], 2.0 / N, None, ALU.mult, partial=(eb > 0))
                    P.dma('sp', S['aT'][c0:c0 + 512, n_off + tbk * 128:n_off + (tbk + 1) * 128].re("(a p) t -> p a t", p=128), ys[:, :, :])
                P.barrier()


def hyena_layer(P, G, l, W):
    S = G.scr
    hyena_tables(P, G, LX, S['cosx'], S['sinx'])
    hyena_tables(P, G, LC, S['cosc'], S['sinc'])
    if CUT <= 1:
        return
    hyena_filters(P, G, W, LX, W['zTx'], W['tnx'], S['ksx'], S['kdx'])
    hyena_filters(P, G, W, LC, W['zTc'], W['tnc'], S['ksc'], S['kdc'])
    if CUT <= 2:
        return
    with ExitStack() as st:
        hT = P.sb([128, 8, T], BF16, st, "hT")
        norm_phase(P, G, hT)
        wb = [P.sb([128, 8, 512], BF16, st, "wb") for _ in range(2)]
        stg = [P.sb([128, 512], F32, st, "stg") for _ in range(4)]
        cnt = [0, 0]
        silu_post = lambda s_, pt_, r0_, ne_: P.act(s_, pt_, AF.Silu)
        proj_fm(P, G, hT, W['w_in'], 0, E, S['rT'], 0, wb, stg, cnt)
        proj_fm(P, G, hT, W['w_in'], E, E, S['kT'], 0, wb, stg, cnt)
        proj_fm(P, G, hT, W['w_in'], 2 * E, E, S['lwT'], 0, wb, stg, cnt)
        proj_fm(P, G, hT, W['w_in'], 3 * E, E, S['lwT'], E, wb, stg, cnt, post=silu_post)
        P.barrier()
    if CUT <= 3:
        return
    with ExitStack() as st:
        p = P.sb([128, T], F32, st, "p")
        sa = P.sb([128, T], F32, st, "sa")
        sb_ = P.sb([128, T], F32, st, "sb")
        ubf = P.sb([128, T], BF16, st, "ubf")
        stgb = [P.sb([128, 4, 128], BF16, st, "stgb") for _ in range(2)]
        cw, cb = W['cw'], W['cb']

        def conv(dst, src, blk):
            P.dma('sp', p[:, :], src, partial=False)
            P.ts(dst[:, :], p[:, :], cw[:, 1, blk:blk + 1], cb[:, blk:blk + 1], ALU.mult, ALU.add)
            for (a, b) in ((0, LC), (LC, T)):
                P.stt(dst[:, a + 1:b], p[:, a:b - 1], cw[:, 0, blk:blk + 1], dst[:, a + 1:b], ALU.mult, ALU.add, partial=True)
                P.stt(dst[:, a:b - 1], p[:, a + 1:b], cw[:, 2, blk:blk + 1], dst[:, a:b - 1], ALU.mult, ALU.add, partial=True)
        for eb in range(16):
            e0 = eb * 128
            conv(sa, S['kT'][e0:e0 + 128, :], 16 + eb)
            conv(sb_, S['lwT'][e0:e0 + 128, :], 32 + eb)
            P.tt(sa[:, :], sa[:, :], sb_[:, :], ALU.mult)
            P.dma('sp', S['kT'][e0:e0 + 128, :], sa[:, :])
            P.copy(ubf[:, :], sa[:, :], eng='act')
            for gi, g4 in enumerate(range(0, T // 128, 4)):
                n4 = min(4, T // 128 - g4)
                pt = G.psum[gi % 2]
                for j in range(n4):
                    P.mm(pt[:, j * 128:(j + 1) * 128], ubf[:, (g4 + j) * 128:(g4 + j + 1) * 128], G.idnb[:, :])
                sg = stgb[gi % 2]
                P.copy(sg[:, :n4, :], pt[:, :n4 * 128].re("p (a e) -> p a e", e=128))
                P.dma('sp', S['utok'][g4 * 128:(g4 + n4) * 128, e0:e0 + 128].re("(a p) e -> p a e", p=128), sg[:, :n4, :])
            conv(sa, S['rT'][e0:e0 + 128, :], eb)
            P.dma('sp', p[:, :], S['lwT'][E + e0:E + e0 + 128, :], partial=False)
            P.tt(sa[:, :], sa[:, :], p[:, :], ALU.mult)
            P.dma('sp', S['rT'][e0:e0 + 128, :], sa[:, :])
        P.barrier()
    if CUT <= 4:
        return
    hyena_dft(P, G, S, LC, 0, S['cosc'], S['sinc'], S['ksc'], S['kdc'])
    if CUT <= 5:
        return
    hyena_dft(P, G, S, LX, LC, S['cosx'], S['sinx'], S['ksx'], S['kdx'])
    with ExitStack() as st:
        y = P.sb([128, T], F32, st, "y")
        u = P.sb([128, T], F32, st, "u")
        gx = P.sb([128, T], F32, st, "gx")
        ob = [P.sb([128, T], BF16, st, "ob") for _ in range(2)]
        for eb in range(16):
            e0 = eb * 128
            P.dma('sp', y[:, :], S['aT'][e0:e0 + 128, :], partial=False)
            P.dma('sp', u[:, :], S['kT'][e0:e0 + 128, :], partial=False)
            P.dma('sp', gx[:, :], S['rT'][e0:e0 + 128, :], partial=False)
            P.stt(y[:, :], u[:, :], W['fbias'][:, eb:eb + 1], y[:, :], ALU.mult, ALU.add)
            P.tt(ob[eb % 2][:, :], y[:, :], gx[:, :], ALU.mult)
            P.dma('sp', S['uT'][e0:e0 + 128, :], ob[eb % 2][:, :])
        P.barrier()


def build(layers=(0, 1, 2, 3)):
    if isinstance(layers, int):
        layers = tuple(range(layers))
    nc = bass.Bass("TRN2", target_bir_lowering=False)
    P = Prog(nc)
    G = Ctx()
    G.xs_in = P.dram("xT0", [D, T], F32, "ExternalInput")
    G.cond = P.dram("cond", [128, 8, 2], F32, "ExternalInput")
    G.ada_w = P.dram("ada_w", [4, D, 3 * D], F32, "ExternalInput")
    G.w_out = P.dram("w_out", [4, E, D], F32, "ExternalInput")
    G.out = P.dram("outT", [D, LX], F32, "ExternalOutput")
    G.xs = P.dram("xs", [D, T], F32)
    S = {}
    S['rT'] = P.dram("s_rT", [E, T], F32)
    S['kT'] = P.dram("s_kT", [E, T], F32)
    S['vtok'] = P.dram("s_vtok", [T, E], F32)
    if 3 in layers and 0 not in layers:
        S['vfirst'] = P.dram("vfirst_in", [T, E], F32, "ExternalInput")
    else:
        S['vfirst'] = P.dram("s_vfirst", [T, E], F32)
    S['gtok'] = P.dram("s_gtok", [T, E], F32)
    S['loT'] = P.dram("s_loT", [256, T], F32)
    S['lovT'] = P.dram("s_lovT", [32, T], F32)
    S['uT'] = P.dram("s_uT", [E, T], BF16)
    S['aT'] = P.dram("s_aT", [2 * E, T], F32)
    S['lwT'] = P.dram("s_lwT", [2 * E, T], F32)
    G.scr = S
    st = P.stack

    def cin(name, shape, dt=F32):
        d = P.dram(name, shape, dt, "ExternalInput")
        b = P.sb(shape, dt, st, name)
        if len(shape) == 2:
            P.dma('sp', b[:, :], d[:, :], partial=False)
        else:
            P.dma('sp', b[:, :, :], d[:, :, :], partial=False)
        return b
    G.masks = cin("masks", [128, 2, 256])
    G.maskt = cin("maskt", [128, 2, 128])
    idn = cin("idn", [128, 128])
    G.blk = cin("blk", [128, 128])
    G.adab = cin("adab", [128, 96])
    G.npre = cin("npre", [128, 32])
    G.npost = cin("npost", [128, 32])
    G.idnb = P.sb([128, 128], BF16, st, "idnb")
    P.copy(G.idnb[:, :], idn[:, :])
    G.ones = P.sb([128, 128], F32, st, "ones")
    P.memset(G.ones[:, :], 1.0)
    G.psum = [P.ps([128, 512], F32, st, "ps") for _ in range(8)]
    G.A = P.sb([128, 8, 2], F32, st, "A")
    G.B = P.sb([128, 8, 2], F32, st, "B")
    G.Gt = P.sb([128, 8, 2], F32, st, "Gt")
    G.scT = P.sb([128, 8, 2], F32, st, "scT")
    P.dma('sp', G.scT[:, :, :], G.cond[:, :, :], partial=False)
    P.act(G.scT[:, :, :], G.scT[:, :, :], AF.Silu)
    for (n0, nt) in TILES:
        P.dma('sp', G.xs[:, n0:n0 + nt], G.xs_in[:, n0:n0 + nt])
    LW = {}
    for l in (0, 3):
        if l not in layers:
            continue
        W = {}
        pre = "l%d_" % l
        ncols = 4 * E + 256 + (32 if l == 3 else 0)
        W['w_in'] = P.dram(pre + "w_in", [D, ncols], F32, "ExternalInput")
        W['mu'] = cin(pre + "mu", [128, 48])
        W['om'] = P.sb([128, 48], F32, st, pre + "om")
        P.ts(W['om'][:, :], W['mu'][:, :], -1.0, 1.0, ALU.mult, ALU.add)
        W['w0'] = cin(pre + "w0", [128, 32])
        W['a0'] = cin(pre + "a0", [128, 32])
        W['w2'] = P.dram(pre + "w2", [128, E], F32, "ExternalInput")
        W['a2'] = P.dram(pre + "a2", [128, E], F32, "ExternalInput")
        W['k_k'] = cin(pre + "k_k", [128, 16])
        W['k_a'] = cin(pre + "k_a", [128, 16])
        W['r_k'] = cin(pre + "r_k", [128, 16])
        W['lnwb'] = cin(pre + "lnw", [128, 16, 64])
        W['lnbb'] = cin(pre + "lnb", [128, 16, 64])
        if l == 3:
            W['v0'] = P.dram(pre + "v0", [1, E], F32, "ExternalInput")
            W['v2'] = P.dram(pre + "v2", [32, E], F32, "ExternalInput")
        LW[l] = W
    if 1 in layers:
        W = {}
        W['w_in'] = P.dram("l1_w_in", [D, 4 * E], F32, "ExternalInput")
        W['cw'] = cin("l1_cw", [128, 3, 48])
        W['cb'] = cin("l1_cb", [128, 48])
        W['fbias'] = cin("l1_fbias", [128, 16])
        W['f_w1'] = cin("l1_f_w1", [33, 64])
        W['f_w2'] = cin("l1_f_w2", [64, 64])
        W['f_w3'] = cin("l1_f_w3", [64, 64])
        W['f_w4'] = P.dram("l1_f_w4", [64, 2 * E], F32, "ExternalInput")
        W['fb'] = cin("l1_fb", [64, 3])
        W['sf'] = cin("l1_sf", [64, 1])
        W['zTx'] = P.dram("zTx", [33, LX], F32, "ExternalInput")
        W['zTc'] = P.dram("zTc", [33, LC], F32, "ExternalInput")
        W['tnx'] = cin("tnx", [128, LX // 128])
        W['tnc'] = cin("tnc", [128, LC // 128])
        G.deltab = cin("deltab", [128, E])
        G.jrow = cin("jrow", [128, 128])
        G.tcol = cin("tcol", [128, 32])
        altf = cin("altf", [128, 2])
        G.altb = P.sb([128, 2], BF16, st, "altb")
        P.copy(G.altb[:, :], altf[:, :])
        altrf = cin("altrf", [1, 128])
        G.altrow = P.sb([1, 128], BF16, st, "altrow")
        P.copy(G.altrow[:, :], altrf[:, :])
        S['cosx'] = P.dram("s_cosx", [LX // 128, 128, LX // 128, 128], BF16)
        S['sinx'] = P.dram("s_sinx", [LX // 128, 128, LX // 128, 128], BF16)
        S['cosc'] = P.dram("s_cosc", [LC // 128, 128, LC // 128, 128], BF16)
        S['sinc'] = P.dram("s_sinc", [LC // 128, 128, LC // 128, 128], BF16)
        S['ksx'] = P.dram("s_ksx", [LX, E], BF16)
        S['kdx'] = P.dram("s_kdx", [LX, E], BF16)
        S['ksc'] = P.dram("s_ksc", [LC, E], BF16)
        S['kdc'] = P.dram("s_kdc", [LC, E], BF16)
        S['utok'] = P.dram("s_utok", [T, E], BF16)
        LW[1] = W
    if 2 in layers:
        W = {}
        W['w_in'] = P.dram("l2_w_in", [D, 5 * E], F32, "ExternalInput")
        W['lbl'] = P.dram("l2_lbl", [128, 4, 16], F32, "ExternalInput")
        W['gnb'] = cin("l2_gnb", [128, 128])
        G.mask32 = cin("mask32", [128, 2, 128])
        LW[2] = W
    P.barrier()
    for l in layers:
        adaln_phase(P, G, l)
        if l == 2:
            hgrn_layer(P, G, l, LW[l])
        if l == 1:
            hyena_layer(P, G, l, LW[l])
        if l in (0, 3):
            rwkv_layer(P, G, l, LW[l], vres=(l == 3))
            if l == 0:
                for tb in range(T // 512 + 1):
                    a0_, a1_ = tb * 512, min(T, tb * 512 + 512)
                    P.dma('sp', S['vfirst'][a0_:a1_, :], S['vtok'][a0_:a1_, :])
                P.barrier()
        outproj_phase(P, G, l, S['uT'])
    if KDBG:
        dbg = P.dram("dbg_uT", [E, T], BF16, "ExternalOutput")
        for i in range(16):
            P.dma('sp', dbg[i * 128:(i + 1) * 128, :], S['uT'][i * 128:(i + 1) * 128, :])
    for i in range(8):
        P.dma('sp', G.out[:, i * 512:(i + 1) * 512], G.xs[:, LC + i * 512:LC + (i + 1) * 512])
    P.barrier()
    return nc, P


def prep_inputs(inp, b, layers=(0, 1, 2, 3)):
    m = {}
    m['xT0'] = np.ascontiguousarray(np.concatenate([inp['ctx'][b], inp['x'][b]], axis=0).T)
    cond = np.stack([inp['c'][b], inp['c_ctx']], axis=-1)
    m['cond'] = np.ascontiguousarray(cond.reshape(8, 128, 2).transpose(1, 0, 2))
    m['ada_w'] = inp['ada_w']
    m['w_out'] = inp['w_out']
    m['adab'] = col_layout(inp['ada_b'].reshape(-1))
    m['npre'] = col_layout(inp['norm_pre'].reshape(-1))
    m['npost'] = col_layout(inp['norm_post'].reshape(-1))
    c = make_consts()
    m['masks'] = np.ascontiguousarray(c['masks'].transpose(1, 0, 2))
    m['maskt'] = np.ascontiguousarray(c['maskt'].transpose(1, 0, 2))
    m['idn'] = c['idn']
    m['blk'] = c['blk']
    if 1 in layers:
        m['l1_w_in'] = inp['l1_w_in']
        m['l1_cw'] = np.ascontiguousarray(inp['l1_conv_w'].reshape(3, 48, 128).transpose(2, 0, 1))
        m['l1_cb'] = col_layout(inp['l1_conv_b'])
        m['l1_fbias'] = col_layout(inp['l1_filter_bias'])
        m['l1_f_w1'] = inp['l1_f_w1']
        m['l1_f_w2'] = inp['l1_f_w2']
        m['l1_f_w3'] = inp['l1_f_w3']
        m['l1_f_w4'] = inp['l1_f_w4']
        m['l1_fb'] = np.ascontiguousarray(np.stack([inp['l1_f_b1'], inp['l1_f_b2'], inp['l1_f_b3']], axis=1))
        m['l1_sf'] = np.ascontiguousarray(inp['l1_sin_freq'].reshape(64, 1))
        m.update(hyena_consts())
    if 2 in layers:
        m['l2_w_in'] = inp['l2_w_in']
        m['l2_lbl'] = np.ascontiguousarray(inp['hgrn_lb_logits'].reshape(4, 16, 128).transpose(2, 0, 1))
        m['l2_gnb'] = np.ascontiguousarray(np.tile(inp['l2_g_norm'][None, :], (128, 1)))
        s_ = np.arange(128)
        same = (s_[:, None] // 32) == (s_[None, :] // 32)
        m32 = np.zeros((128, 2, 128), np.float32)
        m32[:, 0, :] = same & (s_[:, None] <= s_[None, :])
        m32[:, 1, :] = same & (s_[:, None] >= s_[None, :])
        m['mask32'] = m32
    for l in (0, 3):
        if l not in layers:
            continue
        pre = "l%d_" % l
        m[pre + 'w_in'] = inp[pre + 'w_in']
        m[pre + 'mu'] = col_layout(inp[pre + 'mu'].reshape(-1))
        m[pre + 'w0'] = col_layout(inp[pre + 'w0'].reshape(-1))
        m[pre + 'a0'] = col_layout(inp[pre + 'a0'].reshape(-1))
        m[pre + 'w2'] = np.ascontiguousarray(inp[pre + 'w2'].reshape(128, E))
        m[pre + 'a2'] = np.ascontiguousarray(inp[pre + 'a2'].reshape(128, E))
        m[pre + 'k_k'] = col_layout(inp[pre + 'k_k'])
        m[pre + 'k_a'] = col_layout(inp[pre + 'k_a'])
        m[pre + 'r_k'] = col_layout(inp[pre + 'r_k'].reshape(-1))
        lw = inp[pre + 'ln_w'].reshape(16, 2, 64)
        lb = inp[pre + 'ln_b'].reshape(16, 2, 64)
        m[pre + 'lnw'] = np.ascontiguousarray(np.repeat(lw.transpose(1, 0, 2), 64, axis=0))
        m[pre + 'lnb'] = np.ascontiguousarray(np.repeat(lb.transpose(1, 0, 2), 64, axis=0))
        if l == 3:
            m[pre + 'v0'] = inp[pre + 'v0'].reshape(1, E)
            m[pre + 'v2'] = inp[pre + 'v2']
    return m


_CACHE = {}


def kernel(**inputs):
    inp = {k: np.asarray(v) for k, v in inputs.items()}
    if 'nc' not in _CACHE:
        _CACHE['nc'] = build((0, 1, 2, 3))[0]
    nc = _CACHE['nc']
    in_maps = [prep_inputs(inp, b) for b in range(NC8)]
    res = run_bass_kernel_spmd(nc, in_maps, core_ids=list(range(NC8)))
    out = np.stack([np.ascontiguousarray(res.results[b]["outT"].T) for b in range(NC8)], axis=0)
    return out.astype(np.float32)
```

```python
import math
import os
CUT = int(os.environ.get('KCUT', '99'))
KHP = int(os.environ.get('KHP', '16'))
KDBG = int(os.environ.get('KDBG', '0'))
from contextlib import ExitStack
import numpy as np
import concourse.bass as bass
import concourse.mybir as mybir
from concourse.bass_utils import run_bass_kernel_spmd

F32 = mybir.dt.float32
BF16 = mybir.dt.bfloat16
I32 = mybir.dt.int32
AF = mybir.ActivationFunctionType
ALU = mybir.AluOpType

D = 1024
E = 2048
LX = 4096
LC = 256
T = LX + LC
NC8 = 8
EPS = 1e-6
TILES = [(0, 256)] + [(256 + 512 * i, 512) for i in range(8)]


class StopBuild(Exception):
    pass


class Buf:
    def __init__(self, t, name):
        self.t = t
        self.name = name
        self.w = {}
        self.r = {}

    def __getitem__(self, idx):
        return V(self, self.t[idx])


class V:
    def __init__(self, buf, ap):
        self.buf = buf
        self.ap = ap

    def __getitem__(self, idx):
        return V(self.buf, self.ap[idx])

    def re(self, pat, **kw):
        return V(self.buf, self.ap.rearrange(pat, **kw))

    def bc(self, shape):
        return V(self.buf, self.ap.to_broadcast(shape))


class Prog:
    NDMA = 40

    def __init__(self, nc):
        self.nc = nc
        self.stack = ExitStack()
        self.eng = {'pe': nc.tensor, 'dve': nc.vector, 'act': nc.scalar, 'pool': nc.gpsimd, 'sp': nc.sync}
        self.sem = {k: self.stack.enter_context(nc.semaphore("s_" + k)) for k in ('pe', 'dve', 'act', 'pool')}
        self.cnt = {k: 0 for k in self.sem}
        self.dsem = [self.stack.enter_context(nc.semaphore("d%d" % i)) for i in range(self.NDMA)]
        self.dval = [0] * self.NDMA
        self.dnext = 0
        self.seen = {e: {} for e in self.eng}
        self.nins = 0
        self.uid = 0

    def sb(self, shape, dt, stack=None, name=None):
        self.uid += 1
        name = (name or "t") + "_%d" % self.uid
        t = (stack or self.stack).enter_context(self.nc.sbuf_tensor(name, list(shape), dt))
        return Buf(t, name)

    def ps(self, shape, dt=F32, stack=None, name=None):
        self.uid += 1
        name = (name or "p") + "_%d" % self.uid
        t = (stack or self.stack).enter_context(self.nc.psum_tensor(name, list(shape), dt))
        return Buf(t, name)

    def dram(self, name, shape, dt, kind=None):
        if kind:
            t = self.nc.dram_tensor(name, list(shape), dt, kind=kind).ap()
        else:
            t = self.nc.dram_tensor(name, list(shape), dt).ap()
        return Buf(t, name)

    def _wait(self, e, key, val):
        if val <= 0 or (e == 'pe' and key == 'pe'):
            return
        if self.seen[e].get(key, 0) >= val:
            return
        sem = self.sem[key] if isinstance(key, str) else self.dsem[key]
        self.eng[e].wait_ge(sem, val)
        self.nins += 1
        self.seen[e][key] = val

    def _deps(self, e, reads, writes, partial):
        for b in reads:
            for k, v in b.w.items():
                self._wait(e, k, v)
        for b in writes:
            if not partial:
                for k, v in b.w.items():
                    self._wait(e, k, v)
            for k, v in b.r.items():
                self._wait(e, k, v)

    def _mark(self, key, val, reads, writes, partial):
        for b in reads:
            if b.r.get(key, 0) < val:
                b.r[key] = val
        for b in writes:
            if partial:
                b.w[key] = val
            else:
                b.w = {key: val}
                b.r = {}

    limit = None

    def op(self, e, fn, reads, writes, partial=False):
        if self.limit is not None:
            if self.limit <= 0:
                raise StopBuild()
            self.limit -= 1
        reads = [v.buf for v in reads if isinstance(v, V)]
        writes = [v.buf for v in writes]
        self._deps(e, reads, writes, partial)
        ins = fn(self.eng[e])
        self.cnt[e] += 1
        ins.then_inc(self.sem[e], 1)
        self.nins += 1
        self._mark(e, self.cnt[e], reads, writes, partial)

    def dma(self, q, out, in_, partial=True):
        self._deps(q, [in_.buf], [out.buf], partial)
        i = self.dnext
        self.dnext = (i + 1) % self.NDMA
        self._wait(q, i, self.dval[i])
        self.dval[i] += 16
        self.eng[q].dma_start(out=out.ap, in_=in_.ap).then_inc(self.dsem[i], 16)
        self.nins += 1
        self._mark(i, self.dval[i], [in_.buf], [out.buf], partial)

    def barrier(self):
        for e in self.eng:
            for k in self.sem:
                self._wait(e, k, self.cnt[k])
            for i in range(self.NDMA):
                self._wait(e, i, self.dval[i])

    def mm(self, out, lhsT, rhs, start=True, stop=True):
        self.op('pe', lambda e: e.matmul(out.ap, lhsT.ap, rhs.ap, start=start, stop=stop), [lhsT, rhs], [out], partial=True)

    def act(self, out, in_, func=AF.Identity, bias=None, scale=None, partial=False, eng='act'):
        kw = {}
        rd = [in_]
        if bias is not None:
            kw['bias'] = bias.ap if isinstance(bias, V) else bias
            rd.append(bias)
        if scale is not None:
            kw['scale'] = scale.ap if isinstance(scale, V) else scale
            rd.append(scale)
        self.op('act', lambda e: e.activation(out=out.ap, in_=in_.ap, func=func, **kw), rd, [out], partial)

    def tt(self, out, a, b, op, partial=False, eng='dve'):
        self.op(eng, lambda e: e.tensor_tensor(out=out.ap, in0=a.ap, in1=b.ap, op=op), [a, b], [out], partial)

    def ts(self, out, a, s1, s2, op0, op1=None, partial=False, eng='dve'):
        g = lambda s: s.ap if isinstance(s, V) else s
        if op1 is None:
            self.op(eng, lambda e: e.tensor_scalar(out=out.ap, in0=a.ap, scalar1=g(s1), scalar2=None, op0=op0), [a, s1], [out], partial)
        else:
            self.op(eng, lambda e: e.tensor_scalar(out=out.ap, in0=a.ap, scalar1=g(s1), scalar2=g(s2), op0=op0, op1=op1), [a, s1, s2], [out], partial)

    def stt(self, out, a, s, b, op0, op1, partial=False):
        g = s.ap if isinstance(s, V) else s
        self.op('dve', lambda e: e.scalar_tensor_tensor(out=out.ap, in0=a.ap, scalar=g, in1=b.ap, op0=op0, op1=op1), [a, s, b], [out], partial)

    def copy(self, out, in_, partial=False, eng='dve'):
        if eng == 'act':
            self.act(out, in_, AF.Identity, partial=partial)
        else:
            self.op(eng, lambda e: e.tensor_copy(out=out.ap, in_=in_.ap), [in_], [out], partial)

    def memset(self, out, val, eng='dve', partial=False):
        self.op(eng, lambda e: e.memset(out.ap, val), [], [out], partial)

    def rsqrt(self, out, in_, addc, partial=False):
        self.ts(out, in_, addc, None, ALU.add, partial=partial)
        self.act(out, out, AF.Ln, partial=partial)
        self.act(out, out, AF.Exp, scale=-0.5, partial=partial)

    def scan(self, out, d0, d1, init, op0, op1, partial=False):
        g = init.ap if isinstance(init, V) else init
        self.op('dve', lambda e: e.tensor_tensor_scan(out=out.ap, data0=d0.ap, data1=d1.ap, initial=g, op0=op0, op1=op1), [d0, d1, init], [out], partial)


def col_layout(v):
    v = np.asarray(v, np.float32)
    return np.ascontiguousarray(v.reshape(-1, 128).T)


def hyena_consts():
    c = {}
    f32 = np.float32
    for nm, L in (('x', LX), ('c', LC)):
        t = np.linspace(0.0, 1.0, L, dtype=f32)[:, None]
        freqs = np.linspace(1e-4, 15, 16, dtype=f32)[None, :]
        ang = (f32(2.0 * math.pi / L) * np.arange(L, dtype=f32)[:, None]) * freqs
        z = np.concatenate([t, np.cos(ang), -np.sin(ang)], axis=-1).astype(f32)
        c['zT' + nm] = np.ascontiguousarray(z.T)
        c['tn' + nm] = np.ascontiguousarray(-t[:, 0].reshape(L // 128, 128).T)
    deltas = np.abs(np.linspace(math.log(1e-2) / 1.5, math.log(1e-2) / 0.3, E, dtype=f32))
    c['deltab'] = np.ascontiguousarray(np.tile(deltas[None, :], (128, 1)).astype(f32))
    c['jrow'] = np.ascontiguousarray(np.tile(np.arange(128, dtype=f32)[None, :], (128, 1)))
    c['tcol'] = np.ascontiguousarray((np.arange(32, dtype=f32)[None, :] * 128 + np.arange(128, dtype=f32)[:, None]))
    alt = np.where(np.arange(128) % 2 == 0, 1.0, -1.0).astype(f32)
    c['altf'] = np.ascontiguousarray(np.stack([alt, alt], axis=1))
    c['altrf'] = np.ascontiguousarray(alt.reshape(1, 128))
    return c


def make_consts():
    c = {}
    idn = np.eye(128, dtype=np.float32)
    s = np.arange(128)
    blk = (s[:, None] // 64) == (s[None, :] // 64)
    sl = s % 64
    m = np.zeros((2, 128, 256), np.float32)
    m[0, :, :128] = blk & (sl[:, None] < sl[None, :])
    m[0, :, 128:] = blk & (sl[:, None] <= sl[None, :])
    m[1, :, :128] = blk & (sl[:, None] > sl[None, :])
    m[1, :, 128:] = blk & (sl[:, None] >= sl[None, :])
    c['masks'] = m
    mt = np.zeros((2, 128, 128), np.float32)
    mt[0] = m[0, :, :128].T
    mt[1] = m[1, :, :128].T
    c['maskt'] = mt
    c['idn'] = idn
    c['blk'] = blk.astype(np.float32)
    return c


class Ctx:
    pass


def adaln_phase(P, G, l):
    with ExitStack() as st:
        wt = [P.sb([128, 8, 512], F32, st, "adaw") for _ in range(2)]
        mod = P.sb([128, 24, 2], F32, st, "mod")
        for nb in range(6):
            w = wt[nb % 2]
            P.dma('sp', w[:, :, :], G.ada_w[l, :, nb * 512:(nb + 1) * 512].re("(c p) n -> p c n", p=128), partial=False)
            pt = G.psum[nb % 2]
            for j in range(4):
                ob = nb * 4 + j
                for kc in range(8):
                    P.mm(pt[:, j * 2:j * 2 + 2], w[:, kc, j * 128:(j + 1) * 128], G.scT[:, kc, :], start=(kc == 0), stop=(kc == 7))
            P.tt(mod[:, nb * 4:nb * 4 + 4, :], pt[:, 0:8].re("p (a b) -> p a b", b=2),
                 G.adab[:, l * 24 + nb * 4:l * 24 + nb * 4 + 4].re("p (a o) -> p a o", o=1).bc([128, 4, 2]), ALU.add, partial=True)
        P.ts(G.A[:, :, :], mod[:, 8:16, :], 1.0, 32.0, ALU.add, ALU.mult)
        P.tt(G.A[:, :, :], G.A[:, :, :], G.npre[:, l * 8:(l + 1) * 8].re("p (a o) -> p a o", o=1).bc([128, 8, 2]), ALU.mult)
        P.copy(G.B[:, :, :], mod[:, 0:8, :])
        P.ts(G.Gt[:, :, :], mod[:, 16:24, :], 32.0, None, ALU.mult)
        P.tt(G.Gt[:, :, :], G.Gt[:, :, :], G.npost[:, l * 8:(l + 1) * 8].re("p (a o) -> p a o", o=1).bc([128, 8, 2]), ALU.mult)
        P.barrier()


def norm_phase(P, G, hT):
    with ExitStack() as st:
        xt = [P.sb([128, 8, 512], F32, st, "xt") for _ in range(2)]
        sq = P.sb([128, 8, 512], F32, st, "sq")
        rs = P.sb([128, 512], F32, st, "rs")
        tmp = [P.sb([128, 512], F32, st, "tmp") for _ in range(2)]
        for ti, (n0, nt) in enumerate(TILES):
            j = 1 if n0 == 0 else 0
            x = xt[ti % 2]
            P.dma('sp', x[:, :, :nt], G.xs[:, n0:n0 + nt].re("(c p) n -> p c n", p=128), partial=False)
            P.act(sq[:, :, :nt], x[:, :, :nt], AF.Square)
            pt = G.psum[ti % 2]
            for c in range(8):
                P.mm(pt[:, :nt], G.ones[:, :], sq[:, c, :nt], start=(c == 0), stop=(c == 7))
            P.rsqrt(rs[:, :nt], pt[:, :nt], 1024.0 * EPS)
            for c in range(8):
                t = tmp[c % 2]
                P.tt(t[:, :nt], x[:, c, :nt], rs[:, :nt], ALU.mult)
                P.act(hT[:, c, n0:n0 + nt], t[:, :nt], AF.Identity, bias=G.B[:, c, j:j + 1], scale=G.A[:, c, j:j + 1], partial=True)
        P.barrier()


def outproj_phase(P, G, l, uT):
    with ExitStack() as st:
        wo = P.sb([128, 16, 1024], BF16, st, "wo")
        P.dma('pool', wo[:, :, :], G.w_out[l].re("(c p) n -> p c n", p=128), partial=False)
        ut = [P.sb([128, 16, 512], BF16, st, "ut") for _ in range(2)]
        o = P.sb([128, 8, 512], F32, st, "o")
        sq = P.sb([128, 8, 512], F32, st, "sq")
        rs = P.sb([128, 512], F32, st, "rs")
        xt = [P.sb([128, 8, 512], F32, st, "xt") for _ in range(2)]
        for ti, (n0, nt) in enumerate(TILES):
            j = 1 if n0 == 0 else 0
            u = ut[ti % 2]
            x = xt[ti % 2]
            P.dma('sp', u[:, :, :nt], uT[:, n0:n0 + nt].re("(c p) n -> p c n", p=128), partial=False)
            P.dma('sp', x[:, :, :nt], G.xs[:, n0:n0 + nt].re("(c p) n -> p c n", p=128), partial=False)
            for ob in range(8):
                pt = G.psum[ob % 4]
                for ec in range(16):
                    P.mm(pt[:, :nt], wo[:, ec, ob * 128:(ob + 1) * 128], u[:, ec, :nt], start=(ec == 0), stop=(ec == 15))
                P.copy(o[:, ob, :nt], pt[:, :nt], partial=True, eng=('act' if ob % 2 else 'dve'))
            P.act(sq[:, :, :nt], o[:, :, :nt], AF.Square)
            pt = G.psum[4 + ti % 2]
            for c in range(8):
                P.mm(pt[:, :nt], G.ones[:, :], sq[:, c, :nt], start=(c == 0), stop=(c == 7))
            P.rsqrt(rs[:, :nt], pt[:, :nt], 1024.0 * EPS)
            for c in range(8):
                P.tt(o[:, c, :nt], o[:, c, :nt], rs[:, :nt], ALU.mult, partial=True)
                P.stt(x[:, c, :nt], o[:, c, :nt], G.Gt[:, c, j:j + 1], x[:, c, :nt], ALU.mult, ALU.add, partial=True)
            P.dma('sp', G.xs[:, n0:n0 + nt].re("(c p) n -> p c n", p=128), x[:, :, :nt])
        P.barrier()


def load_w_block(P, dst, w_dram, c0, ncols):
    P.dma('pool', dst[:, :, :ncols], w_dram[:, c0:c0 + ncols].re("(c p) n -> p c n", p=128), partial=False)


def proj_fm(P, G, src, w_dram, c0, ncols, dst_dram, r0, wbufs, stg, cnt, post=None):
    for cb in range(0, ncols, 512):
        nb = min(512, ncols - cb)
        w = wbufs[cnt[0] % 2]
        cnt[0] += 1
        load_w_block(P, w, w_dram, c0 + cb, nb)
        for (n0, nt) in TILES:
            for eb in range(0, nb, 128):
                ne = min(128, nb - eb)
                k = cnt[1] % 4
                cnt[1] += 1
                pt = G.psum[k]
                for c in range(8):
                    P.mm(pt[:ne, :nt], w[:, c, eb:eb + ne], src[:, c, n0:n0 + nt], start=(c == 0), stop=(c == 7))
                s = stg[k]
                if post is None:
                    P.copy(s[:ne, :nt], pt[:ne, :nt], eng=('act' if k % 2 else 'dve'))
                else:
                    post(s[:ne, :nt], pt[:ne, :nt], r0 + cb + eb, ne)
                P.dma('sp', dst_dram[r0 + cb + eb:r0 + cb + eb + ne, n0:n0 + nt], s[:ne, :nt])


def proj_tm(P, G, src, w_dram, c0, ncols, dst_dram, wbufs, stg, cnt, func=None):
    for cb in range(0, ncols, 512):
        nb = min(512, ncols - cb)
        w = wbufs[cnt[0] % 2]
        cnt[0] += 1
        load_w_block(P, w, w_dram, c0 + cb, nb)
        for tb in range(T // 128):
            k = cnt[1] % 4
            cnt[1] += 1
            pt = G.psum[k]
            for c in range(8):
                P.mm(pt[:, :nb], src[:, c, tb * 128:(tb + 1) * 128], w[:, c, :nb], start=(c == 0), stop=(c == 7))
            s = stg[k]
            if func is None:
                P.copy(s[:, :nb], pt[:, :nb], eng=('act' if k % 2 else 'dve'))
            else:
                P.act(s[:, :nb], pt[:, :nb], func)
            P.dma('sp', dst_dram[tb * 128:(tb + 1) * 128, cb:cb + nb], s[:, :nb])


def rwkv_layer(P, G, l, W, vres):
    nc = P.nc
    S = G.scr
    with ExitStack() as st:
        hT = P.sb([128, 8, T], BF16, st, "hT")
        norm_phase(P, G, hT)
        hm = P.sb([128, 8, T], BF16, st, "hm")
        wb = [P.sb([128, 8, 512], BF16, st, "wb") for _ in range(2)]
        stg = [P.sb([128, 512], F32, st, "stg") for _ in range(4)]
        cnt = [0, 0]
        mu = W['mu']
        ncol_base = [0, E, 2 * E, 3 * E, 4 * E, 4 * E + 128]
        for g in range(6):
            for c in range(8):
                mcol = mu[:, g * 8 + c:g * 8 + c + 1]
                ocol = W['om'][:, g * 8 + c:g * 8 + c + 1]
                P.act(hm[:, c, :], hT[:, c, :], AF.Identity, scale=ocol, partial=True)
                if c < 4:
                    P.stt(hm[:, c, 1:LC], hT[:, c, 0:LC - 1], mcol, hm[:, c, 1:LC], ALU.mult, ALU.add, partial=True)
                else:
                    P.stt(hm[:, c, 0:LC - 1], hT[:, c, 1:LC], mcol, hm[:, c, 0:LC - 1], ALU.mult, ALU.add, partial=True)
                hx = hT[:, c, LC:].re("p (r w) -> p r w", w=64)
                mx = hm[:, c, LC:].re("p (r w) -> p r w", w=64)
                if c < 2:
                    P.stt(mx[:, :, 1:64], hx[:, :, 0:63], mcol, mx[:, :, 1:64], ALU.mult, ALU.add, partial=True)
                elif c < 4:
                    P.stt(mx[:, :, 0:63], hx[:, :, 1:64], mcol, mx[:, :, 0:63], ALU.mult, ALU.add, partial=True)
                elif c < 6:
                    P.stt(mx[:, 1:64, :], hx[:, 0:63, :], mcol, mx[:, 1:64, :], ALU.mult, ALU.add, partial=True)
                else:
                    P.stt(mx[:, 0:63, :], hx[:, 1:64, :], mcol, mx[:, 0:63, :], ALU.mult, ALU.add, partial=True)
            c0 = ncol_base[g]
            if g == 0:
                proj_fm(P, G, hm, W['w_in'], c0, E, S['rT'], 0, wb, stg, cnt)
            elif g == 1:
                proj_fm(P, G, hm, W['w_in'], c0, E, S['kT'], 0, wb, stg, cnt)
            elif g == 2:
                proj_tm(P, G, hm, W['w_in'], c0, E, S['vtok'], wb, stg, cnt)
                if vres:
                    proj_fm(P, G, hm, W['w_in'], 4 * E + 256, 32, S['lovT'], 0, wb, stg, cnt)
            elif g == 3:
                proj_tm(P, G, hm, W['w_in'], c0, E, S['gtok'], wb, stg, cnt, func=AF.Silu)
            else:
                proj_fm(P, G, hm, W['w_in'], c0, 128, S['loT'], (g - 4) * 128, wb, stg, cnt)
        P.barrier()
    if CUT <= 1:
        return
    if vres:
        with ExitStack() as st:
            lov = P.sb([33, T], F32, st, "lov")
            v2a = P.sb([33, E], F32, st, "v2a")
            P.memset(lov[:, :], 1.0)
            P.dma('sp', lov[0:32, :], S['lovT'][0:32, :], partial=False)
            P.dma('sp', v2a[0:32, :], W['v2'][:, :])
            P.dma('sp', v2a[32:33, :], W['v0'][:, :])
            vt = [P.sb([128, E], F32, st, "vt") for _ in range(2)]
            vf = [P.sb([128, E], F32, st, "vf") for _ in range(2)]
            sg = P.sb([128, E], F32, st, "sg")
            for tb in range(T // 128):
                v = vt[tb % 2]
                f = vf[tb % 2]
                P.dma('sp', v[:, :], S['vtok'][tb * 128:(tb + 1) * 128, :], partial=False)
                P.dma('sp', f[:, :], S['vfirst'][tb * 128:(tb + 1) * 128, :], partial=False)
                for q in range(4):
                    pt = G.psum[q]
                    P.mm(pt[:, :], lov[:, tb * 128:(tb + 1) * 128], v2a[:, q * 512:(q + 1) * 512])
                    P.act(sg[:, q * 512:(q + 1) * 512], pt[:, :], AF.Sigmoid, partial=True)
                P.tt(f[:, :], f[:, :], v[:, :], ALU.subtract)
                P.tt(f[:, :], f[:, :], sg[:, :], ALU.mult)
                P.tt(v[:, :], v[:, :], f[:, :], ALU.add)
                P.dma('sp', S['vtok'][tb * 128:(tb + 1) * 128, :], v[:, :])
            P.barrier()
    with ExitStack() as st:
        lo = P.sb([128, 2, T], F32, st, "lo")
        P.dma('sp', lo[:, 0, :], S['loT'][0:128, :], partial=False)
        P.dma('sp', lo[:, 1, :], S['loT'][128:256, :], partial=True)
        P.act(lo[:, 0, :], lo[:, 0, :], AF.Tanh, partial=True)
        w2 = P.sb([128, E], F32, st, "w2")
        a2 = P.sb([128, E], F32, st, "a2")
        P.dma('sp', w2[:, :], W['w2'][:, :], partial=False)
        P.dma('sp', a2[:, :], W['a2'][:, :], partial=False)
        sa = [P.sb([128, 512], F32, st, "sa") for _ in range(4)]
        i = 0
        for hp in range(16):
            e0 = hp * 128
            for z in range(2):
                zs = slice(z * 64, z * 64 + 64)
                for (n0, nt) in TILES:
                    pt = G.psum[i % 4]
                    s1 = sa[i % 4]
                    P.mm(pt[:, :nt], a2[zs, e0:e0 + 128], lo[zs, 1, n0:n0 + nt])
                    P.act(s1[:, :nt], pt[:, :nt], AF.Sigmoid, bias=W['a0'][:, z * 16 + hp:z * 16 + hp + 1])
                    P.dma('sp', S['aT'][z * E + e0:z * E + e0 + 128, n0:n0 + nt], s1[:, :nt])
                    i += 1
                    pt = G.psum[i % 4]
                    s2 = sa[i % 4]
                    P.mm(pt[:, :nt], w2[zs, e0:e0 + 128], lo[zs, 0, n0:n0 + nt])
                    P.act(s2[:, :nt], pt[:, :nt], AF.Sigmoid, bias=W['w0'][:, z * 16 + hp:z * 16 + hp + 1])
                    P.ts(s2[:, :nt], s2[:, :nt], -math.exp(-0.5), None, ALU.mult)
                    P.dma('sp', S['lwT'][z * E + e0:z * E + e0 + 128, n0:n0 + nt], s2[:, :nt])
                    i += 1
        P.barrier()
    if CUT <= 2:
        return
    with ExitStack() as st:
        NCH = T // 64
        NSLOT = 2 if KHP >= 2 else 1
        HB = []
        for s_ in range(NSLOT):
            H = Ctx()
            H.vst = P.sb([128, NCH, 64], BF16, st, "vst")
            H.ybuf = P.sb([128, NCH, 64], F32, st, "ybuf")
            H.bon = P.sb([128, NCH, 1], F32, st, "bon")
            HB.append(H)
        HC = NCH // 2
        gt_ = P.sb([128, NCH, 64], BF16, st, "gt_")
        tmpR = P.sb([128, HC, 64], F32, st, "tmpR")
        stat = P.sb([128, NCH, 2], F32, st, "stat")
        uob = P.sb([128, HC, 128], BF16, st, "uob")
        P.memset(uob[:, :, :], 0.0, eng='pool')
        uT_sb = P.sb([128, T // 2], BF16, st, "uT_sb")
        onesb = P.sb([128, 2], BF16, st, "onesb")
        P.memset(onesb[:, :], 1.0)
        idst = P.sb([128, 64], BF16, st, "idst")
        P.copy(idst[0:64, :], G.idnb[0:64, 0:64], partial=True)
        P.copy(idst[64:128, :], G.idnb[64:128, 64:128], partial=True)
        ones512 = P.sb([128, 128], F32, st, "ones128")
        P.memset(ones512[:, :], 1.0)
        pp = G.psum
        CH = []
        for ci_ in range(2 * NSLOT):
            C = Ctx()
            C.wk = [P.sb([128, 128], F32, st, "wk") for _ in range(14)]
            C.AR = P.sb([128, 2, 256], BF16, st, "AR")
            C.Kt = P.sb([128, 2, 128], BF16, st, "Kt")
            C.Bt = P.sb([128, 2, 128], BF16, st, "Bt")
            C.Pb = P.sb([128, 2, 128], BF16, st, "Pb")
            for b_ in (C.AR, C.Kt, C.Bt, C.Pb):
                P.memset(b_[:, :, :], 0.0, eng='pool')
            C.cend = P.sb([128, 4], F32, st, "cend")
            C.Mk = P.sb([128, 256], BF16, st, "Mk")
            C.Mb = P.sb([128, 256], BF16, st, "Mb")
            C.PS = [P.sb([128, 256], BF16, st, "PSb") for _ in range(2)]
            C.PT = [P.sb([128, 128], BF16, st, "PTb") for _ in range(2)]
            C.KtT = P.sb([128, 128], BF16, st, "KtT")
            C.BtT = P.sb([128, 128], BF16, st, "BtT")
            C.Wt = P.sb([128, 64], BF16, st, "Wt")
            C.Ut = P.sb([128, 64], BF16, st, "Ut")
            C.Sf = P.sb([128, 64], F32, st, "Sf")
            C.Sb = [P.sb([128, 64], BF16, st, "Sb") for _ in range(2)]
            C.tmpS = P.sb([128, 64], F32, st, "tmpS")
            C.banks = (pp[2 * ci_], pp[2 * ci_ + 1])
            CH.append(C)
        STILES = [(128 * i, 128) for i in range(T // 128)]

        def chain(hp, z, C, H):
            e0 = hp * 128
            vst, ybuf, bon = H.vst, H.ybuf, H.bon
            r, k, a, lw, t1, t2, kk, Gc, Ec, kd, eg, ex, ei, bco = C.wk
            AR, Kt, Bt, Pb, cend = C.AR, C.Kt, C.Bt, C.Pb, C.cend
            Mk, Mb, PS_, PT_, KtT, BtT, Wt, Ut, Sf, Sb, tmpS = C.Mk, C.Mb, C.PS, C.PT, C.KtT, C.BtT, C.Wt, C.Ut, C.Sf, C.Sb, C.tmpS
            bD, b2 = C.banks
            b0 = b1 = bD
            tiles = list(STILES) if z == 0 else [STILES[1], STILES[0]] + list(STILES[:1:-1])
            P.memset(Sf[:, :], 0.0); yield
            P.memset(Sb[0][:, :], 0.0); yield
            sbi = 0
            for (n0, nt) in tiles:
                ncn = nt // 64
                cb = n0 // 64
                P.dma('sp', r[:, :nt], S['rT'][e0:e0 + 128, n0:n0 + nt], partial=False); yield
                P.dma('sp', k[:, :nt], S['kT'][e0:e0 + 128, n0:n0 + nt], partial=False); yield
                P.dma('sp', a[:, :nt], S['aT'][z * E + e0:z * E + e0 + 128, n0:n0 + nt], partial=False); yield
                P.dma('sp', lw[:, :nt], S['lwT'][z * E + e0:z * E + e0 + 128, n0:n0 + nt], partial=False); yield
                P.ts(t1[:, :nt], k[:, :nt], W['k_k'][:, hp:hp + 1], None, ALU.mult); yield
                P.tt(t2[:, :nt], t1[:, :nt], t1[:, :nt], ALU.mult); yield
                P.mm(b1[:, :nt], G.blk[:, :], t2[:, :nt]); yield
                P.ts(kk[:, :nt], b1[:, :nt], 1e-24, None, ALU.add); yield
                P.act(kk[:, :nt], kk[:, :nt], AF.Ln); yield
                P.act(kk[:, :nt], kk[:, :nt], AF.Exp, scale=-0.5); yield
                P.tt(kk[:, :nt], kk[:, :nt], t1[:, :nt], ALU.mult); yield
                P.scan(Gc[:, :nt], ones512[:, :nt], lw[:, :nt], 0.0, ALU.mult, ALU.add); yield
                P.tt(Ec[:, :nt], Gc[:, :nt], lw[:, :nt], ALU.subtract); yield
                v3 = lambda b_: b_[:, :nt].re("p (c s) -> p c s", s=64)
                G3, E3, t13, t23 = v3(Gc), v3(Ec), v3(t1), v3(t2)
                ce3 = cend[:, :ncn].re("p (c o) -> p c o", o=1)
                if z == 0:
                    base = E3[:, :, 0:1].bc([128, ncn, 64])
                    P.tt(t13, G3, base, ALU.subtract); yield
                    P.tt(t23, E3, base, ALU.subtract); yield
                else:
                    base = G3[:, :, 63:64].bc([128, ncn, 64])
                    P.tt(t13, base, E3, ALU.subtract); yield
                    P.tt(t23, base, G3, ALU.subtract); yield
                P.tt(ce3, G3[:, :, 63:64], E3[:, :, 0:1], ALU.subtract); yield
                P.act(cend[:, :ncn], cend[:, :ncn], AF.Exp); yield
                P.ts(kd[:, :nt], a[:, :nt], -1.0, W['k_a'][:, hp:hp + 1], ALU.add, ALU.mult); yield
                P.stt(kd[:, :nt], kd[:, :nt], 1.0, k[:, :nt], ALU.add, ALU.mult); yield
                P.act(eg[:, :nt], t1[:, :nt], AF.Exp); yield
                P.act(ex[:, :nt], t2[:, :nt], AF.Exp); yield
                P.act(ei[:, :nt], t1[:, :nt], AF.Exp, scale=-1.0); yield
                P.tt(bco[:, :nt], kk[:, :nt], a[:, :nt], ALU.mult); yield
                r3, kk3, kd3, eg3, ex3, ei3, bc3 = v3(r), v3(kk), v3(kd), v3(eg), v3(ex), v3(ei), v3(bco)
                for hpar in range(2):
                    ps_ = slice(hpar * 64, hpar * 64 + 64)
                    cs = slice(hpar * 64, hpar * 64 + 64)
                    P.stt(AR[ps_, :ncn, cs], kk3[ps_], -1.0, ex3[ps_], ALU.mult, ALU.mult, partial=True); yield
                    P.tt(AR[ps_, :ncn, 128 + hpar * 64:128 + hpar * 64 + 64], r3[ps_], eg3[ps_], ALU.mult, partial=True); yield
                    P.tt(Kt[ps_, :ncn, cs], kd3[ps_], ei3[ps_], ALU.mult, partial=True); yield
                    P.tt(Bt[ps_, :ncn, cs], bc3[ps_], ei3[ps_], ALU.mult, partial=True); yield
                    P.stt(Pb[ps_, :ncn, cs], r3[ps_], W['r_k'][ps_, hp:hp + 1], kd3[ps_], ALU.mult, ALU.mult, partial=True); yield
                corder = range(ncn) if z == 0 else range(ncn - 1, -1, -1)
                for cl in corder:
                    c = cb + cl
                    ARc, Ktc, Btc = AR[:, cl, :], Kt[:, cl, :], Bt[:, cl, :]
                    P.mm(bD[:, 0:256], Ktc, ARc); yield
                    P.mm(bD[:, 256:512], Btc, ARc); yield
                    P.tt(Mk[:, :], bD[:, 0:256], G.masks[:, z, :], ALU.mult); yield
                    P.tt(Mb[:, :], bD[:, 256:512], G.masks[:, z, :], ALU.mult); yield
                    P.mm(bD[:, 0:128], AR[:, cl, 0:128], Btc); yield
                    P.tt(PT_[0][:, :], bD[:, 0:128], G.maskt[:, z, :], ALU.mult); yield
                    P.mm(bD[:, 128:256], Ktc, G.idnb[:, :]); yield
                    P.mm(bD[:, 256:384], Btc, G.idnb[:, :]); yield
                    P.mm(b2[:, 0:128], PT_[0][:, :], Mb[:, 0:128]); yield
                    P.mm(b2[:, 256:384], Mb[:, 0:128], PT_[0][:, :]); yield
                    P.copy(KtT[:, :], bD[:, 128:256]); yield
                    P.copy(BtT[:, :], bD[:, 256:384]); yield
                    P.copy(PS_[1][:, 0:128], b2[:, 0:128], partial=True, eng='act'); yield
                    P.tt(PS_[1][:, 128:256], Mb[:, 0:128], G.idnb[:, :], ALU.add, partial=True); yield
                    P.copy(PT_[1][:, :], b2[:, 256:384], eng='act'); yield
                    cur = 1
                    for lev in range(1, 6):
                        nxt = 1 - cur
                        if lev < 5:
                            P.mm(b2[:, 0:128], PT_[cur][:, :], PS_[cur][:, 0:128]); yield
                            P.mm(b2[:, 128:256], PT_[cur][:, :], PS_[cur][:, 128:256], start=True, stop=False); yield
                            P.mm(b2[:, 128:256], G.idnb[:, :], PS_[cur][:, 128:256], start=False, stop=True); yield
                            P.mm(b2[:, 256:384], PS_[cur][:, 0:128], PT_[cur][:, :]); yield
                            P.copy(PS_[nxt][:, :], b2[:, 0:256], eng='act'); yield
                            P.copy(PT_[nxt][:, :], b2[:, 256:384], eng='act'); yield
                        else:
                            P.mm(b2[:, 128:256], PT_[cur][:, :], PS_[cur][:, 128:256], start=True, stop=False); yield
                            P.mm(b2[:, 128:256], G.idnb[:, :], PS_[cur][:, 128:256], start=False, stop=True); yield
                            P.copy(PS_[nxt][:, 128:256], b2[:, 128:256], partial=True, eng='act'); yield
                        cur = nxt
                    X = PS_[cur][:, 128:256]
                    S0 = Sb[sbi]
                    P.mm(b2[:, 384:448], AR[:, cl, 0:128], S0[:, :], start=True, stop=False); yield
                    P.mm(b2[:, 384:448], Mk[:, 0:128], vst[:, c, :], start=False, stop=True); yield
                    P.copy(Wt[:, :], b2[:, 384:448], eng='act'); yield
                    P.mm(b2[:, 448:512], X, Wt[:, :]); yield
                    P.copy(Ut[:, :], b2[:, 448:512], eng='act'); yield
                    P.mm(bD[:, 448:512], AR[:, cl, 128:256], S0[:, :], start=True, stop=False); yield
                    P.mm(bD[:, 448:512], Mk[:, 128:256], vst[:, c, :], start=False, stop=False); yield
                    P.mm(bD[:, 448:512], Mb[:, 128:256], Ut[:, :], start=False, stop=True); yield
                    P.mm(bD[:, 0:2], Pb[:, cl, :], onesb[:, :]); yield
                    P.mm(bD[:, 384:448], KtT[:, :], vst[:, c, :], start=True, stop=False); yield
                    P.mm(bD[:, 384:448], BtT[:, :], Ut[:, :], start=False, stop=True); yield
                    P.tt(ybuf[:, c, :], ybuf[:, c, :], bD[:, 448:512], ALU.add, partial=True); yield
                    P.tt(bon[:, c, :], bon[:, c, :], bD[:, 0:1], ALU.add, partial=True); yield
                    P.tt(tmpS[:, :], bD[:, 384:448], Sf[:, :], ALU.add); yield
                    P.ts(Sf[:, :], tmpS[:, :], cend[:, cl:cl + 1], None, ALU.mult); yield
                    sbi = 1 - sbi
                    P.act(Sb[sbi][:, :], tmpS[:, :], AF.Identity, scale=cend[:, cl:cl + 1]); yield

        for hp0 in range(0, KHP, NSLOT):
            hps = list(range(hp0, min(KHP, hp0 + NSLOT)))
            gens = []
            for si, hp in enumerate(hps):
                e0 = hp * 128
                H = HB[si]
                for hpar in range(2):
                    for (n0, nt) in TILES:
                        P.dma('pool', H.vst[hpar * 64:(hpar + 1) * 64, n0 // 64:(n0 + nt) // 64, :],
                              S['vtok'][n0:n0 + nt, e0 + hpar * 64:e0 + hpar * 64 + 64].re("(c s) v -> s c v", s=64), partial=(hpar == 1 or n0 > 0))
                P.memset(H.ybuf[:, :, :], 0.0)
                P.memset(H.bon[:, :, :], 0.0)
                for z in range(2):
                    gens.append(chain(hp, z, CH[2 * si + z], H))
            while gens:
                for g_ in list(gens):
                    try:
                        next(g_)
                    except StopIteration:
                        gens.remove(g_)
            for si, hp in enumerate(hps):
                e0 = hp * 128
                H = HB[si]
                vst, ybuf, bon = H.vst, H.ybuf, H.bon
                for hpar in range(2):
                    for (n0, nt) in TILES:
                        P.dma('pool', gt_[hpar * 64:(hpar + 1) * 64, n0 // 64:(n0 + nt) // 64, :],
                              S['gtok'][n0:n0 + nt, e0 + hpar * 64:e0 + hpar * 64 + 64].re("(c s) v -> s c v", s=64), partial=(hpar == 1 or n0 > 0))
                P.op('dve', lambda e: e.tensor_reduce(out=stat[:, :, 0:1].ap, in_=ybuf[:, :, :].ap, axis=mybir.AxisListType.X, op=ALU.add),
                     [ybuf[:, :, :]], [stat[:, :, 0:1]], partial=True)
                P.ts(stat[:, :, 0:1], stat[:, :, 0:1], 1.0 / 64, None, ALU.mult, partial=True)
                P.tt(ybuf[:, :, :], ybuf[:, :, :], stat[:, :, 0:1].bc([128, NCH, 64]), ALU.subtract)
                for hf_ in range(2):
                    hs = slice(hf_ * HC, (hf_ + 1) * HC)
                    P.tt(tmpR[:, :, :], ybuf[:, hs, :], ybuf[:, hs, :], ALU.mult)
                    P.op('dve', lambda e, hs=hs: e.tensor_reduce(out=stat[:, hs, 1:2].ap, in_=tmpR[:, :, :].ap, axis=mybir.AxisListType.X, op=ALU.add),
                         [tmpR[:, :, :]], [stat[:, hs, 1:2]], partial=True)
                P.ts(stat[:, :, 1:2], stat[:, :, 1:2], 1.0 / 64, None, ALU.mult, partial=True)
                P.rsqrt(stat[:, :, 1:2], stat[:, :, 1:2], 64e-5, partial=True)
                P.tt(ybuf[:, :, :], ybuf[:, :, :], stat[:, :, 1:2].bc([128, NCH, 64]), ALU.mult)
                P.tt(ybuf[:, :, :], ybuf[:, :, :], W['lnwb'][:, hp, :].re("p (o v) -> p o v", o=1).bc([128, NCH, 64]), ALU.mult)
                P.tt(ybuf[:, :, :], ybuf[:, :, :], W['lnbb'][:, hp, :].re("p (o v) -> p o v", o=1).bc([128, NCH, 64]), ALU.add)
                for hf_ in range(2):
                    hs = slice(hf_ * HC, (hf_ + 1) * HC)
                    P.tt(tmpR[:, :, :], vst[:, hs, :], bon[:, hs, :].bc([128, HC, 64]), ALU.mult)
                    P.tt(ybuf[:, hs, :], ybuf[:, hs, :], tmpR[:, :, :], ALU.add, partial=True)
                    for hpar in range(2):
                        ps_ = slice(hpar * 64, hpar * 64 + 64)
                        P.tt(uob[ps_, :, hpar * 64:hpar * 64 + 64], ybuf[ps_, hs, :], gt_[ps_, hs, :], ALU.mult, partial=True)
                    for c in range(HC):
                        pt = pp[6 + c % 2]
                        P.mm(pt[:, 0:64], uob[:, c, :], idst[:, :])
                        P.copy(uT_sb[:, c * 64:(c + 1) * 64], pt[:, 0:64], partial=True, eng=('act' if c % 2 else 'dve'))
                    P.dma('sp', S['uT'][e0:e0 + 128, hf_ * (T // 2):(hf_ + 1) * (T // 2)], uT_sb[:, :])
        P.barrier()


def hgrn_layer(P, G, l, W):
    S = G.scr
    with ExitStack() as st:
        hT = P.sb([128, 8, T], BF16, st, "hT")
        norm_phase(P, G, hT)
        wb = [P.sb([128, 8, 512], BF16, st, "wb") for _ in range(2)]
        stg = [P.sb([128, 512], F32, st, "stg") for _ in range(4)]
        cnt = [0, 0]
        silu_post = lambda s_, pt_, r0_, ne_: P.act(s_, pt_, AF.Silu)
        proj_fm(P, G, hT, W['w_in'], 0, E, S['rT'], 0, wb, stg, cnt, post=silu_post)
        proj_fm(P, G, hT, W['w_in'], E, 2 * E, S['aT'], 0, wb, stg, cnt)
        proj_tm(P, G, hT, W['w_in'], 3 * E, E, S['vtok'], wb, stg, cnt)
        proj_tm(P, G, hT, W['w_in'], 4 * E, E, S['gtok'], wb, stg, cnt, func=AF.Silu)
        P.barrier()
    if CUT <= 1:
        return
    with ExitStack() as st:
        NSC = T // 128
        lg = P.sb([128, 4, 16], F32, st, "lg")
        P.dma('sp', lg[:, :, :], W['lbl'][:, :, :], partial=False)
        P.act(lg[:, :, :], lg[:, :, :], AF.Exp)
        ssum = P.sb([128, 16], F32, st, "ssum")
        lb = P.sb([128, 16], F32, st, "lb")
        oml = P.sb([128, 16], F32, st, "oml")
        P.tt(ssum[:, :], lg[:, 0, :], lg[:, 1, :], ALU.add)
        P.tt(ssum[:, :], ssum[:, :], lg[:, 2, :], ALU.add)
        P.tt(ssum[:, :], ssum[:, :], lg[:, 3, :], ALU.add)
        lo_, hi_ = 1, l
        P.copy(lb[:, :], lg[:, 1, :])
        for i_ in range(2, l + 1):
            P.tt(lb[:, :], lb[:, :], lg[:, i_, :], ALU.add)
        P.op('dve', lambda e: e.reciprocal(out=ssum[:, :].ap, in_=ssum[:, :].ap), [ssum[:, :]], [ssum[:, :]])
        P.tt(lb[:, :], lb[:, :], ssum[:, :], ALU.mult)
        P.ts(oml[:, :], lb[:, :], -1.0, 1.0, ALU.mult, ALU.add)
        vt = P.sb([128, NSC, 128], BF16, st, "vt")
        gt_ = P.sb([128, NSC, 128], BF16, st, "gt_")
        obuf = P.sb([128, NSC, 128], F32, st, "obuf")
        tmpR = P.sb([128, NSC, 128], F32, st, "tmpR")
        stat = P.sb([128, NSC, 1], F32, st, "stat")
        ub = P.sb([128, NSC, 128], BF16, st, "ub")
        uT_sb = P.sb([128, T], BF16, st, "uT_sb")
        ones512 = P.sb([128, 256], F32, st, "ones256")
        P.memset(ones512[:, :], 1.0)
        pp = G.psum
        CH = []
        for z in range(2):
            C = Ctx()
            C.wk = [P.sb([128, 256], F32, st, "wk") for _ in range(10)]
            C.Qe = P.sb([128, 2, 640], BF16, st, "Qe")
            C.Ke = P.sb([128, 2, 640], BF16, st, "Ke")
            C.Qp = P.sb([128, 2, 128], BF16, st, "Qp")
            C.Kp = P.sb([128, 2, 128], BF16, st, "Kp")
            P.memset(C.Qe[:, :, :], 0.0, eng='pool')
            P.memset(C.Ke[:, :, :], 0.0, eng='pool')
            C.cend = P.sb([128, 8], F32, st, "cend")
            C.At = P.sb([128, 128], BF16, st, "At")
            C.KeT = P.sb([128, 512], BF16, st, "KeT")
            C.Sf = P.sb([128, 128], F32, st, "Sf")
            C.Sb = [P.sb([128, 128], BF16, st, "Sb") for _ in range(2)]
            C.tmpS = P.sb([128, 128], F32, st, "tmpS")
            C.banks = (pp[3 * z], pp[3 * z + 1], pp[3 * z + 2])
            CH.append(C)
        STILES = [(256 * i, 256) for i in range(T // 256)]

        def chain(h, z, C):
            e0 = h * 128
            q, fp, f, lf, Gc, Ec, t1, eg, ei, kc = C.wk
            Qe, Ke, Qp, Kp, cend, At, KeT, Sf, Sb, tmpS = C.Qe, C.Ke, C.Qp, C.Kp, C.cend, C.At, C.KeT, C.Sf, C.Sb, C.tmpS
            bD, bA, bS = C.banks
            tiles = list(STILES) if z == 0 else [STILES[0]] + list(STILES[:0:-1])
            P.memset(Sf[:, :], 0.0); yield
            P.memset(Sb[0][:, :], 0.0); yield
            sbi = 0
            for (n0, nt) in tiles:
                nsc = nt // 128
                ncn = nt // 32
                P.dma('sp', q[:, :nt], S['rT'][e0:e0 + 128, n0:n0 + nt], partial=False); yield
                P.dma('sp', fp[:, :nt], S['aT'][z * E + e0:z * E + e0 + 128, n0:n0 + nt], partial=False); yield
                P.act(f[:, :nt], fp[:, :nt], AF.Sigmoid); yield
                P.ts(f[:, :nt], f[:, :nt], oml[:, h:h + 1], lb[:, h:h + 1], ALU.mult, ALU.add); yield
                P.act(lf[:, :nt], f[:, :nt], AF.Ln); yield
                P.ts(kc[:, :nt], f[:, :nt], -1.0, 1.0, ALU.mult, ALU.add); yield
                P.scan(Gc[:, :nt], ones512[:, :nt], lf[:, :nt], 0.0, ALU.mult, ALU.add); yield
                P.tt(Ec[:, :nt], Gc[:, :nt], lf[:, :nt], ALU.subtract); yield
                v3 = lambda b_: b_[:, :nt].re("p (c s) -> p c s", s=32)
                G3, E3, t13 = v3(Gc), v3(Ec), v3(t1)
                if z == 0:
                    P.tt(t13, G3, E3[:, :, 0:1].bc([128, ncn, 32]), ALU.subtract); yield
                else:
                    P.tt(t13, G3[:, :, 31:32].bc([128, ncn, 32]), E3, ALU.subtract); yield
                ce3 = cend[:, :ncn].re("p (c o) -> p c o", o=1)
                P.tt(ce3, G3[:, :, 31:32], E3[:, :, 0:1], ALU.subtract); yield
                P.act(cend[:, :ncn], cend[:, :ncn], AF.Exp); yield
                P.act(eg[:, :nt], t1[:, :nt], AF.Exp); yield
                P.act(ei[:, :nt], t1[:, :nt], AF.Exp, scale=-1.0); yield
                v128 = lambda b_: b_[:, :nt].re("p (a s) -> p a s", s=128)
                P.tt(Qp[:, :nsc, :], v128(q), v128(eg), ALU.mult); yield
                P.tt(Kp[:, :nsc, :], v128(kc), v128(ei), ALU.mult); yield
                v4 = lambda b_: b_[:, :nt].re("p (a j s) -> p a j s", j=4, s=32)
                P.tt(Qe[:, :nsc, :].re("p a (j x) -> p a j x", x=160)[:, :, :, 0:32], v4(q), v4(eg), ALU.mult, partial=True); yield
                P.tt(Ke[:, :nsc, :].re("p a (j x) -> p a j x", x=160)[:, :, :, 0:32], v4(kc), v4(ei), ALU.mult, partial=True); yield
                sc_order = range(nsc) if z == 0 else range(nsc - 1, -1, -1)
                for a_ in sc_order:
                    g = n0 // 128 + a_
                    P.mm(bD[:, 0:128], Kp[:, a_, :], Qp[:, a_, :]); yield
                    for j in range(4):
                        P.mm(bA[:, j * 128:(j + 1) * 128], Ke[:, a_, j * 128:(j + 1) * 128], G.idnb[:, :]); yield
                    P.tt(At[:, :], bD[:, 0:128], G.mask32[:, z, :], ALU.mult); yield
                    P.copy(KeT[:, :], bA[:, :], eng='act'); yield
                    P.mm(bD[:, 128:256], At[:, :], vt[:, g, :], start=True, stop=False); yield
                    jorder = list(range(4)) if z == 0 else [3, 2, 1, 0]
                    for jj, j in enumerate(jorder):
                        P.mm(bD[:, 128:256], Qe[:, a_, j * 128:(j + 1) * 128], Sb[sbi][:, :], start=False, stop=(jj == 3)); yield
                        ps_ = bS[:, 128 * (jj % 2):128 + 128 * (jj % 2)]
                        P.mm(ps_, KeT[:, j * 128:(j + 1) * 128], vt[:, g, :]); yield
                        P.tt(tmpS[:, :], ps_, Sf[:, :], ALU.add); yield
                        cc = cend[:, a_ * 4 + j:a_ * 4 + j + 1]
                        P.ts(Sf[:, :], tmpS[:, :], cc, None, ALU.mult); yield
                        sbi = 1 - sbi
                        P.act(Sb[sbi][:, :], tmpS[:, :], AF.Identity, scale=cc); yield
                    P.tt(obuf[:, g, :], obuf[:, g, :], bD[:, 128:256], ALU.add, partial=True); yield

        for h in range(KHP):
            e0 = h * 128
            for (n0, nt) in TILES:
                P.dma('pool', vt[:, n0 // 128:(n0 + nt) // 128, :], S['vtok'][n0:n0 + nt, e0:e0 + 128].re("(c s) v -> s c v", s=128), partial=(n0 > 0))
                P.dma('pool', gt_[:, n0 // 128:(n0 + nt) // 128, :], S['gtok'][n0:n0 + nt, e0:e0 + 128].re("(c s) v -> s c v", s=128), partial=(n0 > 0))
            P.memset(obuf[:, :, :], 0.0)
            gens = [chain(h, z, CH[z]) for z in range(2)]
            while gens:
                for g_ in list(gens):
                    try:
                        next(g_)
                    except StopIteration:
                        gens.remove(g_)
            P.tt(tmpR[:, :, :], obuf[:, :, :], obuf[:, :, :], ALU.mult)
            P.op('dve', lambda e: e.tensor_reduce(out=stat[:, :, 0:1].ap, in_=tmpR[:, :, :].ap, axis=mybir.AxisListType.X, op=ALU.add),
                 [tmpR[:, :, :]], [stat[:, :, 0:1]])
            P.ts(stat[:, :, :], stat[:, :, :], 1.0 / 128, None, ALU.mult)
            P.rsqrt(stat[:, :, :], stat[:, :, :], EPS)
            P.tt(obuf[:, :, :], obuf[:, :, :], stat[:, :, 0:1].bc([128, NSC, 128]), ALU.mult)
            P.tt(obuf[:, :, :], obuf[:, :, :], W['gnb'][:, :].re("p (o v) -> p o v", o=1).bc([128, NSC, 128]), ALU.mult)
            P.tt(ub[:, :, :], obuf[:, :, :], gt_[:, :, :], ALU.mult)
            for g in range(NSC):
                pt = pp[6 + g % 2]
                P.mm(pt[:, 0:128], ub[:, g, :], G.idnb[:, :])
                P.copy(uT_sb[:, g * 128:(g + 1) * 128], pt[:, 0:128], partial=True)
            P.dma('sp', S['uT'][e0:e0 + 128, :], uT_sb[:, :])
        P.barrier()


TWO_PI = 2.0 * math.pi


def hyena_tables(P, G, L, cosb, sinb):
    nb = L // 128
    N = 2 * L
    with ExitStack() as st:
        arg = P.sb([128, nb, 128], F32, st, "arg")
        m = P.sb([128, nb, 128], F32, st, "m")
        ob = [P.sb([128, nb, 128], BF16, st, "ob") for _ in range(2)]
        fr = P.sb([128, 128], F32, st, "fr")
        ki = P.sb([128, nb, 128], I32, st, "ki")
        for fb in range(nb):
            P.ts(fr[:, :], G.jrow[:, :], float(128 * fb), None, ALU.add)
            P.tt(arg[:, :, :], G.tcol[:, :nb].re("p (c o) -> p c o", o=1).bc([128, nb, 128]),
                 fr[:, :].re("p (o j) -> p o j", o=1).bc([128, nb, 128]), ALU.mult)
            for kind, (off, dst) in enumerate(((N / 4, cosb), (0.0, sinb))):
                P.ts(m[:, :, :], arg[:, :, :], float(off), 1.0 / N, ALU.add, ALU.mult)
                P.copy(ki[:, :, :], m[:, :, :])
                P.tt(m[:, :, :], m[:, :, :], ki[:, :, :], ALU.subtract)
                P.act(ob[kind][:, :, :], m[:, :, :], AF.Sin, scale=TWO_PI)
                P.dma('sp', dst[fb], ob[kind][:, :, :])
        P.barrier()


def hyena_filters(P, G, W, L, zT_d, tnneg, ksum_d, kdiff_d):
    nb = L // 128
    with ExitStack() as st:
        zT = P.sb([33, L], F32, st, "zT")
        P.dma('sp', zT[:, :], zT_d[:, :], partial=False)
        ha = P.sb([64, L], F32, st, "ha")
        hb_ = P.sb([64, L], F32, st, "hb")
        tmp = P.sb([64, 512], F32, st, "ftmp")
        kif = P.sb([64, 512], I32, st, "kif")
        plan = [(zT, 33, W['f_w1'], 0, ha), (ha, 64, W['f_w2'], 1, hb_), (hb_, 64, W['f_w3'], 2, ha)]
        for (src, kd, wm, bi, dst) in plan:
            for t0 in range(0, L, 512):
                nt = min(512, L - t0)
                pt = G.psum[(t0 // 512) % 2]
                P.mm(pt[0:64, :nt], wm[0:kd, :], src[0:kd, t0:t0 + nt])
                P.ts(tmp[:, :nt], pt[0:64, :nt], W['fb'][:, bi:bi + 1], W['sf'][:, 0:1], ALU.add, ALU.mult)
                P.ts(tmp[:, :nt], tmp[:, :nt], 1.0 / TWO_PI, None, ALU.mult)
                P.copy(kif[:, :nt], tmp[:, :nt])
                P.tt(tmp[:, :nt], tmp[:, :nt], kif[:, :nt], ALU.subtract)
                P.act(dst[:, t0:t0 + nt], tmp[:, :nt], AF.Sin, scale=TWO_PI, partial=True)
        h3 = ha
        w4 = P.sb([64, 2 * E], F32, st, "w4")
        P.dma('sp', w4[:, :], W['f_w4'][:, :], partial=False)
        win = P.sb([128, 512], F32, st, "win")
        hf = P.sb([128, 512], F32, st, "hf")
        hk = P.sb([128, 512], F32, st, "hk")
        sk = [P.sb([128, 512], BF16, st, "sk") for _ in range(2)]
        dk = [P.sb([128, 512], BF16, st, "dk") for _ in range(2)]
        i = 0
        for tb in range(nb):
            for ebk in range(4):
                pf = G.psum[0]
                pb = G.psum[1]
                P.mm(pf[:, :], h3[:, tb * 128:(tb + 1) * 128], w4[:, ebk * 512:(ebk + 1) * 512])
                P.mm(pb[:, :], h3[:, tb * 128:(tb + 1) * 128], w4[:, E + ebk * 512:E + (ebk + 1) * 512])
                P.act(win[:, :], G.deltab[:, ebk * 512:(ebk + 1) * 512], AF.Exp, scale=tnneg[:, tb:tb + 1])
                P.tt(hf[:, :], pf[:, :], win[:, :], ALU.mult)
                P.tt(hk[:, :], pb[:, :], win[:, :], ALU.mult)
                P.tt(sk[i % 2][:, :], hf[:, :], hk[:, :], ALU.add)
                P.tt(dk[i % 2][:, :], hf[:, :], hk[:, :], ALU.subtract)
                P.dma('sp', ksum_d[tb * 128:(tb + 1) * 128, ebk * 512:(ebk + 1) * 512], sk[i % 2][:, :])
                P.dma('sp', kdiff_d[tb * 128:(tb + 1) * 128, ebk * 512:(ebk + 1) * 512], dk[i % 2][:, :])
                i += 1
        P.barrier()


def hyena_dft(P, G, S, L, n_off, cosb, sinb, ksum_d, kdiff_d):
    nb = L // 128
    N = 2 * L
    pp = G.psum
    for unit in range(4):
        c0 = unit * 512
        with ExitStack() as st0:
            KY = P.sb([128, nb, 2, 512], BF16, st0, "KY")
            kn = P.sb([2, 512], F32, st0, "kn")
            ynb = P.sb([2, 512], BF16, st0, "ynb")
            with ExitStack() as st:
                ta = P.sb([128, nb, 512], BF16, st, "ta")
                tb_ = P.sb([128, nb, 512], BF16, st, "tb")
                slab = [[P.sb([128, nb, 128], BF16, st, "slab") for _ in range(2)] for _ in range(2)]
                P.dma('sp', ta[:, :, :], ksum_d[:, c0:c0 + 512].re("(c p) e -> p c e", p=128), partial=False)
                P.dma('sp', tb_[:, :, :], kdiff_d[:, c0:c0 + 512].re("(c p) e -> p c e", p=128), partial=False)
                for fb in range(nb):
                    cs, ss = slab[0][fb % 2], slab[1][fb % 2]
                    P.dma('sp', cs[:, :, :], cosb[fb], partial=False)
                    P.dma('sp', ss[:, :, :], sinb[fb], partial=False)
                    for tc in range(nb):
                        P.mm(pp[0][:, :], cs[:, tc, :], ta[:, tc, :], start=(tc == 0), stop=(tc == nb - 1))
                    for tc in range(nb):
                        P.mm(pp[1][:, :], ss[:, tc, :], tb_[:, tc, :], start=(tc == 0), stop=(tc == nb - 1))
                    P.copy(KY[:, fb, 0, :], pp[0][:, :], partial=True)
                    P.copy(KY[:, fb, 1, :], pp[1][:, :], partial=True)
                for tc in range(nb):
                    P.mm(pp[4][0:2, :], G.altb[:, :], ta[:, tc, :], start=(tc == 0), stop=(tc == nb - 1))
                P.copy(kn[:, :], pp[4][0:2, :])
                P.barrier()
            with ExitStack() as st:
                ta = P.sb([128, nb, 512], BF16, st, "ta")
                slab = [[P.sb([128, nb, 128], BF16, st, "slab") for _ in range(2)] for _ in range(2)]
                t = [P.sb([128, 512], F32, st, "t") for _ in range(4)]
                P.dma('sp', ta[:, :, :], S['utok'][n_off:n_off + L, c0:c0 + 512].re("(c p) e -> p c e", p=128), partial=False)
                for fb in range(nb):
                    cs, ss = slab[0][fb % 2], slab[1][fb % 2]
                    P.dma('sp', cs[:, :, :], cosb[fb], partial=False)
                    P.dma('sp', ss[:, :, :], sinb[fb], partial=False)
                    for tc in range(nb):
                        P.mm(pp[2][:, :], cs[:, tc, :], ta[:, tc, :], start=(tc == 0), stop=(tc == nb - 1))
                    for tc in range(nb):
                        P.mm(pp[3][:, :], ss[:, tc, :], ta[:, tc, :], start=(tc == 0), stop=(tc == nb - 1))
                    P.tt(t[0][:, :], pp[2][:, :], KY[:, fb, 0, :], ALU.mult)
                    P.tt(t[1][:, :], pp[3][:, :], KY[:, fb, 1, :], ALU.mult)
                    P.tt(t[2][:, :], pp[2][:, :], KY[:, fb, 1, :], ALU.mult)
                    P.tt(t[3][:, :], pp[3][:, :], KY[:, fb, 0, :], ALU.mult)
                    P.tt(KY[:, fb, 0, :], t[0][:, :], t[1][:, :], ALU.subtract, partial=True)
                    P.tt(KY[:, fb, 1, :], t[2][:, :], t[3][:, :], ALU.add, partial=True)
                    if fb == 0:
                        P.ts(KY[0:1, 0, 0, :], KY[0:1, 0, 0, :], 0.5, None, ALU.mult, partial=True)
                for tc in range(nb):
                    P.mm(pp[4][0:2, :], G.altb[:, :], ta[:, tc, :], start=(tc == 0), stop=(tc == nb - 1))
                P.stt(ynb[:, :], pp[4][0:2, :], 0.5, kn[:, :], ALU.mult, ALU.mult)
                P.barrier()
            with ExitStack() as st:
                slab = [[P.sb([128, nb, 128], BF16, st, "slab") for _ in range(2)] for _ in range(2)]
                yst = [P.sb([128, 4, 128], F32, st, "yst") for _ in range(2)]
                k = 0
                for tbk in range(nb):
                    cs, ss = slab[0][tbk % 2], slab[1][tbk % 2]
                    P.dma('sp', cs[:, :, :], cosb[tbk], partial=False)
                    P.dma('sp', ss[:, :, :], sinb[tbk], partial=False)
                    ys = yst[tbk % 2]
                    for eb in range(4):
                        po = pp[5 + k % 3]
                        k += 1
                        for fc in range(nb):
                            P.mm(po[:, 0:128], KY[:, fc, 0, eb * 128:(eb + 1) * 128], cs[:, fc, :], start=(fc == 0), stop=False)
                            P.mm(po[:, 0:128], KY[:, fc, 1, eb * 128:(eb + 1) * 128], ss[:, fc, :], start=False, stop=False)
                        P.mm(po[:, 0:128], ynb[0:1, eb * 128:(eb + 1) * 128], G.altrow[0:1, :], start=False, stop=True)
                        P.ts(ys[:, eb, :], po[:, 0:128], 2.0 / N, None, ALU.mult, partial=(eb > 0))
                    P.dma('sp', S['aT'][c0:c0 + 512, n_off + tbk * 128:n_off + (tbk + 1) * 128].re("(a p) t -> p a t", p=128), ys[:, :, :])
                P.barrier()


def hyena_layer(P, G, l, W):
    S = G.scr
    hyena_tables(P, G, LX, S['cosx'], S['sinx'])
    hyena_tables(P, G, LC, S['cosc'], S['sinc'])
    if CUT <= 1:
        return
    hyena_filters(P, G, W, LX, W['zTx'], W['tnx'], S['ksx'], S['kdx'])
    hyena_filters(P, G, W, LC, W['zTc'], W['tnc'], S['ksc'], S['kdc'])
    if CUT <= 2:
        return
    with ExitStack() as st:
        hT = P.sb([128, 8, T], BF16, st, "hT")
        norm_phase(P, G, hT)
        wb = [P.sb([128, 8, 512], BF16, st, "wb") for _ in range(2)]
        stg = [P.sb([128, 512], F32, st, "stg") for _ in range(4)]
        cnt = [0, 0]
        silu_post = lambda s_, pt_, r0_, ne_: P.act(s_, pt_, AF.Silu)
        proj_fm(P, G, hT, W['w_in'], 0, E, S['rT'], 0, wb, stg, cnt)
        proj_fm(P, G, hT, W['w_in'], E, E, S['kT'], 0, wb, stg, cnt)
        proj_fm(P, G, hT, W['w_in'], 2 * E, E, S['lwT'], 0, wb, stg, cnt)
        proj_fm(P, G, hT, W['w_in'], 3 * E, E, S['lwT'], E, wb, stg, cnt, post=silu_post)
        P.barrier()
    if CUT <= 3:
        return
    with ExitStack() as st:
        p = P.sb([128, T], F32, st, "p")
        sa = P.sb([128, T], F32, st, "sa")
        sb_ = P.sb([128, T], F32, st, "sb")
        ubf = P.sb([128, T], BF16, st, "ubf")
        stgb = [P.sb([128, 4, 128], BF16, st, "stgb") for _ in range(2)]
        cw, cb = W['cw'], W['cb']

        def conv(dst, src, blk):
            P.dma('sp', p[:, :], src, partial=False)
            P.ts(dst[:, :], p[:, :], cw[:, 1, blk:blk + 1], cb[:, blk:blk + 1], ALU.mult, ALU.add)
            for (a, b) in ((0, LC), (LC, T)):
                P.stt(dst[:, a + 1:b], p[:, a:b - 1], cw[:, 0, blk:blk + 1], dst[:, a + 1:b], ALU.mult, ALU.add, partial=True)
                P.stt(dst[:, a:b - 1], p[:, a + 1:b], cw[:, 2, blk:blk + 1], dst[:, a:b - 1], ALU.mult, ALU.add, partial=True)
        for eb in range(16):
            e0 = eb * 128
            conv(sa, S['kT'][e0:e0 + 128, :], 16 + eb)
            conv(sb_, S['lwT'][e0:e0 + 128, :], 32 + eb)
            P.tt(sa[:, :], sa[:, :], sb_[:, :], ALU.mult)
            P.dma('sp', S['kT'][e0:e0 + 128, :], sa[:, :])
            P.copy(ubf[:, :], sa[:, :], eng='act')
            for gi, g4 in enumerate(range(0, T // 128, 4)):
                n4 = min(4, T // 128 - g4)
                pt = G.psum[gi % 2]
                for j in range(n4):
                    P.mm(pt[:, j * 128:(j + 1) * 128], ubf[:, (g4 + j) * 128:(g4 + j + 1) * 128], G.idnb[:, :])
                sg = stgb[gi % 2]
                P.copy(sg[:, :n4, :], pt[:, :n4 * 128].re("p (a e) -> p a e", e=128))
                P.dma('sp', S['utok'][g4 * 128:(g4 + n4) * 128, e0:e0 + 128].re("(a p) e -> p a e", p=128), sg[:, :n4, :])
            conv(sa, S['rT'][e0:e0 + 128, :], eb)
            P.dma('sp', p[:, :], S['lwT'][E + e0:E + e0 + 128, :], partial=False)
            P.tt(sa[:, :], sa[:, :], p[:, :], ALU.mult)
            P.dma('sp', S['rT'][e0:e0 + 128, :], sa[:, :])
        P.barrier()
    if CUT <= 4:
        return
    hyena_dft(P, G, S, LC, 0, S['cosc'], S['sinc'], S['ksc'], S['kdc'])
    if CUT <= 5:
        return
    hyena_dft(P, G, S, LX, LC, S['cosx'], S['sinx'], S['ksx'], S['kdx'])
    with ExitStack() as st:
        y = P.sb([128, T], F32, st, "y")
        u = P.sb([128, T], F32, st, "u")
        gx = P.sb([128, T], F32, st, "gx")
        ob = [P.sb([128, T], BF16, st, "ob") for _ in range(2)]
        for eb in range(16):
            e0 = eb * 128
            P.dma('sp', y[:, :], S['aT'][e0:e0 + 128, :], partial=False)
            P.dma('sp', u[:, :], S['kT'][e0:e0 + 128, :], partial=False)
            P.dma('sp', gx[:, :], S['rT'][e0:e0 + 128, :], partial=False)
            P.stt(y[:, :], u[:, :], W['fbias'][:, eb:eb + 1], y[:, :], ALU.mult, ALU.add)
            P.tt(ob[eb % 2][:, :], y[:, :], gx[:, :], ALU.mult)
            P.dma('sp', S['uT'][e0:e0 + 128, :], ob[eb % 2][:, :])
        P.barrier()


def build(layers=(0, 1, 2, 3)):
    if isinstance(layers, int):
        layers = tuple(range(layers))
    nc = bass.Bass("TRN2", target_bir_lowering=False)
    P = Prog(nc)
    G = Ctx()
    G.xs_in = P.dram("xT0", [D, T], F32, "ExternalInput")
    G.cond = P.dram("cond", [128, 8, 2], F32, "ExternalInput")
    G.ada_w = P.dram("ada_w", [4, D, 3 * D], F32, "ExternalInput")
    G.w_out = P.dram("w_out", [4, E, D], F32, "ExternalInput")
    G.out = P.dram("outT", [D, LX], F32, "ExternalOutput")
    G.xs = P.dram("xs", [D, T], F32)
    S = {}
    S['rT'] = P.dram("s_rT", [E, T], F32)
    S['kT'] = P.dram("s_kT", [E, T], F32)
    S['vtok'] = P.dram("s_vtok", [T, E], F32)
    if 3 in layers and 0 not in layers:
        S['vfirst'] = P.dram("vfirst_in", [T, E], F32, "ExternalInput")
    else:
        S['vfirst'] = P.dram("s_vfirst", [T, E], F32)
    S['gtok'] = P.dram("s_gtok", [T, E], F32)
    S['loT'] = P.dram("s_loT", [256, T], F32)
    S['lovT'] = P.dram("s_lovT", [32, T], F32)
    S['uT'] = P.dram("s_uT", [E, T], BF16)
    S['aT'] = P.dram("s_aT", [2 * E, T], F32)
    S['lwT'] = P.dram("s_lwT", [2 * E, T], F32)
    G.scr = S
    st = P.stack

    def cin(name, shape, dt=F32):
        d = P.dram(name, shape, dt, "ExternalInput")
        b = P.sb(shape, dt, st, name)
        if len(shape) == 2:
            P.dma('sp', b[:, :], d[:, :], partial=False)
        else:
            P.dma('sp', b[:, :, :], d[:, :, :], partial=False)
        return b
    G.masks = cin("masks", [128, 2, 256])
    G.maskt = cin("maskt", [128, 2, 128])
    idn = cin("idn", [128, 128])
    G.blk = cin("blk", [128, 128])
    G.adab = cin("adab", [128, 96])
    G.npre = cin("npre", [128, 32])
    G.npost = cin("npost", [128, 32])
    G.idnb = P.sb([128, 128], BF16, st, "idnb")
    P.copy(G.idnb[:, :], idn[:, :])
    G.ones = P.sb([128, 128], F32, st, "ones")
    P.memset(G.ones[:, :], 1.0)
    G.psum = [P.ps([128, 512], F32, st, "ps") for _ in range(8)]
    G.A = P.sb([128, 8, 2], F32, st, "A")
    G.B = P.sb([128, 8, 2], F32, st, "B")
    G.Gt = P.sb([128, 8, 2], F32, st, "Gt")
    G.scT = P.sb([128, 8, 2], F32, st, "scT")
    P.dma('sp', G.scT[:, :, :], G.cond[:, :, :], partial=False)
    P.act(G.scT[:, :, :], G.scT[:, :, :], AF.Silu)
    for (n0, nt) in TILES:
        P.dma('sp', G.xs[:, n0:n0 + nt], G.xs_in[:, n0:n0 + nt])
    LW = {}
    for l in (0, 3):
        if l not in layers:
            continue
        W = {}
        pre = "l%d_" % l
        ncols = 4 * E + 256 + (32 if l == 3 else 0)
        W['w_in'] = P.dram(pre + "w_in", [D, ncols], F32, "ExternalInput")
        W['mu'] = cin(pre + "mu", [128, 48])
        W['om'] = P.sb([128, 48], F32, st, pre + "om")
        P.ts(W['om'][:, :], W['mu'][:, :], -1.0, 1.0, ALU.mult, ALU.add)
        W['w0'] = cin(pre + "w0", [128, 32])
        W['a0'] = cin(pre + "a0", [128, 32])
        W['w2'] = P.dram(pre + "w2", [128, E], F32, "ExternalInput")
        W['a2'] = P.dram(pre + "a2", [128, E], F32, "ExternalInput")
        W['k_k'] = cin(pre + "k_k", [128, 16])
        W['k_a'] = cin(pre + "k_a", [128, 16])
        W['r_k'] = cin(pre + "r_k", [128, 16])
        W['lnwb'] = cin(pre + "lnw", [128, 16, 64])
        W['lnbb'] = cin(pre + "lnb", [128, 16, 64])
        if l == 3:
            W['v0'] = P.dram(pre + "v0", [1, E], F32, "ExternalInput")
            W['v2'] = P.dram(pre + "v2", [32, E], F32, "ExternalInput")
        LW[l] = W
    if 1 in layers:
        W = {}
        W['w_in'] = P.dram("l1_w_in", [D, 4 * E], F32, "ExternalInput")
        W['cw'] = cin("l1_cw", [128, 3, 48])
        W['cb'] = cin("l1_cb", [128, 48])
        W['fbias'] = cin("l1_fbias", [128, 16])
        W['f_w1'] = cin("l1_f_w1", [33, 64])
        W['f_w2'] = cin("l1_f_w2", [64, 64])
        W['f_w3'] = cin("l1_f_w3", [64, 64])
        W['f_w4'] = P.dram("l1_f_w4", [64, 2 * E], F32, "ExternalInput")
        W['fb'] = cin("l1_fb", [64, 3])
        W['sf'] = cin("l1_sf", [64, 1])
        W['zTx'] = P.dram("zTx", [33, LX], F32, "ExternalInput")
        W['zTc'] = P.dram("zTc", [33, LC], F32, "ExternalInput")
        W['tnx'] = cin("tnx", [128, LX // 128])
        W['tnc'] = cin("tnc", [128, LC // 128])
        G.deltab = cin("deltab", [128, E])
        G.jrow = cin("jrow", [128, 128])
        G.tcol = cin("tcol", [128, 32])
        altf = cin("altf", [128, 2])
        G.altb = P.sb([128, 2], BF16, st, "altb")
        P.copy(G.altb[:, :], altf[:, :])
        altrf = cin("altrf", [1, 128])
        G.altrow = P.sb([1, 128], BF16, st, "altrow")
        P.copy(G.altrow[:, :], altrf[:, :])
        S['cosx'] = P.dram("s_cosx", [LX // 128, 128, LX // 128, 128], BF16)
        S['sinx'] = P.dram("s_sinx", [LX // 128, 128, LX // 128, 128], BF16)
        S['cosc'] = P.dram("s_cosc", [LC // 128, 128, LC // 128, 128], BF16)
        S['sinc'] = P.dram("s_sinc", [LC // 128, 128, LC // 128, 128], BF16)
        S['ksx'] = P.dram("s_ksx", [LX, E], BF16)
        S['kdx'] = P.dram("s_kdx", [LX, E], BF16)
        S['ksc'] = P.dram("s_ksc", [LC, E], BF16)
        S['kdc'] = P.dram("s_kdc", [LC, E], BF16)
        S['utok'] = P.dram("s_utok", [T, E], BF16)
        LW[1] = W
    if 2 in layers:
        W = {}
        W['w_in'] = P.dram("l2_w_in", [D, 5 * E], F32, "ExternalInput")
        W['lbl'] = P.dram("l2_lbl", [128, 4, 16], F32, "ExternalInput")
        W['gnb'] = cin("l2_gnb", [128, 128])
        G.mask32 = cin("mask32", [128, 2, 128])
        LW[2] = W
    P.barrier()
    for l in layers:
        adaln_phase(P, G, l)
        if l == 2:
            hgrn_layer(P, G, l, LW[l])
        if l == 1:
            hyena_layer(P, G, l, LW[l])
        if l in (0, 3):
            rwkv_layer(P, G, l, LW[l], vres=(l == 3))
            if l == 0:
                for tb in range(T // 512 + 1):
                    a0_, a1_ = tb * 512, min(T, tb * 512 + 512)
                    P.dma('sp', S['vfirst'][a0_:a1_, :], S['vtok'][a0_:a1_, :])
                P.barrier()
        outproj_phase(P, G, l, S['uT'])
    if KDBG:
        dbg = P.dram("dbg_uT", [E, T], BF16, "ExternalOutput")
        for i in range(16):
            P.dma('sp', dbg[i * 128:(i + 1) * 128, :], S['uT'][i * 128:(i + 1) * 128, :])
    for i in range(8):
        P.dma('sp', G.out[:, i * 512:(i + 1) * 512], G.xs[:, LC + i * 512:LC + (i + 1) * 512])
    P.barrier()
    return nc, P


def prep_inputs(inp, b, layers=(0, 1, 2, 3)):
    m = {}
    m['xT0'] = np.ascontiguousarray(np.concatenate([inp['ctx'][b], inp['x'][b]], axis=0).T)
    cond = np.stack([inp['c'][b], inp['c_ctx']], axis=-1)
    m['cond'] = np.ascontiguousarray(cond.reshape(8, 128, 2).transpose(1, 0, 2))
    m['ada_w'] = inp['ada_w']
    m['w_out'] = inp['w_out']
    m['adab'] = col_layout(inp['ada_b'].reshape(-1))
    m['npre'] = col_layout(inp['norm_pre'].reshape(-1))
    m['npost'] = col_layout(inp['norm_post'].reshape(-1))
    c = make_consts()
    m['masks'] = np.ascontiguousarray(c['masks'].transpose(1, 0, 2))
    m['maskt'] = np.ascontiguousarray(c['maskt'].transpose(1, 0, 2))
    m['idn'] = c['idn']
    m['blk'] = c['blk']
    if 1 in layers:
        m['l1_w_in'] = inp['l1_w_in']
        m['l1_cw'] = np.ascontiguousarray(inp['l1_conv_w'].reshape(3, 48, 128).transpose(2, 0, 1))
        m['l1_cb'] = col_layout(inp['l1_conv_b'])
        m['l1_fbias'] = col_layout(inp['l1_filter_bias'])
        m['l1_f_w1'] = inp['l1_f_w1']
        m['l1_f_w2'] = inp['l1_f_w2']
        m['l1_f_w3'] = inp['l1_f_w3']
        m['l1_f_w4'] = inp['l1_f_w4']
        m['l1_fb'] = np.ascontiguousarray(np.stack([inp['l1_f_b1'], inp['l1_f_b2'], inp['l1_f_b3']], axis=1))
        m['l1_sf'] = np.ascontiguousarray(inp['l1_sin_freq'].reshape(64, 1))
        m.update(hyena_consts())
    if 2 in layers:
        m['l2_w_in'] = inp['l2_w_in']
        m['l2_lbl'] = np.ascontiguousarray(inp['hgrn_lb_logits'].reshape(4, 16, 128).transpose(2, 0, 1))
        m['l2_gnb'] = np.ascontiguousarray(np.tile(inp['l2_g_norm'][None, :], (128, 1)))
        s_ = np.arange(128)
        same = (s_[:, None] // 32) == (s_[None, :] // 32)
        m32 = np.zeros((128, 2, 128), np.float32)
        m32[:, 0, :] = same & (s_[:, None] <= s_[None, :])
        m32[:, 1, :] = same & (s_[:, None] >= s_[None, :])
        m['mask32'] = m32
    for l in (0, 3):
        if l not in layers:
            continue
        pre = "l%d_" % l
        m[pre + 'w_in'] = inp[pre + 'w_in']
        m[pre + 'mu'] = col_layout(inp[pre + 'mu'].reshape(-1))
        m[pre + 'w0'] = col_layout(inp[pre + 'w0'].reshape(-1))
        m[pre + 'a0'] = col_layout(inp[pre + 'a0'].reshape(-1))
        m[pre + 'w2'] = np.ascontiguousarray(inp[pre + 'w2'].reshape(128, E))
        m[pre + 'a2'] = np.ascontiguousarray(inp[pre + 'a2'].reshape(128, E))
        m[pre + 'k_k'] = col_layout(inp[pre + 'k_k'])
        m[pre + 'k_a'] = col_layout(inp[pre + 'k_a'])
        m[pre + 'r_k'] = col_layout(inp[pre + 'r_k'].reshape(-1))
        lw = inp[pre + 'ln_w'].reshape(16, 2, 64)
        lb = inp[pre + 'ln_b'].reshape(16, 2, 64)
        m[pre + 'lnw'] = np.ascontiguousarray(np.repeat(lw.transpose(1, 0, 2), 64, axis=0))
        m[pre + 'lnb'] = np.ascontiguousarray(np.repeat(lb.transpose(1, 0, 2), 64, axis=0))
        if l == 3:
            m[pre + 'v0'] = inp[pre + 'v0'].reshape(1, E)
            m[pre + 'v2'] = inp[pre + 'v2']
    return m


_CACHE = {}


def kernel(**inputs):
    inp = {k: np.asarray(v) for k, v in inputs.items()}
    if 'nc' not in _CACHE:
        _CACHE['nc'] = build((0, 1, 2, 3))[0]
    nc = _CACHE['nc']
    in_maps = [prep_inputs(inp, b) for b in range(NC8)]
    res = run_bass_kernel_spmd(nc, in_maps, core_ids=list(range(NC8)))
    out = np.stack([np.ascontiguousarray(res.results[b]["outT"].T) for b in range(NC8)], axis=0)
    return out.astype(np.float32)
```

```python
import math
import os
CUT = int(os.environ.get('KCUT', '99'))
KHP = int(os.environ.get('KHP', '16'))
KDBG = int(os.environ.get('KDBG', '0'))
KOFF = [int(x) for x in os.environ.get('KOFF', '0,19,37,56').split(',')]
KSELF = int(os.environ.get('KSELF', '1'))
from contextlib import ExitStack
import numpy as np
import concourse.bass as bass
import concourse.mybir as mybir
from concourse.bass_utils import run_bass_kernel_spmd

F32 = mybir.dt.float32
BF16 = mybir.dt.bfloat16
I32 = mybir.dt.int32
AF = mybir.ActivationFunctionType
ALU = mybir.AluOpType

D = 1024
E = 2048
LX = 4096
LC = 256
T = LX + LC
NC8 = 8
EPS = 1e-6
TILES = [(0, 256)] + [(256 + 512 * i, 512) for i in range(8)]


class StopBuild(Exception):
    pass


class Buf:
    def __init__(self, t, name):
        self.t = t
        self.name = name
        self.w = {}
        self.r = {}

    def __getitem__(self, idx):
        return V(self, self.t[idx])


class V:
    def __init__(self, buf, ap):
        self.buf = buf
        self.ap = ap

    def __getitem__(self, idx):
        return V(self.buf, self.ap[idx])

    def re(self, pat, **kw):
        return V(self.buf, self.ap.rearrange(pat, **kw))

    def bc(self, shape):
        return V(self.buf, self.ap.to_broadcast(shape))


class Prog:
    NDMA = 40

    def __init__(self, nc):
        self.nc = nc
        self.stack = ExitStack()
        self.eng = {'pe': nc.tensor, 'dve': nc.vector, 'act': nc.scalar, 'pool': nc.gpsimd, 'sp': nc.sync}
        self.sem = {k: self.stack.enter_context(nc.semaphore("s_" + k)) for k in ('pe', 'dve', 'act', 'pool')}
        self.cnt = {k: 0 for k in self.sem}
        self.dsem = [self.stack.enter_context(nc.semaphore("d%d" % i)) for i in range(self.NDMA)]
        self.dval = [0] * self.NDMA
        self.dnext = 0
        self.seen = {e: {} for e in self.eng}
        self.nins = 0
        self.uid = 0

    def sb(self, shape, dt, stack=None, name=None):
        self.uid += 1
        name = (name or "t") + "_%d" % self.uid
        t = (stack or self.stack).enter_context(self.nc.sbuf_tensor(name, list(shape), dt))
        return Buf(t, name)

    def ps(self, shape, dt=F32, stack=None, name=None):
        self.uid += 1
        name = (name or "p") + "_%d" % self.uid
        t = (stack or self.stack).enter_context(self.nc.psum_tensor(name, list(shape), dt))
        return Buf(t, name)

    def dram(self, name, shape, dt, kind=None):
        if kind:
            t = self.nc.dram_tensor(name, list(shape), dt, kind=kind).ap()
        else:
            t = self.nc.dram_tensor(name, list(shape), dt).ap()
        return Buf(t, name)

    def _wait(self, e, key, val):
        if val <= 0 or (e == 'pe' and key == 'pe'):
            return
        if not KSELF and e == key and e in ('dve', 'act'):
            return
        if self.seen[e].get(key, 0) >= val:
            return
        sem = self.sem[key] if isinstance(key, str) else self.dsem[key]
        self.eng[e].wait_ge(sem, val)
        self.nins += 1
        self.seen[e][key] = val

    def _deps(self, e, reads, writes, partial):
        for b in reads:
            for k, v in b.w.items():
                self._wait(e, k, v)
        for b in writes:
            if not partial:
                for k, v in b.w.items():
                    self._wait(e, k, v)
            for k, v in b.r.items():
                self._wait(e, k, v)

    def _mark(self, key, val, reads, writes, partial):
        for b in reads:
            if b.r.get(key, 0) < val:
                b.r[key] = val
        for b in writes:
            if partial:
                b.w[key] = val
            else:
                b.w = {key: val}
                b.r = {}

    limit = None

    def op(self, e, fn, reads, writes, partial=False):
        if self.limit is not None:
            if self.limit <= 0:
                raise StopBuild()
            self.limit -= 1
        reads = [v.buf for v in reads if isinstance(v, V)]
        writes = [v.buf for v in writes]
        self._deps(e, reads, writes, partial)
        ins = fn(self.eng[e])
        self.cnt[e] += 1
        ins.then_inc(self.sem[e], 1)
        self.nins += 1
        self._mark(e, self.cnt[e], reads, writes, partial)

    def dma(self, q, out, in_, partial=True):
        self._deps(q, [in_.buf], [out.buf], partial)
        i = self.dnext
        self.dnext = (i + 1) % self.NDMA
        self._wait(q, i, self.dval[i])
        self.dval[i] += 16
        self.eng[q].dma_start(out=out.ap, in_=in_.ap).then_inc(self.dsem[i], 16)
        self.nins += 1
        self._mark(i, self.dval[i], [in_.buf], [out.buf], partial)

    def barrier(self):
        for e in self.eng:
            for k in self.sem:
                self._wait(e, k, self.cnt[k])
            for i in range(self.NDMA):
                self._wait(e, i, self.dval[i])

    def mm(self, out, lhsT, rhs, start=True, stop=True):
        self.op('pe', lambda e: e.matmul(out.ap, lhsT.ap, rhs.ap, start=start, stop=stop), [lhsT, rhs], [out], partial=True)

    def act(self, out, in_, func=AF.Identity, bias=None, scale=None, partial=False, eng='act'):
        kw = {}
        rd = [in_]
        if bias is not None:
            kw['bias'] = bias.ap if isinstance(bias, V) else bias
            rd.append(bias)
        if scale is not None:
            kw['scale'] = scale.ap if isinstance(scale, V) else scale
            rd.append(scale)
        self.op('act', lambda e: e.activation(out=out.ap, in_=in_.ap, func=func, **kw), rd, [out], partial)

    def tt(self, out, a, b, op, partial=False, eng='dve'):
        self.op(eng, lambda e: e.tensor_tensor(out=out.ap, in0=a.ap, in1=b.ap, op=op), [a, b], [out], partial)

    def ts(self, out, a, s1, s2, op0, op1=None, partial=False, eng='dve'):
        g = lambda s: s.ap if isinstance(s, V) else s
        if op1 is None:
            self.op(eng, lambda e: e.tensor_scalar(out=out.ap, in0=a.ap, scalar1=g(s1), scalar2=None, op0=op0), [a, s1], [out], partial)
        else:
            self.op(eng, lambda e: e.tensor_scalar(out=out.ap, in0=a.ap, scalar1=g(s1), scalar2=g(s2), op0=op0, op1=op1), [a, s1, s2], [out], partial)

    def stt(self, out, a, s, b, op0, op1, partial=False):
        g = s.ap if isinstance(s, V) else s
        self.op('dve', lambda e: e.scalar_tensor_tensor(out=out.ap, in0=a.ap, scalar=g, in1=b.ap, op0=op0, op1=op1), [a, s, b], [out], partial)

    def copy(self, out, in_, partial=False, eng='dve'):
        if eng == 'act':
            self.act(out, in_, AF.Identity, partial=partial)
        else:
            self.op(eng, lambda e: e.tensor_copy(out=out.ap, in_=in_.ap), [in_], [out], partial)

    def memset(self, out, val, eng='dve', partial=False):
        self.op(eng, lambda e: e.memset(out.ap, val), [], [out], partial)

    def rsqrt(self, out, in_, addc, partial=False):
        self.ts(out, in_, addc, None, ALU.add, partial=partial)
        self.act(out, out, AF.Ln, partial=partial)
        self.act(out, out, AF.Exp, scale=-0.5, partial=partial)

    def scan(self, out, d0, d1, init, op0, op1, partial=False):
        g = init.ap if isinstance(init, V) else init
        self.op('dve', lambda e: e.tensor_tensor_scan(out=out.ap, data0=d0.ap, data1=d1.ap, initial=g, op0=op0, op1=op1), [d0, d1, init], [out], partial)


def col_layout(v):
    v = np.asarray(v, np.float32)
    return np.ascontiguousarray(v.reshape(-1, 128).T)


def hyena_consts():
    c = {}
    f32 = np.float32
    for nm, L in (('x', LX), ('c', LC)):
        t = np.linspace(0.0, 1.0, L, dtype=f32)[:, None]
        freqs = np.linspace(1e-4, 15, 16, dtype=f32)[None, :]
        ang = (f32(2.0 * math.pi / L) * np.arange(L, dtype=f32)[:, None]) * freqs
        z = np.concatenate([t, np.cos(ang), -np.sin(ang)], axis=-1).astype(f32)
        c['zT' + nm] = np.ascontiguousarray(z.T)
        c['tn' + nm] = np.ascontiguousarray(-t[:, 0].reshape(L // 128, 128).T)
    deltas = np.abs(np.linspace(math.log(1e-2) / 1.5, math.log(1e-2) / 0.3, E, dtype=f32))
    c['deltab'] = np.ascontiguousarray(np.tile(deltas[None, :], (128, 1)).astype(f32))
    c['jrow'] = np.ascontiguousarray(np.tile(np.arange(128, dtype=f32)[None, :], (128, 1)))
    c['tcol'] = np.ascontiguousarray((np.arange(32, dtype=f32)[None, :] * 128 + np.arange(128, dtype=f32)[:, None]))
    alt = np.where(np.arange(128) % 2 == 0, 1.0, -1.0).astype(f32)
    c['altf'] = np.ascontiguousarray(np.stack([alt, alt], axis=1))
    c['altrf'] = np.ascontiguousarray(alt.reshape(1, 128))
    return c


def make_consts():
    c = {}
    idn = np.eye(128, dtype=np.float32)
    s = np.arange(128)
    blk = (s[:, None] // 64) == (s[None, :] // 64)
    sl = s % 64
    m = np.zeros((2, 128, 256), np.float32)
    m[0, :, :128] = blk & (sl[:, None] < sl[None, :])
    m[0, :, 128:] = blk & (sl[:, None] <= sl[None, :])
    m[1, :, :128] = blk & (sl[:, None] > sl[None, :])
    m[1, :, 128:] = blk & (sl[:, None] >= sl[None, :])
    c['masks'] = m
    mt = np.zeros((2, 128, 128), np.float32)
    mt[0] = m[0, :, :128].T
    mt[1] = m[1, :, :128].T
    c['maskt'] = mt
    c['idn'] = idn
    c['blk'] = blk.astype(np.float32)
    return c


class Ctx:
    pass


def adaln_phase(P, G, l):
    with ExitStack() as st:
        wt = [P.sb([128, 8, 512], F32, st, "adaw") for _ in range(2)]
        mod = P.sb([128, 24, 2], F32, st, "mod")
        for nb in range(6):
            w = wt[nb % 2]
            P.dma('sp', w[:, :, :], G.ada_w[l, :, nb * 512:(nb + 1) * 512].re("(c p) n -> p c n", p=128), partial=False)
            pt = G.psum[nb % 2]
            for j in range(4):
                ob = nb * 4 + j
                for kc in range(8):
                    P.mm(pt[:, j * 2:j * 2 + 2], w[:, kc, j * 128:(j + 1) * 128], G.scT[:, kc, :], start=(kc == 0), stop=(kc == 7))
            P.tt(mod[:, nb * 4:nb * 4 + 4, :], pt[:, 0:8].re("p (a b) -> p a b", b=2),
                 G.adab[:, l * 24 + nb * 4:l * 24 + nb * 4 + 4].re("p (a o) -> p a o", o=1).bc([128, 4, 2]), ALU.add, partial=True)
        P.ts(G.A[:, :, :], mod[:, 8:16, :], 1.0, 32.0, ALU.add, ALU.mult)
        P.tt(G.A[:, :, :], G.A[:, :, :], G.npre[:, l * 8:(l + 1) * 8].re("p (a o) -> p a o", o=1).bc([128, 8, 2]), ALU.mult)
        P.copy(G.B[:, :, :], mod[:, 0:8, :])
        P.ts(G.Gt[:, :, :], mod[:, 16:24, :], 32.0, None, ALU.mult)
        P.tt(G.Gt[:, :, :], G.Gt[:, :, :], G.npost[:, l * 8:(l + 1) * 8].re("p (a o) -> p a o", o=1).bc([128, 8, 2]), ALU.mult)
        P.barrier()


def norm_phase(P, G, hT):
    with ExitStack() as st:
        xt = [P.sb([128, 8, 512], F32, st, "xt") for _ in range(2)]
        sq = P.sb([128, 8, 512], F32, st, "sq")
        rs = P.sb([128, 512], F32, st, "rs")
        tmp = [P.sb([128, 512], F32, st, "tmp") for _ in range(2)]
        for ti, (n0, nt) in enumerate(TILES):
            j = 1 if n0 == 0 else 0
            x = xt[ti % 2]
            P.dma('sp', x[:, :, :nt], G.xs[:, n0:n0 + nt].re("(c p) n -> p c n", p=128), partial=False)
            P.act(sq[:, :, :nt], x[:, :, :nt], AF.Square)
            pt = G.psum[ti % 2]
            for c in range(8):
                P.mm(pt[:, :nt], G.ones[:, :], sq[:, c, :nt], start=(c == 0), stop=(c == 7))
            P.rsqrt(rs[:, :nt], pt[:, :nt], 1024.0 * EPS)
            for c in range(8):
                t = tmp[c % 2]
                P.tt(t[:, :nt], x[:, c, :nt], rs[:, :nt], ALU.mult)
                P.act(hT[:, c, n0:n0 + nt], t[:, :nt], AF.Identity, bias=G.B[:, c, j:j + 1], scale=G.A[:, c, j:j + 1], partial=True)
        P.barrier()


def outproj_phase(P, G, l, uT):
    with ExitStack() as st:
        wo = P.sb([128, 16, 1024], BF16, st, "wo")
        P.dma('pool', wo[:, :, :], G.w_out[l].re("(c p) n -> p c n", p=128), partial=False)
        ut = [P.sb([128, 16, 512], BF16, st, "ut") for _ in range(2)]
        o = P.sb([128, 8, 512], F32, st, "o")
        sq = P.sb([128, 8, 512], F32, st, "sq")
        rs = P.sb([128, 512], F32, st, "rs")
        xt = [P.sb([128, 8, 512], F32, st, "xt") for _ in range(2)]
        for ti, (n0, nt) in enumerate(TILES):
            j = 1 if n0 == 0 else 0
            u = ut[ti % 2]
            x = xt[ti % 2]
            P.dma('sp', u[:, :, :nt], uT[:, n0:n0 + nt].re("(c p) n -> p c n", p=128), partial=False)
            P.dma('sp', x[:, :, :nt], G.xs[:, n0:n0 + nt].re("(c p) n -> p c n", p=128), partial=False)
            for ob in range(8):
                pt = G.psum[ob % 4]
                for ec in range(16):
                    P.mm(pt[:, :nt], wo[:, ec, ob * 128:(ob + 1) * 128], u[:, ec, :nt], start=(ec == 0), stop=(ec == 15))
                P.copy(o[:, ob, :nt], pt[:, :nt], partial=True, eng=('act' if ob % 2 else 'dve'))
            P.act(sq[:, :, :nt], o[:, :, :nt], AF.Square)
            pt = G.psum[4 + ti % 2]
            for c in range(8):
                P.mm(pt[:, :nt], G.ones[:, :], sq[:, c, :nt], start=(c == 0), stop=(c == 7))
            P.rsqrt(rs[:, :nt], pt[:, :nt], 1024.0 * EPS)
            for c in range(8):
                P.tt(o[:, c, :nt], o[:, c, :nt], rs[:, :nt], ALU.mult, partial=True)
                P.stt(x[:, c, :nt], o[:, c, :nt], G.Gt[:, c, j:j + 1], x[:, c, :nt], ALU.mult, ALU.add, partial=True)
            P.dma('sp', G.xs[:, n0:n0 + nt].re("(c p) n -> p c n", p=128), x[:, :, :nt])
        P.barrier()


def load_w_block(P, dst, w_dram, c0, ncols):
    P.dma('pool', dst[:, :, :ncols], w_dram[:, c0:c0 + ncols].re("(c p) n -> p c n", p=128), partial=False)


def proj_fm(P, G, src, w_dram, c0, ncols, dst_dram, r0, wbufs, stg, cnt, post=None):
    for cb in range(0, ncols, 512):
        nb = min(512, ncols - cb)
        w = wbufs[cnt[0] % 2]
        cnt[0] += 1
        load_w_block(P, w, w_dram, c0 + cb, nb)
        for (n0, nt) in TILES:
            for eb in range(0, nb, 128):
                ne = min(128, nb - eb)
                k = cnt[1] % 4
                cnt[1] += 1
                pt = G.psum[k]
                for c in range(8):
                    P.mm(pt[:ne, :nt], w[:, c, eb:eb + ne], src[:, c, n0:n0 + nt], start=(c == 0), stop=(c == 7))
                s = stg[k]
                if post is None:
                    P.copy(s[:ne, :nt], pt[:ne, :nt], eng=('act' if k % 2 else 'dve'))
                else:
                    post(s[:ne, :nt], pt[:ne, :nt], r0 + cb + eb, ne)
                P.dma('sp', dst_dram[r0 + cb + eb:r0 + cb + eb + ne, n0:n0 + nt], s[:ne, :nt])


def proj_tm(P, G, src, w_dram, c0, ncols, dst_dram, wbufs, stg, cnt, func=None):
    for cb in range(0, ncols, 512):
        nb = min(512, ncols - cb)
        w = wbufs[cnt[0] % 2]
        cnt[0] += 1
        load_w_block(P, w, w_dram, c0 + cb, nb)
        for tb in range(T // 128):
            k = cnt[1] % 4
            cnt[1] += 1
            pt = G.psum[k]
            for c in range(8):
                P.mm(pt[:, :nb], src[:, c, tb * 128:(tb + 1) * 128], w[:, c, :nb], start=(c == 0), stop=(c == 7))
            s = stg[k]
            if func is None:
                P.copy(s[:, :nb], pt[:, :nb], eng=('act' if k % 2 else 'dve'))
            else:
                P.act(s[:, :nb], pt[:, :nb], func)
            P.dma('sp', dst_dram[tb * 128:(tb + 1) * 128, cb:cb + nb], s[:, :nb])


def rwkv_layer(P, G, l, W, vres):
    nc = P.nc
    S = G.scr
    with ExitStack() as st:
        hT = P.sb([128, 8, T], BF16, st, "hT")
        norm_phase(P, G, hT)
        hm = P.sb([128, 8, T], BF16, st, "hm")
        wb = [P.sb([128, 8, 512], BF16, st, "wb") for _ in range(2)]
        stg = [P.sb([128, 512], F32, st, "stg") for _ in range(4)]
        cnt = [0, 0]
        mu = W['mu']
        ncol_base = [0, E, 2 * E, 3 * E, 4 * E, 4 * E + 128]
        for g in range(6):
            for c in range(8):
                mcol = mu[:, g * 8 + c:g * 8 + c + 1]
                ocol = W['om'][:, g * 8 + c:g * 8 + c + 1]
                P.act(hm[:, c, :], hT[:, c, :], AF.Identity, scale=ocol, partial=True)
                if c < 4:
                    P.stt(hm[:, c, 1:LC], hT[:, c, 0:LC - 1], mcol, hm[:, c, 1:LC], ALU.mult, ALU.add, partial=True)
                else:
                    P.stt(hm[:, c, 0:LC - 1], hT[:, c, 1:LC], mcol, hm[:, c, 0:LC - 1], ALU.mult, ALU.add, partial=True)
                hx = hT[:, c, LC:].re("p (r w) -> p r w", w=64)
                mx = hm[:, c, LC:].re("p (r w) -> p r w", w=64)
                if c < 2:
                    P.stt(mx[:, :, 1:64], hx[:, :, 0:63], mcol, mx[:, :, 1:64], ALU.mult, ALU.add, partial=True)
                elif c < 4:
                    P.stt(mx[:, :, 0:63], hx[:, :, 1:64], mcol, mx[:, :, 0:63], ALU.mult, ALU.add, partial=True)
                elif c < 6:
                    P.stt(mx[:, 1:64, :], hx[:, 0:63, :], mcol, mx[:, 1:64, :], ALU.mult, ALU.add, partial=True)
                else:
                    P.stt(mx[:, 0:63, :], hx[:, 1:64, :], mcol, mx[:, 0:63, :], ALU.mult, ALU.add, partial=True)
            c0 = ncol_base[g]
            if g == 0:
                proj_fm(P, G, hm, W['w_in'], c0, E, S['rT'], 0, wb, stg, cnt)
            elif g == 1:
                proj_fm(P, G, hm, W['w_in'], c0, E, S['kT'], 0, wb, stg, cnt)
            elif g == 2:
                proj_tm(P, G, hm, W['w_in'], c0, E, S['vtok'], wb, stg, cnt)
                if vres:
                    proj_fm(P, G, hm, W['w_in'], 4 * E + 256, 32, S['lovT'], 0, wb, stg, cnt)
            elif g == 3:
                proj_tm(P, G, hm, W['w_in'], c0, E, S['gtok'], wb, stg, cnt, func=AF.Silu)
            else:
                proj_fm(P, G, hm, W['w_in'], c0, 128, S['loT'], (g - 4) * 128, wb, stg, cnt)
        P.barrier()
    if CUT <= 1:
        return
    if vres:
        with ExitStack() as st:
            lov = P.sb([33, T], F32, st, "lov")
            v2a = P.sb([33, E], F32, st, "v2a")
            P.memset(lov[:, :], 1.0)
            P.dma('sp', lov[0:32, :], S['lovT'][0:32, :], partial=False)
            P.dma('sp', v2a[0:32, :], W['v2'][:, :])
            P.dma('sp', v2a[32:33, :], W['v0'][:, :])
            vt = [P.sb([128, E], F32, st, "vt") for _ in range(2)]
            vf = [P.sb([128, E], F32, st, "vf") for _ in range(2)]
            sg = P.sb([128, E], F32, st, "sg")
            for tb in range(T // 128):
                v = vt[tb % 2]
                f = vf[tb % 2]
                P.dma('sp', v[:, :], S['vtok'][tb * 128:(tb + 1) * 128, :], partial=False)
                P.dma('sp', f[:, :], S['vfirst'][tb * 128:(tb + 1) * 128, :], partial=False)
                for q in range(4):
                    pt = G.psum[q]
                    P.mm(pt[:, :], lov[:, tb * 128:(tb + 1) * 128], v2a[:, q * 512:(q + 1) * 512])
                    P.act(sg[:, q * 512:(q + 1) * 512], pt[:, :], AF.Sigmoid, partial=True)
                P.tt(f[:, :], f[:, :], v[:, :], ALU.subtract)
                P.tt(f[:, :], f[:, :], sg[:, :], ALU.mult)
                P.tt(v[:, :], v[:, :], f[:, :], ALU.add)
                P.dma('sp', S['vtok'][tb * 128:(tb + 1) * 128, :], v[:, :])
            P.barrier()
    with ExitStack() as st:
        lo = P.sb([128, 2, T], F32, st, "lo")
        P.dma('sp', lo[:, 0, :], S['loT'][0:128, :], partial=False)
        P.dma('sp', lo[:, 1, :], S['loT'][128:256, :], partial=True)
        P.act(lo[:, 0, :], lo[:, 0, :], AF.Tanh, partial=True)
        w2 = P.sb([128, E], F32, st, "w2")
        a2 = P.sb([128, E], F32, st, "a2")
        P.dma('sp', w2[:, :], W['w2'][:, :], partial=False)
        P.dma('sp', a2[:, :], W['a2'][:, :], partial=False)
        sa = [P.sb([128, 512], F32, st, "sa") for _ in range(4)]
        i = 0
        for hp in range(16):
            e0 = hp * 128
            for z in range(2):
                zs = slice(z * 64, z * 64 + 64)
                for (n0, nt) in TILES:
                    pt = G.psum[i % 4]
                    s1 = sa[i % 4]
                    P.mm(pt[:, :nt], a2[zs, e0:e0 + 128], lo[zs, 1, n0:n0 + nt])
                    P.act(s1[:, :nt], pt[:, :nt], AF.Sigmoid, bias=W['a0'][:, z * 16 + hp:z * 16 + hp + 1])
                    P.dma('sp', S['aT'][z * E + e0:z * E + e0 + 128, n0:n0 + nt], s1[:, :nt])
                    i += 1
                    pt = G.psum[i % 4]
                    s2 = sa[i % 4]
                    P.mm(pt[:, :nt], w2[zs, e0:e0 + 128], lo[zs, 0, n0:n0 + nt])
                    P.act(s2[:, :nt], pt[:, :nt], AF.Sigmoid, bias=W['w0'][:, z * 16 + hp:z * 16 + hp + 1])
                    P.ts(s2[:, :nt], s2[:, :nt], -math.exp(-0.5), None, ALU.mult)
                    P.dma('sp', S['lwT'][z * E + e0:z * E + e0 + 128, n0:n0 + nt], s2[:, :nt])
                    i += 1
        P.barrier()
    if CUT <= 2:
        return
    with ExitStack() as st:
        NCH = T // 64
        NSLOT = 2 if KHP >= 2 else 1
        HB = []
        for s_ in range(NSLOT):
            H = Ctx()
            H.vst = P.sb([128, NCH, 64], BF16, st, "vst")
            H.ybuf = P.sb([128, NCH, 64], F32, st, "ybuf")
            H.bon = P.sb([128, NCH, 1], F32, st, "bon")
            HB.append(H)
        HC = NCH // 2
        gt_ = P.sb([128, NCH, 64], BF16, st, "gt_")
        tmpR = P.sb([128, HC, 64], F32, st, "tmpR")
        stat = P.sb([128, NCH, 2], F32, st, "stat")
        uob = P.sb([128, HC, 128], BF16, st, "uob")
        P.memset(uob[:, :, :], 0.0, eng='pool')
        uT_sb = P.sb([128, T // 2], BF16, st, "uT_sb")
        onesb = P.sb([128, 2], BF16, st, "onesb")
        P.memset(onesb[:, :], 1.0)
        idst = P.sb([128, 64], BF16, st, "idst")
        P.copy(idst[0:64, :], G.idnb[0:64, 0:64], partial=True)
        P.copy(idst[64:128, :], G.idnb[64:128, 64:128], partial=True)
        ones512 = P.sb([128, 128], F32, st, "ones128")
        P.memset(ones512[:, :], 1.0)
        pp = G.psum
        CH = []
        for ci_ in range(2 * NSLOT):
            C = Ctx()
            C.wk = [P.sb([128, 128], F32, st, "wk") for _ in range(14)]
            C.AR = P.sb([128, 2, 256], BF16, st, "AR")
            C.Kt = P.sb([128, 2, 128], BF16, st, "Kt")
            C.Bt = P.sb([128, 2, 128], BF16, st, "Bt")
            C.Pb = P.sb([128, 2, 128], BF16, st, "Pb")
            for b_ in (C.AR, C.Kt, C.Bt, C.Pb):
                P.memset(b_[:, :, :], 0.0, eng='pool')
            C.cend = P.sb([128, 4], F32, st, "cend")
            C.Mk = P.sb([128, 256], BF16, st, "Mk")
            C.Mb = P.sb([128, 256], BF16, st, "Mb")
            C.PS = [P.sb([128, 256], BF16, st, "PSb") for _ in range(2)]
            C.PT = [P.sb([128, 128], BF16, st, "PTb") for _ in range(2)]
            C.KtT = P.sb([128, 128], BF16, st, "KtT")
            C.BtT = P.sb([128, 128], BF16, st, "BtT")
            C.Wt = P.sb([128, 64], BF16, st, "Wt")
            C.Ut = P.sb([128, 64], BF16, st, "Ut")
            C.Sf = P.sb([128, 64], F32, st, "Sf")
            C.Sb = [P.sb([128, 64], BF16, st, "Sb") for _ in range(2)]
            C.tmpS = P.sb([128, 64], F32, st, "tmpS")
            C.banks = (pp[2 * ci_], pp[2 * ci_ + 1])
            CH.append(C)
        STILES = [(128 * i, 128) for i in range(T // 128)]

        def chain(hp, z, C, H):
            e0 = hp * 128
            vst, ybuf, bon = H.vst, H.ybuf, H.bon
            r, k, a, lw, t1, t2, kk, Gc, Ec, kd, eg, ex, ei, bco = C.wk
            AR, Kt, Bt, Pb, cend = C.AR, C.Kt, C.Bt, C.Pb, C.cend
            Mk, Mb, PS_, PT_, KtT, BtT, Wt, Ut, Sf, Sb, tmpS = C.Mk, C.Mb, C.PS, C.PT, C.KtT, C.BtT, C.Wt, C.Ut, C.Sf, C.Sb, C.tmpS
            bD, b2 = C.banks
            b0 = b1 = bD
            tiles = list(STILES) if z == 0 else [STILES[1], STILES[0]] + list(STILES[:1:-1])
            P.memset(Sf[:, :], 0.0); yield
            P.memset(Sb[0][:, :], 0.0); yield
            sbi = 0
            for (n0, nt) in tiles:
                ncn = nt // 64
                cb = n0 // 64
                P.dma('sp', r[:, :nt], S['rT'][e0:e0 + 128, n0:n0 + nt], partial=False); yield
                P.dma('sp', k[:, :nt], S['kT'][e0:e0 + 128, n0:n0 + nt], partial=False); yield
                P.dma('sp', a[:, :nt], S['aT'][z * E + e0:z * E + e0 + 128, n0:n0 + nt], partial=False); yield
                P.dma('sp', lw[:, :nt], S['lwT'][z * E + e0:z * E + e0 + 128, n0:n0 + nt], partial=False); yield
                P.ts(t1[:, :nt], k[:, :nt], W['k_k'][:, hp:hp + 1], None, ALU.mult); yield
                P.tt(t2[:, :nt], t1[:, :nt], t1[:, :nt], ALU.mult); yield
                P.mm(b1[:, :nt], G.blk[:, :], t2[:, :nt]); yield
                P.ts(kk[:, :nt], b1[:, :nt], 1e-24, None, ALU.add); yield
                P.act(kk[:, :nt], kk[:, :nt], AF.Ln); yield
                P.act(kk[:, :nt], kk[:, :nt], AF.Exp, scale=-0.5); yield
                P.tt(kk[:, :nt], kk[:, :nt], t1[:, :nt], ALU.mult); yield
                P.scan(Gc[:, :nt], ones512[:, :nt], lw[:, :nt], 0.0, ALU.mult, ALU.add); yield
                P.tt(Ec[:, :nt], Gc[:, :nt], lw[:, :nt], ALU.subtract); yield
                v3 = lambda b_: b_[:, :nt].re("p (c s) -> p c s", s=64)
                G3, E3, t13, t23 = v3(Gc), v3(Ec), v3(t1), v3(t2)
                ce3 = cend[:, :ncn].re("p (c o) -> p c o", o=1)
                if z == 0:
                    base = E3[:, :, 0:1].bc([128, ncn, 64])
                    P.tt(t13, G3, base, ALU.subtract); yield
                    P.tt(t23, E3, base, ALU.subtract); yield
                else:
                    base = G3[:, :, 63:64].bc([128, ncn, 64])
                    P.tt(t13, base, E3, ALU.subtract); yield
                    P.tt(t23, base, G3, ALU.subtract); yield
                P.tt(ce3, G3[:, :, 63:64], E3[:, :, 0:1], ALU.subtract); yield
                P.act(cend[:, :ncn], cend[:, :ncn], AF.Exp); yield
                P.ts(kd[:, :nt], a[:, :nt], -1.0, W['k_a'][:, hp:hp + 1], ALU.add, ALU.mult); yield
                P.stt(kd[:, :nt], kd[:, :nt], 1.0, k[:, :nt], ALU.add, ALU.mult); yield
                P.act(eg[:, :nt], t1[:, :nt], AF.Exp); yield
                P.act(ex[:, :nt], t2[:, :nt], AF.Exp); yield
                P.act(ei[:, :nt], t1[:, :nt], AF.Exp, scale=-1.0); yield
                P.tt(bco[:, :nt], kk[:, :nt], a[:, :nt], ALU.mult); yield
                r3, kk3, kd3, eg3, ex3, ei3, bc3 = v3(r), v3(kk), v3(kd), v3(eg), v3(ex), v3(ei), v3(bco)
                for hpar in range(2):
                    ps_ = slice(hpar * 64, hpar * 64 + 64)
                    cs = slice(hpar * 64, hpar * 64 + 64)
                    P.stt(AR[ps_, :ncn, cs], kk3[ps_], -1.0, ex3[ps_], ALU.mult, ALU.mult, partial=True); yield
                    P.tt(AR[ps_, :ncn, 128 + hpar * 64:128 + hpar * 64 + 64], r3[ps_], eg3[ps_], ALU.mult, partial=True); yield
                    P.tt(Kt[ps_, :ncn, cs], kd3[ps_], ei3[ps_], ALU.mult, partial=True); yield
                    P.tt(Bt[ps_, :ncn, cs], bc3[ps_], ei3[ps_], ALU.mult, partial=True); yield
                    P.stt(Pb[ps_, :ncn, cs], r3[ps_], W['r_k'][ps_, hp:hp + 1], kd3[ps_], ALU.mult, ALU.mult, partial=True); yield
                corder = range(ncn) if z == 0 else range(ncn - 1, -1, -1)
                for cl in corder:
                    c = cb + cl
                    ARc, Ktc, Btc = AR[:, cl, :], Kt[:, cl, :], Bt[:, cl, :]
                    P.mm(bD[:, 0:256], Ktc, ARc); yield
                    P.mm(bD[:, 256:512], Btc, ARc); yield
                    P.tt(Mk[:, :], bD[:, 0:256], G.masks[:, z, :], ALU.mult); yield
                    P.tt(Mb[:, :], bD[:, 256:512], G.masks[:, z, :], ALU.mult); yield
                    P.mm(bD[:, 0:128], AR[:, cl, 0:128], Btc); yield
                    P.tt(PT_[0][:, :], bD[:, 0:128], G.maskt[:, z, :], ALU.mult); yield
                    P.mm(bD[:, 128:256], Ktc, G.idnb[:, :]); yield
                    P.mm(bD[:, 256:384], Btc, G.idnb[:, :]); yield
                    P.mm(b2[:, 0:128], PT_[0][:, :], Mb[:, 0:128]); yield
                    P.mm(b2[:, 256:384], Mb[:, 0:128], PT_[0][:, :]); yield
                    P.copy(KtT[:, :], bD[:, 128:256]); yield
                    P.copy(BtT[:, :], bD[:, 256:384]); yield
                    P.copy(PS_[1][:, 0:128], b2[:, 0:128], partial=True, eng='act'); yield
                    P.tt(PS_[1][:, 128:256], Mb[:, 0:128], G.idnb[:, :], ALU.add, partial=True); yield
                    P.copy(PT_[1][:, :], b2[:, 256:384], eng='act'); yield
                    cur = 1
                    for lev in range(1, 6):
                        nxt = 1 - cur
                        if lev < 5:
                            P.mm(b2[:, 0:128], PT_[cur][:, :], PS_[cur][:, 0:128]); yield
                            P.mm(b2[:, 128:256], PT_[cur][:, :], PS_[cur][:, 128:256], start=True, stop=False); yield
                            P.mm(b2[:, 128:256], G.idnb[:, :], PS_[cur][:, 128:256], start=False, stop=True); yield
                            P.mm(b2[:, 256:384], PS_[cur][:, 0:128], PT_[cur][:, :]); yield
                            P.copy(PS_[nxt][:, :], b2[:, 0:256], eng='act'); yield
                            P.copy(PT_[nxt][:, :], b2[:, 256:384], eng='act'); yield
                        else:
                            P.mm(b2[:, 128:256], PT_[cur][:, :], PS_[cur][:, 128:256], start=True, stop=False); yield
                            P.mm(b2[:, 128:256], G.idnb[:, :], PS_[cur][:, 128:256], start=False, stop=True); yield
                            P.copy(PS_[nxt][:, 128:256], b2[:, 128:256], partial=True, eng='act'); yield
                        cur = nxt
                    X = PS_[cur][:, 128:256]
                    S0 = Sb[sbi]
                    P.mm(b2[:, 384:448], AR[:, cl, 0:128], S0[:, :], start=True, stop=False); yield
                    P.mm(b2[:, 384:448], Mk[:, 0:128], vst[:, c, :], start=False, stop=True); yield
                    P.copy(Wt[:, :], b2[:, 384:448], eng='act'); yield
                    P.mm(b2[:, 448:512], X, Wt[:, :]); yield
                    P.copy(Ut[:, :], b2[:, 448:512], eng='act'); yield
                    P.mm(bD[:, 448:512], AR[:, cl, 128:256], S0[:, :], start=True, stop=False); yield
                    P.mm(bD[:, 448:512], Mk[:, 128:256], vst[:, c, :], start=False, stop=False); yield
                    P.mm(bD[:, 448:512], Mb[:, 128:256], Ut[:, :], start=False, stop=True); yield
                    P.mm(bD[:, 0:2], Pb[:, cl, :], onesb[:, :]); yield
                    P.mm(bD[:, 384:448], KtT[:, :], vst[:, c, :], start=True, stop=False); yield
                    P.mm(bD[:, 384:448], BtT[:, :], Ut[:, :], start=False, stop=True); yield
                    P.tt(ybuf[:, c, :], ybuf[:, c, :], bD[:, 448:512], ALU.add, partial=True); yield
                    P.tt(bon[:, c, :], bon[:, c, :], bD[:, 0:1], ALU.add, partial=True); yield
                    P.tt(tmpS[:, :], bD[:, 384:448], Sf[:, :], ALU.add); yield
                    P.ts(Sf[:, :], tmpS[:, :], cend[:, cl:cl + 1], None, ALU.mult); yield
                    sbi = 1 - sbi
                    P.act(Sb[sbi][:, :], tmpS[:, :], AF.Identity, scale=cend[:, cl:cl + 1]); yield

        for hp0 in range(0, KHP, NSLOT):
            hps = list(range(hp0, min(KHP, hp0 + NSLOT)))
            gens = []
            for si, hp in enumerate(hps):
                e0 = hp * 128
                H = HB[si]
                for hpar in range(2):
                    for (n0, nt) in TILES:
                        P.dma('pool', H.vst[hpar * 64:(hpar + 1) * 64, n0 // 64:(n0 + nt) // 64, :],
                              S['vtok'][n0:n0 + nt, e0 + hpar * 64:e0 + hpar * 64 + 64].re("(c s) v -> s c v", s=64), partial=(hpar == 1 or n0 > 0))
                P.memset(H.ybuf[:, :, :], 0.0)
                P.memset(H.bon[:, :, :], 0.0)
                for z in range(2):
                    gens.append(chain(hp, z, CH[2 * si + z], H))
            for gi_, g_ in enumerate(gens):
                for _ in range(KOFF[gi_ % 4]):
                    next(g_)
            while gens:
                for g_ in list(gens):
                    try:
                        next(g_)
                    except StopIteration:
                        gens.remove(g_)
            for si, hp in enumerate(hps):
                e0 = hp * 128
                H = HB[si]
                vst, ybuf, bon = H.vst, H.ybuf, H.bon
                for hpar in range(2):
                    for (n0, nt) in TILES:
                        P.dma('pool', gt_[hpar * 64:(hpar + 1) * 64, n0 // 64:(n0 + nt) // 64, :],
                              S['gtok'][n0:n0 + nt, e0 + hpar * 64:e0 + hpar * 64 + 64].re("(c s) v -> s c v", s=64), partial=(hpar == 1 or n0 > 0))
                P.op('dve', lambda e: e.tensor_reduce(out=stat[:, :, 0:1].ap, in_=ybuf[:, :, :].ap, axis=mybir.AxisListType.X, op=ALU.add),
                     [ybuf[:, :, :]], [stat[:, :, 0:1]], partial=True)
                P.ts(stat[:, :, 0:1], stat[:, :, 0:1], 1.0 / 64, None, ALU.mult, partial=True)
                P.tt(ybuf[:, :, :], ybuf[:, :, :], stat[:, :, 0:1].bc([128, NCH, 64]), ALU.subtract)
                for hf_ in range(2):
                    hs = slice(hf_ * HC, (hf_ + 1) * HC)
                    P.tt(tmpR[:, :, :], ybuf[:, hs, :], ybuf[:, hs, :], ALU.mult)
                    P.op('dve', lambda e, hs=hs: e.tensor_reduce(out=stat[:, hs, 1:2].ap, in_=tmpR[:, :, :].ap, axis=mybir.AxisListType.X, op=ALU.add),
                         [tmpR[:, :, :]], [stat[:, hs, 1:2]], partial=True)
                P.ts(stat[:, :, 1:2], stat[:, :, 1:2], 1.0 / 64, None, ALU.mult, partial=True)
                P.rsqrt(stat[:, :, 1:2], stat[:, :, 1:2], 64e-5, partial=True)
                P.tt(ybuf[:, :, :], ybuf[:, :, :], stat[:, :, 1:2].bc([128, NCH, 64]), ALU.mult)
                P.tt(ybuf[:, :, :], ybuf[:, :, :], W['lnwb'][:, hp, :].re("p (o v) -> p o v", o=1).bc([128, NCH, 64]), ALU.mult)
                P.tt(ybuf[:, :, :], ybuf[:, :, :], W['lnbb'][:, hp, :].re("p (o v) -> p o v", o=1).bc([128, NCH, 64]), ALU.add)
                for hf_ in range(2):
                    hs = slice(hf_ * HC, (hf_ + 1) * HC)
                    P.tt(tmpR[:, :, :], vst[:, hs, :], bon[:, hs, :].bc([128, HC, 64]), ALU.mult)
                    P.tt(ybuf[:, hs, :], ybuf[:, hs, :], tmpR[:, :, :], ALU.add, partial=True)
                    for hpar in range(2):
                        ps_ = slice(hpar * 64, hpar * 64 + 64)
                        P.tt(uob[ps_, :, hpar * 64:hpar * 64 + 64], ybuf[ps_, hs, :], gt_[ps_, hs, :], ALU.mult, partial=True)
                    for c in range(HC):
                        pt = pp[6 + c % 2]
                        P.mm(pt[:, 0:64], uob[:, c, :], idst[:, :])
                        P.copy(uT_sb[:, c * 64:(c + 1) * 64], pt[:, 0:64], partial=True, eng=('act' if c % 2 else 'dve'))
                    P.dma('sp', S['uT'][e0:e0 + 128, hf_ * (T // 2):(hf_ + 1) * (T // 2)], uT_sb[:, :])
        P.barrier()


def hgrn_layer(P, G, l, W):
    S = G.scr
    with ExitStack() as st:
        hT = P.sb([128, 8, T], BF16, st, "hT")
        norm_phase(P, G, hT)
        wb = [P.sb([128, 8, 512], BF16, st, "wb") for _ in range(2)]
        stg = [P.sb([128, 512], F32, st, "stg") for _ in range(4)]
        cnt = [0, 0]
        silu_post = lambda s_, pt_, r0_, ne_: P.act(s_, pt_, AF.Silu)
        proj_fm(P, G, hT, W['w_in'], 0, E, S['rT'], 0, wb, stg, cnt, post=silu_post)
        proj_fm(P, G, hT, W['w_in'], E, 2 * E, S['aT'], 0, wb, stg, cnt)
        proj_tm(P, G, hT, W['w_in'], 3 * E, E, S['vtok'], wb, stg, cnt)
        proj_tm(P, G, hT, W['w_in'], 4 * E, E, S['gtok'], wb, stg, cnt, func=AF.Silu)
        P.barrier()
    if CUT <= 1:
        return
    with ExitStack() as st:
        NSC = T // 128
        lg = P.sb([128, 4, 16], F32, st, "lg")
        P.dma('sp', lg[:, :, :], W['lbl'][:, :, :], partial=False)
        P.act(lg[:, :, :], lg[:, :, :], AF.Exp)
        ssum = P.sb([128, 16], F32, st, "ssum")
        lb = P.sb([128, 16], F32, st, "lb")
        oml = P.sb([128, 16], F32, st, "oml")
        P.tt(ssum[:, :], lg[:, 0, :], lg[:, 1, :], ALU.add)
        P.tt(ssum[:, :], ssum[:, :], lg[:, 2, :], ALU.add)
        P.tt(ssum[:, :], ssum[:, :], lg[:, 3, :], ALU.add)
        lo_, hi_ = 1, l
        P.copy(lb[:, :], lg[:, 1, :])
        for i_ in range(2, l + 1):
            P.tt(lb[:, :], lb[:, :], lg[:, i_, :], ALU.add)
        P.op('dve', lambda e: e.reciprocal(out=ssum[:, :].ap, in_=ssum[:, :].ap), [ssum[:, :]], [ssum[:, :]])
        P.tt(lb[:, :], lb[:, :], ssum[:, :], ALU.mult)
        P.ts(oml[:, :], lb[:, :], -1.0, 1.0, ALU.mult, ALU.add)
        vt = P.sb([128, NSC, 128], BF16, st, "vt")
        gt_ = P.sb([128, NSC, 128], BF16, st, "gt_")
        obuf = P.sb([128, NSC, 128], F32, st, "obuf")
        tmpR = P.sb([128, NSC, 128], F32, st, "tmpR")
        stat = P.sb([128, NSC, 1], F32, st, "stat")
        ub = P.sb([128, NSC, 128], BF16, st, "ub")
        uT_sb = P.sb([128, T], BF16, st, "uT_sb")
        ones512 = P.sb([128, 256], F32, st, "ones256")
        P.memset(ones512[:, :], 1.0)
        pp = G.psum
        CH = []
        for z in range(2):
            C = Ctx()
            C.wk = [P.sb([128, 256], F32, st, "wk") for _ in range(10)]
            C.Qe = P.sb([128, 2, 640], BF16, st, "Qe")
            C.Ke = P.sb([128, 2, 640], BF16, st, "Ke")
            C.Qp = P.sb([128, 2, 128], BF16, st, "Qp")
            C.Kp = P.sb([128, 2, 128], BF16, st, "Kp")
            P.memset(C.Qe[:, :, :], 0.0, eng='pool')
            P.memset(C.Ke[:, :, :], 0.0, eng='pool')
            C.cend = P.sb([128, 8], F32, st, "cend")
            C.At = P.sb([128, 128], BF16, st, "At")
            C.KeT = P.sb([128, 512], BF16, st, "KeT")
            C.Sf = P.sb([128, 128], F32, st, "Sf")
            C.Sb = [P.sb([128, 128], BF16, st, "Sb") for _ in range(2)]
            C.tmpS = P.sb([128, 128], F32, st, "tmpS")
            C.banks = (pp[3 * z], pp[3 * z + 1], pp[3 * z + 2])
            CH.append(C)
        STILES = [(256 * i, 256) for i in range(T // 256)]

        def chain(h, z, C):
            e0 = h * 128
            q, fp, f, lf, Gc, Ec, t1, eg, ei, kc = C.wk
            Qe, Ke, Qp, Kp, cend, At, KeT, Sf, Sb, tmpS = C.Qe, C.Ke, C.Qp, C.Kp, C.cend, C.At, C.KeT, C.Sf, C.Sb, C.tmpS
            bD, bA, bS = C.banks
            tiles = list(STILES) if z == 0 else [STILES[0]] + list(STILES[:0:-1])
            P.memset(Sf[:, :], 0.0); yield
            P.memset(Sb[0][:, :], 0.0); yield
            sbi = 0
            for (n0, nt) in tiles:
                nsc = nt // 128
                ncn = nt // 32
                P.dma('sp', q[:, :nt], S['rT'][e0:e0 + 128, n0:n0 + nt], partial=False); yield
                P.dma('sp', fp[:, :nt], S['aT'][z * E + e0:z * E + e0 + 128, n0:n0 + nt], partial=False); yield
                P.act(f[:, :nt], fp[:, :nt], AF.Sigmoid); yield
                P.ts(f[:, :nt], f[:, :nt], oml[:, h:h + 1], lb[:, h:h + 1], ALU.mult, ALU.add); yield
                P.act(lf[:, :nt], f[:, :nt], AF.Ln); yield
                P.ts(kc[:, :nt], f[:, :nt], -1.0, 1.0, ALU.mult, ALU.add); yield
                P.scan(Gc[:, :nt], ones512[:, :nt], lf[:, :nt], 0.0, ALU.mult, ALU.add); yield
                P.tt(Ec[:, :nt], Gc[:, :nt], lf[:, :nt], ALU.subtract); yield
                v3 = lambda b_: b_[:, :nt].re("p (c s) -> p c s", s=32)
                G3, E3, t13 = v3(Gc), v3(Ec), v3(t1)
                if z == 0:
                    P.tt(t13, G3, E3[:, :, 0:1].bc([128, ncn, 32]), ALU.subtract); yield
                else:
                    P.tt(t13, G3[:, :, 31:32].bc([128, ncn, 32]), E3, ALU.subtract); yield
                ce3 = cend[:, :ncn].re("p (c o) -> p c o", o=1)
                P.tt(ce3, G3[:, :, 31:32], E3[:, :, 0:1], ALU.subtract); yield
                P.act(cend[:, :ncn], cend[:, :ncn], AF.Exp); yield
                P.act(eg[:, :nt], t1[:, :nt], AF.Exp); yield
                P.act(ei[:, :nt], t1[:, :nt], AF.Exp, scale=-1.0); yield
                v128 = lambda b_: b_[:, :nt].re("p (a s) -> p a s", s=128)
                P.tt(Qp[:, :nsc, :], v128(q), v128(eg), ALU.mult); yield
                P.tt(Kp[:, :nsc, :], v128(kc), v128(ei), ALU.mult); yield
                v4 = lambda b_: b_[:, :nt].re("p (a j s) -> p a j s", j=4, s=32)
                P.tt(Qe[:, :nsc, :].re("p a (j x) -> p a j x", x=160)[:, :, :, 0:32], v4(q), v4(eg), ALU.mult, partial=True); yield
                P.tt(Ke[:, :nsc, :].re("p a (j x) -> p a j x", x=160)[:, :, :, 0:32], v4(kc), v4(ei), ALU.mult, partial=True); yield
                sc_order = range(nsc) if z == 0 else range(nsc - 1, -1, -1)
                for a_ in sc_order:
                    g = n0 // 128 + a_
                    P.mm(bD[:, 0:128], Kp[:, a_, :], Qp[:, a_, :]); yield
                    for j in range(4):
                        P.mm(bA[:, j * 128:(j + 1) * 128], Ke[:, a_, j * 128:(j + 1) * 128], G.idnb[:, :]); yield
                    P.tt(At[:, :], bD[:, 0:128], G.mask32[:, z, :], ALU.mult); yield
                    P.copy(KeT[:, :], bA[:, :], eng='act'); yield
                    P.mm(bD[:, 128:256], At[:, :], vt[:, g, :], start=True, stop=False); yield
                    jorder = list(range(4)) if z == 0 else [3, 2, 1, 0]
                    for jj, j in enumerate(jorder):
                        P.mm(bD[:, 128:256], Qe[:, a_, j * 128:(j + 1) * 128], Sb[sbi][:, :], start=False, stop=(jj == 3)); yield
                        ps_ = bS[:, 128 * (jj % 2):128 + 128 * (jj % 2)]
                        P.mm(ps_, KeT[:, j * 128:(j + 1) * 128], vt[:, g, :]); yield
                        P.tt(tmpS[:, :], ps_, Sf[:, :], ALU.add); yield
                        cc = cend[:, a_ * 4 + j:a_ * 4 + j + 1]
                        P.ts(Sf[:, :], tmpS[:, :], cc, None, ALU.mult); yield
                        sbi = 1 - sbi
                        P.act(Sb[sbi][:, :], tmpS[:, :], AF.Identity, scale=cc); yield
                    P.tt(obuf[:, g, :], obuf[:, g, :], bD[:, 128:256], ALU.add, partial=True); yield

        for h in range(KHP):
            e0 = h * 128
            for (n0, nt) in TILES:
                P.dma('pool', vt[:, n0 // 128:(n0 + nt) // 128, :], S['vtok'][n0:n0 + nt, e0:e0 + 128].re("(c s) v -> s c v", s=128), partial=(n0 > 0))
                P.dma('pool', gt_[:, n0 // 128:(n0 + nt) // 128, :], S['gtok'][n0:n0 + nt, e0:e0 + 128].re("(c s) v -> s c v", s=128), partial=(n0 > 0))
            P.memset(obuf[:, :, :], 0.0)
            gens = [chain(h, z, CH[z]) for z in range(2)]
            for _ in range(14):
                next(gens[1])
            while gens:
                for g_ in list(gens):
                    try:
                        next(g_)
                    except StopIteration:
                        gens.remove(g_)
            P.tt(tmpR[:, :, :], obuf[:, :, :], obuf[:, :, :], ALU.mult)
            P.op('dve', lambda e: e.tensor_reduce(out=stat[:, :, 0:1].ap, in_=tmpR[:, :, :].ap, axis=mybir.AxisListType.X, op=ALU.add),
                 [tmpR[:, :, :]], [stat[:, :, 0:1]])
            P.ts(stat[:, :, :], stat[:, :, :], 1.0 / 128, None, ALU.mult)
            P.rsqrt(stat[:, :, :], stat[:, :, :], EPS)
            P.tt(obuf[:, :, :], obuf[:, :, :], stat[:, :, 0:1].bc([128, NSC, 128]), ALU.mult)
            P.tt(obuf[:, :, :], obuf[:, :, :], W['gnb'][:, :].re("p (o v) -> p o v", o=1).bc([128, NSC, 128]), ALU.mult)
            P.tt(ub[:, :, :], obuf[:, :, :], gt_[:, :, :], ALU.mult)
            for g in range(NSC):
                pt = pp[6 + g % 2]
                P.mm(pt[:, 0:128], ub[:, g, :], G.idnb[:, :])
                P.copy(uT_sb[:, g * 128:(g + 1) * 128], pt[:, 0:128], partial=True)
            P.dma('sp', S['uT'][e0:e0 + 128, :], uT_sb[:, :])
        P.barrier()


TWO_PI = 2.0 * math.pi


def hyena_tables(P, G, L, cosb, sinb):
    nb = L // 128
    N = 2 * L
    with ExitStack() as st:
        arg = P.sb([128, nb, 128], F32, st, "arg")
        m = P.sb([128, nb, 128], F32, st, "m")
        ob = [P.sb([128, nb, 128], BF16, st, "ob") for _ in range(2)]
        fr = P.sb([128, 128], F32, st, "fr")
        ki = P.sb([128, nb, 128], I32, st, "ki")
        for fb in range(nb):
            P.ts(fr[:, :], G.jrow[:, :], float(128 * fb), None, ALU.add)
            P.tt(arg[:, :, :], G.tcol[:, :nb].re("p (c o) -> p c o", o=1).bc([128, nb, 128]),
                 fr[:, :].re("p (o j) -> p o j", o=1).bc([128, nb, 128]), ALU.mult)
            for kind, (off, dst) in enumerate(((N / 4, cosb), (0.0, sinb))):
                P.ts(m[:, :, :], arg[:, :, :], float(off), 1.0 / N, ALU.add, ALU.mult)
                P.copy(ki[:, :, :], m[:, :, :])
                P.tt(m[:, :, :], m[:, :, :], ki[:, :, :], ALU.subtract)
                P.act(ob[kind][:, :, :], m[:, :, :], AF.Sin, scale=TWO_PI)
                P.dma('sp', dst[fb], ob[kind][:, :, :])
        P.barrier()


def hyena_filters(P, G, W, L, zT_d, tnneg, ksum_d, kdiff_d):
    nb = L // 128
    with ExitStack() as st:
        zT = P.sb([33, L], F32, st, "zT")
        P.dma('sp', zT[:, :], zT_d[:, :], partial=False)
        ha = P.sb([64, L], F32, st, "ha")
        hb_ = P.sb([64, L], F32, st, "hb")
        tmp = P.sb([64, 512], F32, st, "ftmp")
        kif = P.sb([64, 512], I32, st, "kif")
        plan = [(zT, 33, W['f_w1'], 0, ha), (ha, 64, W['f_w2'], 1, hb_), (hb_, 64, W['f_w3'], 2, ha)]
        for (src, kd, wm, bi, dst) in plan:
            for t0 in range(0, L, 512):
                nt = min(512, L - t0)
                pt = G.psum[(t0 // 512) % 2]
                P.mm(pt[0:64, :nt], wm[0:kd, :], src[0:kd, t0:t0 + nt])
                P.ts(tmp[:, :nt], pt[0:64, :nt], W['fb'][:, bi:bi + 1], W['sf'][:, 0:1], ALU.add, ALU.mult)
                P.ts(tmp[:, :nt], tmp[:, :nt], 1.0 / TWO_PI, None, ALU.mult)
                P.copy(kif[:, :nt], tmp[:, :nt])
                P.tt(tmp[:, :nt], tmp[:, :nt], kif[:, :nt], ALU.subtract)
                P.act(dst[:, t0:t0 + nt], tmp[:, :nt], AF.Sin, scale=TWO_PI, partial=True)
        h3 = ha
        w4 = P.sb([64, 2 * E], F32, st, "w4")
        P.dma('sp', w4[:, :], W['f_w4'][:, :], partial=False)
        win = P.sb([128, 512], F32, st, "win")
        hf = P.sb([128, 512], F32, st, "hf")
        hk = P.sb([128, 512], F32, st, "hk")
        sk = [P.sb([128, 512], BF16, st, "sk") for _ in range(2)]
        dk = [P.sb([128, 512], BF16, st, "dk") for _ in range(2)]
        i = 0
        for tb in range(nb):
            for ebk in range(4):
                pf = G.psum[0]
                pb = G.psum[1]
                P.mm(pf[:, :], h3[:, tb * 128:(tb + 1) * 128], w4[:, ebk * 512:(ebk + 1) * 512])
                P.mm(pb[:, :], h3[:, tb * 128:(tb + 1) * 128], w4[:, E + ebk * 512:E + (ebk + 1) * 512])
                P.act(win[:, :], G.deltab[:, ebk * 512:(ebk + 1) * 512], AF.Exp, scale=tnneg[:, tb:tb + 1])
                P.tt(hf[:, :], pf[:, :], win[:, :], ALU.mult)
                P.tt(hk[:, :], pb[:, :], win[:, :], ALU.mult)
                P.tt(sk[i % 2][:, :], hf[:, :], hk[:, :], ALU.add)
                P.tt(dk[i % 2][:, :], hf[:, :], hk[:, :], ALU.subtract)
                P.dma('sp', ksum_d[tb * 128:(tb + 1) * 128, ebk * 512:(ebk + 1) * 512], sk[i % 2][:, :])
                P.dma('sp', kdiff_d[tb * 128:(tb + 1) * 128, ebk * 512:(ebk + 1) * 512], dk[i % 2][:, :])
                i += 1
        P.barrier()


def hyena_dft(P, G, S, L, n_off, cosb, sinb, ksum_d, kdiff_d):
    nb = L // 128
    N = 2 * L
    pp = G.psum
    for unit in range(4):
        c0 = unit * 512
        with ExitStack() as st0:
            KY = P.sb([128, nb, 2, 512], BF16, st0, "KY")
            kn = P.sb([2, 512], F32, st0, "kn")
            ynb = P.sb([2, 512], BF16, st0, "ynb")
            with ExitStack() as st:
                ta = P.sb([128, nb, 512], BF16, st, "ta")
                tb_ = P.sb([128, nb, 512], BF16, st, "tb")
                slab = [[P.sb([128, nb, 128], BF16, st, "slab") for _ in range(2)] for _ in range(2)]
                P.dma('sp', ta[:, :, :], ksum_d[:, c0:c0 + 512].re("(c p) e -> p c e", p=128), partial=False)
                P.dma('sp', tb_[:, :, :], kdiff_d[:, c0:c0 + 512].re("(c p) e -> p c e", p=128), partial=False)
                for fb in range(nb):
                    cs, ss = slab[0][fb % 2], slab[1][fb % 2]
                    P.dma('sp', cs[:, :, :], cosb[fb], partial=False)
                    P.dma('sp', ss[:, :, :], sinb[fb], partial=False)
                    for tc in range(nb):
                        P.mm(pp[0][:, :], cs[:, tc, :], ta[:, tc, :], start=(tc == 0), stop=(tc == nb - 1))
                    for tc in range(nb):
                        P.mm(pp[1][:, :], ss[:, tc, :], tb_[:, tc, :], start=(tc == 0), stop=(tc == nb - 1))
                    P.copy(KY[:, fb, 0, :], pp[0][:, :], partial=True)
                    P.copy(KY[:, fb, 1, :], pp[1][:, :], partial=True)
                for tc in range(nb):
                    P.mm(pp[4][0:2, :], G.altb[:, :], ta[:, tc, :], start=(tc == 0), stop=(tc == nb - 1))
                P.copy(kn[:, :], pp[4][0:2, :])
                P.barrier()
            with ExitStack() as st:
                ta = P.sb([128, nb, 512], BF16, st, "ta")
                slab = [[P.sb([128, nb, 128], BF16, st, "slab") for _ in range(2)] for _ in range(2)]
                t = [P.sb([128, 512], F32, st, "t") for _ in range(4)]
                P.dma('sp', ta[:, :, :], S['utok'][n_off:n_off + L, c0:c0 + 512].re("(c p) e -> p c e", p=128), partial=False)
                for fb in range(nb):
                    cs, ss = slab[0][fb % 2], slab[1][fb % 2]
                    P.dma('sp', cs[:, :, :], cosb[fb], partial=False)
                    P.dma('sp', ss[:, :, :], sinb[fb], partial=False)
                    for tc in range(nb):
                        P.mm(pp[2][:, :], cs[:, tc, :], ta[:, tc, :], start=(tc == 0), stop=(tc == nb - 1))
                    for tc in range(nb):
                        P.mm(pp[3][:, :], ss[:, tc, :], ta[:, tc, :], start=(tc == 0), stop=(tc == nb - 1))
                    P.tt(t[0][:, :], pp[2][:, :], KY[:, fb, 0, :], ALU.mult)
                    P.tt(t[1][:, :], pp[3][:, :], KY[:, fb, 1, :], ALU.mult)
                    P.tt(t[2][:, :], pp[2][:, :], KY[:, fb, 1, :], ALU.mult)
                    P.tt(t[3][:, :], pp[3][:, :], KY[:, fb, 0, :], ALU.mult)
                    P.tt(KY[:, fb, 0, :], t[0][:, :], t[1][:, :], ALU.subtract, partial=True)
                    P.tt(KY[:, fb, 1, :], t[2][:, :], t[3][:, :], ALU.add, partial=True)
                    if fb == 0:
                        P.ts(KY[0:1, 0, 0, :], KY[0:1, 0, 0, :], 0.5, None, ALU.mult, partial=True)
                for tc in range(nb):
                    P.mm(pp[4][0:2, :], G.altb[:, :], ta[:, tc, :], start=(tc == 0), stop=(tc == nb - 1))
                P.stt(ynb[:, :], pp[4][0:2, :], 0.5, kn[:, :], ALU.mult, ALU.mult)
                P.barrier()
            with ExitStack() as st:
                slab = [[P.sb([128, nb, 128], BF16, st, "slab") for _ in range(2)] for _ in range(2)]
                yst = [P.sb([128, 4, 128], F32, st, "yst") for _ in range(2)]
                k = 0
                for tbk in range(nb):
                    cs, ss = slab[0][tbk % 2], slab[1][tbk % 2]
                    P.dma('sp', cs[:, :, :], cosb[tbk], partial=False)
                    P.dma('sp', ss[:, :, :], sinb[tbk], partial=False)
                    ys = yst[tbk % 2]
                    for eb in range(4):
                        po = pp[5 + k % 3]
                        k += 1
                        for fc in range(nb):
                            P.mm(po[:, 0:128], KY[:, fc, 0, eb * 128:(eb + 1) * 128], cs[:, fc, :], start=(fc == 0), stop=False)
                            P.mm(po[:, 0:128], KY[:, fc, 1, eb * 128:(eb + 1) * 128], ss[:, fc, :], start=False, stop=False)
                        P.mm(po[:, 0:128], ynb[0:1, eb * 128:(eb + 1) * 128], G.altrow[0:1, :], start=False, stop=True)
                        P.ts(ys[:, eb, :], po[:, 0:128], 2.0 / N, None, ALU.mult, partial=(eb > 0))
                    P.dma('sp', S['aT'][c0:c0 + 512, n_off + tbk * 128:n_off + (tbk + 1) * 128].re("(a p) t -> p a t", p=128), ys[:, :, :])
                P.barrier()


def hyena_layer(P, G, l, W):
    S = G.scr
    hyena_tables(P, G, LX, S['cosx'], S['sinx'])
    hyena_tables(P, G, LC, S['cosc'], S['sinc'])
    if CUT <= 1:
        return
    hyena_filters(P, G, W, LX, W['zTx'], W['tnx'], S['ksx'], S['kdx'])
    hyena_filters(P, G, W, LC, W['zTc'], W['tnc'], S['ksc'], S['kdc'])
    if CUT <= 2:
        return
    with ExitStack() as st:
        hT = P.sb([128, 8, T], BF16, st, "hT")
        norm_phase(P, G, hT)
        wb = [P.sb([128, 8, 512], BF16, st, "wb") for _ in range(2)]
        stg = [P.sb([128, 512], F32, st, "stg") for _ in range(4)]
        cnt = [0, 0]
        silu_post = lambda s_, pt_, r0_, ne_: P.act(s_, pt_, AF.Silu)
        proj_fm(P, G, hT, W['w_in'], 0, E, S['rT'], 0, wb, stg, cnt)
        proj_fm(P, G, hT, W['w_in'], E, E, S['kT'], 0, wb, stg, cnt)
        proj_fm(P, G, hT, W['w_in'], 2 * E, E, S['lwT'], 0, wb, stg, cnt)
        proj_fm(P, G, hT, W['w_in'], 3 * E, E, S['lwT'], E, wb, stg, cnt, post=silu_post)
        P.barrier()
    if CUT <= 3:
        return
    with ExitStack() as st:
        p = P.sb([128, T], F32, st, "p")
        sa = P.sb([128, T], F32, st, "sa")
        sb_ = P.sb([128, T], F32, st, "sb")
        ubf = P.sb([128, T], BF16, st, "ubf")
        stgb = [P.sb([128, 4, 128], BF16, st, "stgb") for _ in range(2)]
        cw, cb = W['cw'], W['cb']

        def conv(dst, src, blk):
            P.dma('sp', p[:, :], src, partial=False)
            P.ts(dst[:, :], p[:, :], cw[:, 1, blk:blk + 1], cb[:, blk:blk + 1], ALU.mult, ALU.add)
            for (a, b) in ((0, LC), (LC, T)):
                P.stt(dst[:, a + 1:b], p[:, a:b - 1], cw[:, 0, blk:blk + 1], dst[:, a + 1:b], ALU.mult, ALU.add, partial=True)
                P.stt(dst[:, a:b - 1], p[:, a + 1:b], cw[:, 2, blk:blk + 1], dst[:, a:b - 1], ALU.mult, ALU.add, partial=True)
        for eb in range(16):
            e0 = eb * 128
            conv(sa, S['kT'][e0:e0 + 128, :], 16 + eb)
            conv(sb_, S['lwT'][e0:e0 + 128, :], 32 + eb)
            P.tt(sa[:, :], sa[:, :], sb_[:, :], ALU.mult)
            P.dma('sp', S['kT'][e0:e0 + 128, :], sa[:, :])
            P.copy(ubf[:, :], sa[:, :], eng='act')
            for gi, g4 in enumerate(range(0, T // 128, 4)):
                n4 = min(4, T // 128 - g4)
                pt = G.psum[gi % 2]
                for j in range(n4):
                    P.mm(pt[:, j * 128:(j + 1) * 128], ubf[:, (g4 + j) * 128:(g4 + j + 1) * 128], G.idnb[:, :])
                sg = stgb[gi % 2]
                P.copy(sg[:, :n4, :], pt[:, :n4 * 128].re("p (a e) -> p a e", e=128))
                P.dma('sp', S['utok'][g4 * 128:(g4 + n4) * 128, e0:e0 + 128].re("(a p) e -> p a e", p=128), sg[:, :n4, :])
            conv(sa, S['rT'][e0:e0 + 128, :], eb)
            P.dma('sp', p[:, :], S['lwT'][E + e0:E + e0 + 128, :], partial=False)
            P.tt(sa[:, :], sa[:, :], p[:, :], ALU.mult)
            P.dma('sp', S['rT'][e0:e0 + 128, :], sa[:, :])
        P.barrier()
    if CUT <= 4:
        return
    hyena_dft(P, G, S, LC, 0, S['cosc'], S['sinc'], S['ksc'], S['kdc'])
    if CUT <= 5:
        return
    hyena_dft(P, G, S, LX, LC, S['cosx'], S['sinx'], S['ksx'], S['kdx'])
    with ExitStack() as st:
        y = P.sb([128, T], F32, st, "y")
        u = P.sb([128, T], F32, st, "u")
        gx = P.sb([128, T], F32, st, "gx")
        ob = [P.sb([128, T], BF16, st, "ob") for _ in range(2)]
        for eb in range(16):
            e0 = eb * 128
            P.dma('sp', y[:, :], S['aT'][e0:e0 + 128, :], partial=False)
            P.dma('sp', u[:, :], S['kT'][e0:e0 + 128, :], partial=False)
            P.dma('sp', gx[:, :], S['rT'][e0:e0 + 128, :], partial=False)
            P.stt(y[:, :], u[:, :], W['fbias'][:, eb:eb + 1], y[:, :], ALU.mult, ALU.add)
            P.tt(ob[eb % 2][:, :], y[:, :], gx[:, :], ALU.mult)
            P.dma('sp', S['uT'][e0:e0 + 128, :], ob[eb % 2][:, :])
        P.barrier()


def build(layers=(0, 1, 2, 3)):
    if isinstance(layers, int):
        layers = tuple(range(layers))
    nc = bass.Bass("TRN2", target_bir_lowering=False)
    P = Prog(nc)
    G = Ctx()
    G.xs_in = P.dram("xT0", [D, T], F32, "ExternalInput")
    G.cond = P.dram("cond", [128, 8, 2], F32, "ExternalInput")
    G.ada_w = P.dram("ada_w", [4, D, 3 * D], F32, "ExternalInput")
    G.w_out = P.dram("w_out", [4, E, D], F32, "ExternalInput")
    G.out = P.dram("outT", [D, LX], F32, "ExternalOutput")
    G.xs = P.dram("xs", [D, T], F32)
    S = {}
    S['rT'] = P.dram("s_rT", [E, T], F32)
    S['kT'] = P.dram("s_kT", [E, T], F32)
    S['vtok'] = P.dram("s_vtok", [T, E], F32)
    if 3 in layers and 0 not in layers:
        S['vfirst'] = P.dram("vfirst_in", [T, E], F32, "ExternalInput")
    else:
        S['vfirst'] = P.dram("s_vfirst", [T, E], F32)
    S['gtok'] = P.dram("s_gtok", [T, E], F32)
    S['loT'] = P.dram("s_loT", [256, T], F32)
    S['lovT'] = P.dram("s_lovT", [32, T], F32)
    S['uT'] = P.dram("s_uT", [E, T], BF16)
    S['aT'] = P.dram("s_aT", [2 * E, T], F32)
    S['lwT'] = P.dram("s_lwT", [2 * E, T], F32)
    G.scr = S
    st = P.stack

    def cin(name, shape, dt=F32):
        d = P.dram(name, shape, dt, "ExternalInput")
        b = P.sb(shape, dt, st, name)
        if len(shape) == 2:
            P.dma('sp', b[:, :], d[:, :], partial=False)
        else:
            P.dma('sp', b[:, :, :], d[:, :, :], partial=False)
        return b
    G.masks = cin("masks", [128, 2, 256])
    G.maskt = cin("maskt", [128, 2, 128])
    idn = cin("idn", [128, 128])
    G.blk = cin("blk", [128, 128])
    G.adab = cin("adab", [128, 96])
    G.npre = cin("npre", [128, 32])
    G.npost = cin("npost", [128, 32])
    G.idnb = P.sb([128, 128], BF16, st, "idnb")
    P.copy(G.idnb[:, :], idn[:, :])
    G.ones = P.sb([128, 128], F32, st, "ones")
    P.memset(G.ones[:, :], 1.0)
    G.psum = [P.ps([128, 512], F32, st, "ps") for _ in range(8)]
    G.A = P.sb([128, 8, 2], F32, st, "A")
    G.B = P.sb([128, 8, 2], F32, st, "B")
    G.Gt = P.sb([128, 8, 2], F32, st, "Gt")
    G.scT = P.sb([128, 8, 2], F32, st, "scT")
    P.dma('sp', G.scT[:, :, :], G.cond[:, :, :], partial=False)
    P.act(G.scT[:, :, :], G.scT[:, :, :], AF.Silu)
    for (n0, nt) in TILES:
        P.dma('sp', G.xs[:, n0:n0 + nt], G.xs_in[:, n0:n0 + nt])
    LW = {}
    for l in (0, 3):
        if l not in layers:
            continue
        W = {}
        pre = "l%d_" % l
        ncols = 4 * E + 256 + (32 if l == 3 else 0)
        W['w_in'] = P.dram(pre + "w_in", [D, ncols], F32, "ExternalInput")
        W['mu'] = cin(pre + "mu", [128, 48])
        W['om'] = P.sb([128, 48], F32, st, pre + "om")
        P.ts(W['om'][:, :], W['mu'][:, :], -1.0, 1.0, ALU.mult, ALU.add)
        W['w0'] = cin(pre + "w0", [128, 32])
        W['a0'] = cin(pre + "a0", [128, 32])
        W['w2'] = P.dram(pre + "w2", [128, E], F32, "ExternalInput")
        W['a2'] = P.dram(pre + "a2", [128, E], F32, "ExternalInput")
        W['k_k'] = cin(pre + "k_k", [128, 16])
        W['k_a'] = cin(pre + "k_a", [128, 16])
        W['r_k'] = cin(pre + "r_k", [128, 16])
        W['lnwb'] = cin(pre + "lnw", [128, 16, 64])
        W['lnbb'] = cin(pre + "lnb", [128, 16, 64])
        if l == 3:
            W['v0'] = P.dram(pre + "v0", [1, E], F32, "ExternalInput")
            W['v2'] = P.dram(pre + "v2", [32, E], F32, "ExternalInput")
        LW[l] = W
    if 1 in layers:
        W = {}
        W['w_in'] = P.dram("l1_w_in", [D, 4 * E], F32, "ExternalInput")
        W['cw'] = cin("l1_cw", [128, 3, 48])
        W['cb'] = cin("l1_cb", [128, 48])
        W['fbias'] = cin("l1_fbias", [128, 16])
        W['f_w1'] = cin("l1_f_w1", [33, 64])
        W['f_w2'] = cin("l1_f_w2", [64, 64])
        W['f_w3'] = cin("l1_f_w3", [64, 64])
        W['f_w4'] = P.dram("l1_f_w4", [64, 2 * E], F32, "ExternalInput")
        W['fb'] = cin("l1_fb", [64, 3])
        W['sf'] = cin("l1_sf", [64, 1])
        W['zTx'] = P.dram("zTx", [33, LX], F32, "ExternalInput")
        W['zTc'] = P.dram("zTc", [33, LC], F32, "ExternalInput")
        W['tnx'] = cin("tnx", [128, LX // 128])
        W['tnc'] = cin("tnc", [128, LC // 128])
        G.deltab = cin("deltab", [128, E])
        G.jrow = cin("jrow", [128, 128])
        G.tcol = cin("tcol", [128, 32])
        altf = cin("altf", [128, 2])
        G.altb = P.sb([128, 2], BF16, st, "altb")
        P.copy(G.altb[:, :], altf[:, :])
        altrf = cin("altrf", [1, 128])
        G.altrow = P.sb([1, 128], BF16, st, "altrow")
        P.copy(G.altrow[:, :], altrf[:, :])
        S['cosx'] = P.dram("s_cosx", [LX // 128, 128, LX // 128, 128], BF16)
        S['sinx'] = P.dram("s_sinx", [LX // 128, 128, LX // 128, 128], BF16)
        S['cosc'] = P.dram("s_cosc", [LC // 128, 128, LC // 128, 128], BF16)
        S['sinc'] = P.dram("s_sinc", [LC // 128, 128, LC // 128, 128], BF16)
        S['ksx'] = P.dram("s_ksx", [LX, E], BF16)
        S['kdx'] = P.dram("s_kdx", [LX, E], BF16)
        S['ksc'] = P.dram("s_ksc", [LC, E], BF16)
        S['kdc'] = P.dram("s_kdc", [LC, E], BF16)
        S['utok'] = P.dram("s_utok", [T, E], BF16)
        LW[1] = W
    if 2 in layers:
        W = {}
        W['w_in'] = P.dram("l2_w_in", [D, 5 * E], F32, "ExternalInput")
        W['lbl'] = P.dram("l2_lbl", [128, 4, 16], F32, "ExternalInput")
        W['gnb'] = cin("l2_gnb", [128, 128])
        G.mask32 = cin("mask32", [128, 2, 128])
        LW[2] = W
    P.barrier()
    for l in layers:
        adaln_phase(P, G, l)
        if l == 2:
            hgrn_layer(P, G, l, LW[l])
        if l == 1:
            hyena_layer(P, G, l, LW[l])
        if l in (0, 3):
            rwkv_layer(P, G, l, LW[l], vres=(l == 3))
            if l == 0:
                for tb in range(T // 512 + 1):
                    a0_, a1_ = tb * 512, min(T, tb * 512 + 512)
                    P.dma('sp', S['vfirst'][a0_:a1_, :], S['vtok'][a0_:a1_, :])
                P.barrier()
        outproj_phase(P, G, l, S['uT'])
    if KDBG:
        dbg = P.dram("dbg_uT", [E, T], BF16, "ExternalOutput")
        for i in range(16):
            P.dma('sp', dbg[i * 128:(i + 1) * 128, :], S['uT'][i * 128:(i + 1) * 128, :])
    for i in range(8):
        P.dma('sp', G.out[:, i * 512:(i + 1) * 512], G.xs[:, LC + i * 512:LC + (i + 1) * 512])
    P.barrier()
    return nc, P


def prep_inputs(inp, b, layers=(0, 1, 2, 3)):
    m = {}
    m['xT0'] = np.ascontiguousarray(np.concatenate([inp['ctx'][b], inp['x'][b]], axis=0).T)
    cond = np.stack([inp['c'][b], inp['c_ctx']], axis=-1)
    m['cond'] = np.ascontiguousarray(cond.reshape(8, 128, 2).transpose(1, 0, 2))
    m['ada_w'] = inp['ada_w']
    m['w_out'] = inp['w_out']
    m['adab'] = col_layout(inp['ada_b'].reshape(-1))
    m['npre'] = col_layout(inp['norm_pre'].reshape(-1))
    m['npost'] = col_layout(inp['norm_post'].reshape(-1))
    c = make_consts()
    m['masks'] = np.ascontiguousarray(c['masks'].transpose(1, 0, 2))
    m['maskt'] = np.ascontiguousarray(c['maskt'].transpose(1, 0, 2))
    m['idn'] = c['idn']
    m['blk'] = c['blk']
    if 1 in layers:
        m['l1_w_in'] = inp['l1_w_in']
        m['l1_cw'] = np.ascontiguousarray(inp['l1_conv_w'].reshape(3, 48, 128).transpose(2, 0, 1))
        m['l1_cb'] = col_layout(inp['l1_conv_b'])
        m['l1_fbias'] = col_layout(inp['l1_filter_bias'])
        m['l1_f_w1'] = inp['l1_f_w1']
        m['l1_f_w2'] = inp['l1_f_w2']
        m['l1_f_w3'] = inp['l1_f_w3']
        m['l1_f_w4'] = inp['l1_f_w4']
        m['l1_fb'] = np.ascontiguousarray(np.stack([inp['l1_f_b1'], inp['l1_f_b2'], inp['l1_f_b3']], axis=1))
        m['l1_sf'] = np.ascontiguousarray(inp['l1_sin_freq'].reshape(64, 1))
        m.update(hyena_consts())
    if 2 in layers:
        m['l2_w_in'] = inp['l2_w_in']
        m['l2_lbl'] = np.ascontiguousarray(inp['hgrn_lb_logits'].reshape(4, 16, 128).transpose(2, 0, 1))
        m['l2_gnb'] = np.ascontiguousarray(np.tile(inp['l2_g_norm'][None, :], (128, 1)))
        s_ = np.arange(128)
        same = (s_[:, None] // 32) == (s_[None, :] // 32)
        m32 = np.zeros((128, 2, 128), np.float32)
        m32[:, 0, :] = same & (s_[:, None] <= s_[None, :])
        m32[:, 1, :] = same & (s_[:, None] >= s_[None, :])
        m['mask32'] = m32
    for l in (0, 3):
        if l not in layers:
            continue
        pre = "l%d_" % l
        m[pre + 'w_in'] = inp[pre + 'w_in']
        m[pre + 'mu'] = col_layout(inp[pre + 'mu'].reshape(-1))
        m[pre + 'w0'] = col_layout(inp[pre + 'w0'].reshape(-1))
        m[pre + 'a0'] = col_layout(inp[pre + 'a0'].reshape(-1))
        m[pre + 'w2'] = np.ascontiguousarray(inp[pre + 'w2'].reshape(128, E))
        m[pre + 'a2'] = np.ascontiguousarray(inp[pre + 'a2'].reshape(128, E))
        m[pre + 'k_k'] = col_layout(inp[pre + 'k_k'])
        m[pre + 'k_a'] = col_layout(inp[pre + 'k_a'])
        m[pre + 'r_k'] = col_layout(inp[pre + 'r_k'].reshape(-1))
        lw = inp[pre + 'ln_w'].reshape(16, 2, 64)
        lb = inp[pre + 'ln_b'].reshape(16, 2, 64)
        m[pre + 'lnw'] = np.ascontiguousarray(np.repeat(lw.transpose(1, 0, 2), 64, axis=0))
        m[pre + 'lnb'] = np.ascontiguousarray(np.repeat(lb.transpose(1, 0, 2), 64, axis=0))
        if l == 3:
            m[pre + 'v0'] = inp[pre + 'v0'].reshape(1, E)
            m[pre + 'v2'] = inp[pre + 'v2']
    return m


_CACHE = {}


def kernel(**inputs):
    inp = {k: np.asarray(v) for k, v in inputs.items()}
    if 'nc' not in _CACHE:
        _CACHE['nc'] = build((0, 1, 2, 3))[0]
    nc = _CACHE['nc']
    in_maps = [prep_inputs(inp, b) for b in range(NC8)]
    res = run_bass_kernel_spmd(nc, in_maps, core_ids=list(range(NC8)))
    out = np.stack([np.ascontiguousarray(res.results[b]["outT"].T) for b in range(NC8)], axis=0)
    return out.astype(np.float32)
```

```python
import math
import os
CUT = int(os.environ.get('KCUT', '99'))
KHP = int(os.environ.get('KHP', '16'))
KDBG = int(os.environ.get('KDBG', '0'))
KOFF = [int(x) for x in os.environ.get('KOFF', '0,18,36,54').split(',')]
KSELF = int(os.environ.get('KSELF', '1'))
from contextlib import ExitStack
import numpy as np
import concourse.bass as bass
import concourse.mybir as mybir
from concourse.bass_utils import run_bass_kernel_spmd

F32 = mybir.dt.float32
BF16 = mybir.dt.bfloat16
I32 = mybir.dt.int32
AF = mybir.ActivationFunctionType
ALU = mybir.AluOpType

D = 1024
E = 2048
LX = 4096
LC = 256
T = LX + LC
NC8 = 8
EPS = 1e-6
TILES = [(0, 256)] + [(256 + 512 * i, 512) for i in range(8)]


class StopBuild(Exception):
    pass


class Buf:
    def __init__(self, t, name):
        self.t = t
        self.name = name
        self.w = {}
        self.r = {}

    def __getitem__(self, idx):
        return V(self, self.t[idx])


class V:
    def __init__(self, buf, ap):
        self.buf = buf
        self.ap = ap

    def __getitem__(self, idx):
        return V(self.buf, self.ap[idx])

    def re(self, pat, **kw):
        return V(self.buf, self.ap.rearrange(pat, **kw))

    def bc(self, shape):
        return V(self.buf, self.ap.to_broadcast(shape))


class Prog:
    NDMA = 40

    def __init__(self, nc):
        self.nc = nc
        self.stack = ExitStack()
        self.eng = {'pe': nc.tensor, 'dve': nc.vector, 'act': nc.scalar, 'pool': nc.gpsimd, 'sp': nc.sync}
        self.sem = {k: self.stack.enter_context(nc.semaphore("s_" + k)) for k in ('pe', 'dve', 'act', 'pool')}
        self.cnt = {k: 0 for k in self.sem}
        self.dsem = [self.stack.enter_context(nc.semaphore("d%d" % i)) for i in range(self.NDMA)]
        self.dval = [0] * self.NDMA
        self.dnext = 0
        self.seen = {e: {} for e in self.eng}
        self.nins = 0
        self.uid = 0

    def sb(self, shape, dt, stack=None, name=None):
        self.uid += 1
        name = (name or "t") + "_%d" % self.uid
        t = (stack or self.stack).enter_context(self.nc.sbuf_tensor(name, list(shape), dt))
        return Buf(t, name)

    def ps(self, shape, dt=F32, stack=None, name=None):
        self.uid += 1
        name = (name or "p") + "_%d" % self.uid
        t = (stack or self.stack).enter_context(self.nc.psum_tensor(name, list(shape), dt))
        return Buf(t, name)

    def dram(self, name, shape, dt, kind=None):
        if kind:
            t = self.nc.dram_tensor(name, list(shape), dt, kind=kind).ap()
        else:
            t = self.nc.dram_tensor(name, list(shape), dt).ap()
        return Buf(t, name)

    def _wait(self, e, key, val):
        if val <= 0 or (e == 'pe' and key == 'pe'):
            return
        if not KSELF and e == key and e in ('dve', 'act'):
            return
        if self.seen[e].get(key, 0) >= val:
            return
        sem = self.sem[key] if isinstance(key, str) else self.dsem[key]
        self.eng[e].wait_ge(sem, val)
        self.nins += 1
        self.seen[e][key] = val

    def _deps(self, e, reads, writes, partial):
        for b in reads:
            for k, v in b.w.items():
                self._wait(e, k, v)
        for b in writes:
            if not partial:
                for k, v in b.w.items():
                    self._wait(e, k, v)
            for k, v in b.r.items():
                self._wait(e, k, v)

    def _mark(self, key, val, reads, writes, partial):
        for b in reads:
            if b.r.get(key, 0) < val:
                b.r[key] = val
        for b in writes:
            if partial:
                b.w[key] = val
            else:
                b.w = {key: val}
                b.r = {}

    limit = None

    def op(self, e, fn, reads, writes, partial=False):
        if self.limit is not None:
            if self.limit <= 0:
                raise StopBuild()
            self.limit -= 1
        reads = [v.buf for v in reads if isinstance(v, V)]
        writes = [v.buf for v in writes]
        self._deps(e, reads, writes, partial)
        ins = fn(self.eng[e])
        self.cnt[e] += 1
        ins.then_inc(self.sem[e], 1)
        self.nins += 1
        self._mark(e, self.cnt[e], reads, writes, partial)

    def dma(self, q, out, in_, partial=True):
        self._deps(q, [in_.buf], [out.buf], partial)
        i = self.dnext
        self.dnext = (i + 1) % self.NDMA
        self._wait(q, i, self.dval[i])
        self.dval[i] += 16
        self.eng[q].dma_start(out=out.ap, in_=in_.ap).then_inc(self.dsem[i], 16)
        self.nins += 1
        self._mark(i, self.dval[i], [in_.buf], [out.buf], partial)

    def barrier(self):
        for e in self.eng:
            for k in self.sem:
                self._wait(e, k, self.cnt[k])
            for i in range(self.NDMA):
                self._wait(e, i, self.dval[i])

    def mm(self, out, lhsT, rhs, start=True, stop=True):
        self.op('pe', lambda e: e.matmul(out.ap, lhsT.ap, rhs.ap, start=start, stop=stop), [lhsT, rhs], [out], partial=True)

    def act(self, out, in_, func=AF.Identity, bias=None, scale=None, partial=False, eng='act'):
        kw = {}
        rd = [in_]
        if bias is not None:
            kw['bias'] = bias.ap if isinstance(bias, V) else bias
            rd.append(bias)
        if scale is not None:
            kw['scale'] = scale.ap if isinstance(scale, V) else scale
            rd.append(scale)
        self.op('act', lambda e: e.activation(out=out.ap, in_=in_.ap, func=func, **kw), rd, [out], partial)

    def tt(self, out, a, b, op, partial=False, eng='dve'):
        self.op(eng, lambda e: e.tensor_tensor(out=out.ap, in0=a.ap, in1=b.ap, op=op), [a, b], [out], partial)

    def ts(self, out, a, s1, s2, op0, op1=None, partial=False, eng='dve'):
        g = lambda s: s.ap if isinstance(s, V) else s
        if op1 is None:
            self.op(eng, lambda e: e.tensor_scalar(out=out.ap, in0=a.ap, scalar1=g(s1), scalar2=None, op0=op0), [a, s1], [out], partial)
        else:
            self.op(eng, lambda e: e.tensor_scalar(out=out.ap, in0=a.ap, scalar1=g(s1), scalar2=g(s2), op0=op0, op1=op1), [a, s1, s2], [out], partial)

    def stt(self, out, a, s, b, op0, op1, partial=False):
        g = s.ap if isinstance(s, V) else s
        self.op('dve', lambda e: e.scalar_tensor_tensor(out=out.ap, in0=a.ap, scalar=g, in1=b.ap, op0=op0, op1=op1), [a, s, b], [out], partial)

    def copy(self, out, in_, partial=False, eng='dve'):
        if eng == 'act':
            self.act(out, in_, AF.Identity, partial=partial)
        else:
            self.op(eng, lambda e: e.tensor_copy(out=out.ap, in_=in_.ap), [in_], [out], partial)

    def memset(self, out, val, eng='dve', partial=False):
        self.op(eng, lambda e: e.memset(out.ap, val), [], [out], partial)

    def rsqrt(self, out, in_, addc, partial=False):
        self.ts(out, in_, addc, None, ALU.add, partial=partial)
        self.act(out, out, AF.Ln, partial=partial)
        self.act(out, out, AF.Exp, scale=-0.5, partial=partial)

    def scan(self, out, d0, d1, init, op0, op1, partial=False):
        g = init.ap if isinstance(init, V) else init
        self.op('dve', lambda e: e.tensor_tensor_scan(out=out.ap, data0=d0.ap, data1=d1.ap, initial=g, op0=op0, op1=op1), [d0, d1, init], [out], partial)


def col_layout(v):
    v = np.asarray(v, np.float32)
    return np.ascontiguousarray(v.reshape(-1, 128).T)


def hyena_consts():
    c = {}
    f32 = np.float32
    for nm, L in (('x', LX), ('c', LC)):
        t = np.linspace(0.0, 1.0, L, dtype=f32)[:, None]
        freqs = np.linspace(1e-4, 15, 16, dtype=f32)[None, :]
        ang = (f32(2.0 * math.pi / L) * np.arange(L, dtype=f32)[:, None]) * freqs
        z = np.concatenate([t, np.cos(ang), -np.sin(ang)], axis=-1).astype(f32)
        c['zT' + nm] = np.ascontiguousarray(z.T)
        c['tn' + nm] = np.ascontiguousarray(-t[:, 0].reshape(L // 128, 128).T)
    deltas = np.abs(np.linspace(math.log(1e-2) / 1.5, math.log(1e-2) / 0.3, E, dtype=f32))
    c['deltab'] = np.ascontiguousarray(np.tile(deltas[None, :], (128, 1)).astype(f32))
    c['jrow'] = np.ascontiguousarray(np.tile(np.arange(128, dtype=f32)[None, :], (128, 1)))
    c['tcol'] = np.ascontiguousarray((np.arange(32, dtype=f32)[None, :] * 128 + np.arange(128, dtype=f32)[:, None]))
    alt = np.where(np.arange(128) % 2 == 0, 1.0, -1.0).astype(f32)
    c['altf'] = np.ascontiguousarray(np.stack([alt, alt], axis=1))
    c['altrf'] = np.ascontiguousarray(alt.reshape(1, 128))
    return c


def make_consts():
    c = {}
    idn = np.eye(128, dtype=np.float32)
    s = np.arange(128)
    blk = (s[:, None] // 64) == (s[None, :] // 64)
    sl = s % 64
    m = np.zeros((2, 128, 256), np.float32)
    m[0, :, :128] = blk & (sl[:, None] < sl[None, :])
    m[0, :, 128:] = blk & (sl[:, None] <= sl[None, :])
    m[1, :, :128] = blk & (sl[:, None] > sl[None, :])
    m[1, :, 128:] = blk & (sl[:, None] >= sl[None, :])
    c['masks'] = m
    mt = np.zeros((2, 128, 128), np.float32)
    mt[0] = m[0, :, :128].T
    mt[1] = m[1, :, :128].T
    c['maskt'] = mt
    c['idn'] = idn
    c['blk'] = blk.astype(np.float32)
    return c


class Ctx:
    pass


def adaln_phase(P, G, l):
    with ExitStack() as st:
        wt = [P.sb([128, 8, 512], F32, st, "adaw") for _ in range(2)]
        mod = P.sb([128, 24, 2], F32, st, "mod")
        for nb in range(6):
            w = wt[nb % 2]
            P.dma('sp', w[:, :, :], G.ada_w[l, :, nb * 512:(nb + 1) * 512].re("(c p) n -> p c n", p=128), partial=False)
            pt = G.psum[nb % 2]
            for j in range(4):
                ob = nb * 4 + j
                for kc in range(8):
                    P.mm(pt[:, j * 2:j * 2 + 2], w[:, kc, j * 128:(j + 1) * 128], G.scT[:, kc, :], start=(kc == 0), stop=(kc == 7))
            P.tt(mod[:, nb * 4:nb * 4 + 4, :], pt[:, 0:8].re("p (a b) -> p a b", b=2),
                 G.adab[:, l * 24 + nb * 4:l * 24 + nb * 4 + 4].re("p (a o) -> p a o", o=1).bc([128, 4, 2]), ALU.add, partial=True)
        P.ts(G.A[:, :, :], mod[:, 8:16, :], 1.0, 32.0, ALU.add, ALU.mult)
        P.tt(G.A[:, :, :], G.A[:, :, :], G.npre[:, l * 8:(l + 1) * 8].re("p (a o) -> p a o", o=1).bc([128, 8, 2]), ALU.mult)
        P.copy(G.B[:, :, :], mod[:, 0:8, :])
        P.ts(G.Gt[:, :, :], mod[:, 16:24, :], 32.0, None, ALU.mult)
        P.tt(G.Gt[:, :, :], G.Gt[:, :, :], G.npost[:, l * 8:(l + 1) * 8].re("p (a o) -> p a o", o=1).bc([128, 8, 2]), ALU.mult)
        P.barrier()


def norm_phase(P, G, hT):
    with ExitStack() as st:
        xt = [P.sb([128, 8, 512], F32, st, "xt") for _ in range(2)]
        sq = P.sb([128, 8, 512], F32, st, "sq")
        rs = P.sb([128, 512], F32, st, "rs")
        tmp = [P.sb([128, 512], F32, st, "tmp") for _ in range(2)]
        for ti, (n0, nt) in enumerate(TILES):
            j = 1 if n0 == 0 else 0
            x = xt[ti % 2]
            P.dma('sp', x[:, :, :nt], G.xs[:, n0:n0 + nt].re("(c p) n -> p c n", p=128), partial=False)
            P.act(sq[:, :, :nt], x[:, :, :nt], AF.Square)
            pt = G.psum[ti % 2]
            for c in range(8):
                P.mm(pt[:, :nt], G.ones[:, :], sq[:, c, :nt], start=(c == 0), stop=(c == 7))
            P.rsqrt(rs[:, :nt], pt[:, :nt], 1024.0 * EPS)
            for c in range(8):
                t = tmp[c % 2]
                P.tt(t[:, :nt], x[:, c, :nt], rs[:, :nt], ALU.mult)
                P.act(hT[:, c, n0:n0 + nt], t[:, :nt], AF.Identity, bias=G.B[:, c, j:j + 1], scale=G.A[:, c, j:j + 1], partial=True)
        P.barrier()


def outproj_phase(P, G, l, uT):
    with ExitStack() as st:
        wo = P.sb([128, 16, 1024], BF16, st, "wo")
        P.dma('pool', wo[:, :, :], G.w_out[l].re("(c p) n -> p c n", p=128), partial=False)
        ut = [P.sb([128, 16, 512], BF16, st, "ut") for _ in range(2)]
        o = P.sb([128, 8, 512], F32, st, "o")
        sq = P.sb([128, 8, 512], F32, st, "sq")
        rs = P.sb([128, 512], F32, st, "rs")
        xt = [P.sb([128, 8, 512], F32, st, "xt") for _ in range(2)]
        for ti, (n0, nt) in enumerate(TILES):
            j = 1 if n0 == 0 else 0
            u = ut[ti % 2]
            x = xt[ti % 2]
            P.dma('sp', u[:, :, :nt], uT[:, n0:n0 + nt].re("(c p) n -> p c n", p=128), partial=False)
            P.dma('sp', x[:, :, :nt], G.xs[:, n0:n0 + nt].re("(c p) n -> p c n", p=128), partial=False)
            for ob in range(8):
                pt = G.psum[ob % 4]
                for ec in range(16):
                    P.mm(pt[:, :nt], wo[:, ec, ob * 128:(ob + 1) * 128], u[:, ec, :nt], start=(ec == 0), stop=(ec == 15))
                P.copy(o[:, ob, :nt], pt[:, :nt], partial=True, eng=('act' if ob % 2 else 'dve'))
            P.act(sq[:, :, :nt], o[:, :, :nt], AF.Square)
            pt = G.psum[4 + ti % 2]
            for c in range(8):
                P.mm(pt[:, :nt], G.ones[:, :], sq[:, c, :nt], start=(c == 0), stop=(c == 7))
            P.rsqrt(rs[:, :nt], pt[:, :nt], 1024.0 * EPS)
            for c in range(8):
                P.tt(o[:, c, :nt], o[:, c, :nt], rs[:, :nt], ALU.mult, partial=True)
                P.stt(x[:, c, :nt], o[:, c, :nt], G.Gt[:, c, j:j + 1], x[:, c, :nt], ALU.mult, ALU.add, partial=True)
            P.dma('sp', G.xs[:, n0:n0 + nt].re("(c p) n -> p c n", p=128), x[:, :, :nt])
        P.barrier()


def load_w_block(P, dst, w_dram, c0, ncols):
    P.dma('pool', dst[:, :, :ncols], w_dram[:, c0:c0 + ncols].re("(c p) n -> p c n", p=128), partial=False)


def proj_fm(P, G, src, w_dram, c0, ncols, dst_dram, r0, wbufs, stg, cnt, post=None):
    for cb in range(0, ncols, 512):
        nb = min(512, ncols - cb)
        w = wbufs[cnt[0] % 2]
        cnt[0] += 1
        load_w_block(P, w, w_dram, c0 + cb, nb)
        for (n0, nt) in TILES:
            for eb in range(0, nb, 128):
                ne = min(128, nb - eb)
                k = cnt[1] % 4
                cnt[1] += 1
                pt = G.psum[k]
                for c in range(8):
                    P.mm(pt[:ne, :nt], w[:, c, eb:eb + ne], src[:, c, n0:n0 + nt], start=(c == 0), stop=(c == 7))
                s = stg[k]
                if post is None:
                    P.copy(s[:ne, :nt], pt[:ne, :nt], eng=('act' if k % 2 else 'dve'))
                else:
                    post(s[:ne, :nt], pt[:ne, :nt], r0 + cb + eb, ne)
                P.dma('sp', dst_dram[r0 + cb + eb:r0 + cb + eb + ne, n0:n0 + nt], s[:ne, :nt])


def proj_tm(P, G, src, w_dram, c0, ncols, dst_dram, wbufs, stg, cnt, func=None):
    for cb in range(0, ncols, 512):
        nb = min(512, ncols - cb)
        w = wbufs[cnt[0] % 2]
        cnt[0] += 1
        load_w_block(P, w, w_dram, c0 + cb, nb)
        for tb in range(T // 128):
            k = cnt[1] % 4
            cnt[1] += 1
            pt = G.psum[k]
            for c in range(8):
                P.mm(pt[:, :nb], src[:, c, tb * 128:(tb + 1) * 128], w[:, c, :nb], start=(c == 0), stop=(c == 7))
            s = stg[k]
            if func is None:
                P.copy(s[:, :nb], pt[:, :nb], eng=('act' if k % 2 else 'dve'))
            else:
                P.act(s[:, :nb], pt[:, :nb], func)
            P.dma('sp', dst_dram[tb * 128:(tb + 1) * 128, cb:cb + nb], s[:, :nb])


def rwkv_layer(P, G, l, W, vres):
    nc = P.nc
    S = G.scr
    with ExitStack() as st:
        hT = P.sb([128, 8, T], BF16, st, "hT")
        norm_phase(P, G, hT)
        hm = P.sb([128, 8, T], BF16, st, "hm")
        wb = [P.sb([128, 8, 512], BF16, st, "wb") for _ in range(2)]
        stg = [P.sb([128, 512], F32, st, "stg") for _ in range(4)]
        cnt = [0, 0]
        mu = W['mu']
        ncol_base = [0, E, 2 * E, 3 * E, 4 * E, 4 * E + 128]
        for g in range(6):
            for c in range(8):
                mcol = mu[:, g * 8 + c:g * 8 + c + 1]
                ocol = W['om'][:, g * 8 + c:g * 8 + c + 1]
                P.act(hm[:, c, :], hT[:, c, :], AF.Identity, scale=ocol, partial=True)
                if c < 4:
                    P.stt(hm[:, c, 1:LC], hT[:, c, 0:LC - 1], mcol, hm[:, c, 1:LC], ALU.mult, ALU.add, partial=True)
                else:
                    P.stt(hm[:, c, 0:LC - 1], hT[:, c, 1:LC], mcol, hm[:, c, 0:LC - 1], ALU.mult, ALU.add, partial=True)
                hx = hT[:, c, LC:].re("p (r w) -> p r w", w=64)
                mx = hm[:, c, LC:].re("p (r w) -> p r w", w=64)
                if c < 2:
                    P.stt(mx[:, :, 1:64], hx[:, :, 0:63], mcol, mx[:, :, 1:64], ALU.mult, ALU.add, partial=True)
                elif c < 4:
                    P.stt(mx[:, :, 0:63], hx[:, :, 1:64], mcol, mx[:, :, 0:63], ALU.mult, ALU.add, partial=True)
                elif c < 6:
                    P.stt(mx[:, 1:64, :], hx[:, 0:63, :], mcol, mx[:, 1:64, :], ALU.mult, ALU.add, partial=True)
                else:
                    P.stt(mx[:, 0:63, :], hx[:, 1:64, :], mcol, mx[:, 0:63, :], ALU.mult, ALU.add, partial=True)
            c0 = ncol_base[g]
            if g == 0:
                proj_fm(P, G, hm, W['w_in'], c0, E, S['rT'], 0, wb, stg, cnt)
            elif g == 1:
                proj_fm(P, G, hm, W['w_in'], c0, E, S['kT'], 0, wb, stg, cnt)
            elif g == 2:
                proj_tm(P, G, hm, W['w_in'], c0, E, S['vtok'], wb, stg, cnt)
                if vres:
                    proj_fm(P, G, hm, W['w_in'], 4 * E + 256, 32, S['lovT'], 0, wb, stg, cnt)
            elif g == 3:
                proj_tm(P, G, hm, W['w_in'], c0, E, S['gtok'], wb, stg, cnt, func=AF.Silu)
            else:
                proj_fm(P, G, hm, W['w_in'], c0, 128, S['loT'], (g - 4) * 128, wb, stg, cnt)
        P.barrier()
    if CUT <= 1:
        return
    if vres:
        with ExitStack() as st:
            lov = P.sb([33, T], F32, st, "lov")
            v2a = P.sb([33, E], F32, st, "v2a")
            P.memset(lov[:, :], 1.0)
            P.dma('sp', lov[0:32, :], S['lovT'][0:32, :], partial=False)
            P.dma('sp', v2a[0:32, :], W['v2'][:, :])
            P.dma('sp', v2a[32:33, :], W['v0'][:, :])
            vt = [P.sb([128, E], F32, st, "vt") for _ in range(2)]
            vf = [P.sb([128, E], F32, st, "vf") for _ in range(2)]
            sg = P.sb([128, E], F32, st, "sg")
            for tb in range(T // 128):
                v = vt[tb % 2]
                f = vf[tb % 2]
                P.dma('sp', v[:, :], S['vtok'][tb * 128:(tb + 1) * 128, :], partial=False)
                P.dma('sp', f[:, :], S['vfirst'][tb * 128:(tb + 1) * 128, :], partial=False)
                for q in range(4):
                    pt = G.psum[q]
                    P.mm(pt[:, :], lov[:, tb * 128:(tb + 1) * 128], v2a[:, q * 512:(q + 1) * 512])
                    P.act(sg[:, q * 512:(q + 1) * 512], pt[:, :], AF.Sigmoid, partial=True)
                P.tt(f[:, :], f[:, :], v[:, :], ALU.subtract)
                P.tt(f[:, :], f[:, :], sg[:, :], ALU.mult)
                P.tt(v[:, :], v[:, :], f[:, :], ALU.add)
                P.dma('sp', S['vtok'][tb * 128:(tb + 1) * 128, :], v[:, :])
            P.barrier()
    with ExitStack() as st:
        lo = P.sb([128, 2, T], F32, st, "lo")
        P.dma('sp', lo[:, 0, :], S['loT'][0:128, :], partial=False)
        P.dma('sp', lo[:, 1, :], S['loT'][128:256, :], partial=True)
        P.act(lo[:, 0, :], lo[:, 0, :], AF.Tanh, partial=True)
        w2 = P.sb([128, E], F32, st, "w2")
        a2 = P.sb([128, E], F32, st, "a2")
        P.dma('sp', w2[:, :], W['w2'][:, :], partial=False)
        P.dma('sp', a2[:, :], W['a2'][:, :], partial=False)
        sa = [P.sb([128, 512], F32, st, "sa") for _ in range(4)]
        i = 0
        for hp in range(16):
            e0 = hp * 128
            for z in range(2):
                zs = slice(z * 64, z * 64 + 64)
                for (n0, nt) in TILES:
                    pt = G.psum[i % 4]
                    s1 = sa[i % 4]
                    P.mm(pt[:, :nt], a2[zs, e0:e0 + 128], lo[zs, 1, n0:n0 + nt])
                    P.act(s1[:, :nt], pt[:, :nt], AF.Sigmoid, bias=W['a0'][:, z * 16 + hp:z * 16 + hp + 1])
                    P.dma('sp', S['aT'][z * E + e0:z * E + e0 + 128, n0:n0 + nt], s1[:, :nt])
                    i += 1
                    pt = G.psum[i % 4]
                    s2 = sa[i % 4]
                    P.mm(pt[:, :nt], w2[zs, e0:e0 + 128], lo[zs, 0, n0:n0 + nt])
                    P.act(s2[:, :nt], pt[:, :nt], AF.Sigmoid, bias=W['w0'][:, z * 16 + hp:z * 16 + hp + 1])
                    P.ts(s2[:, :nt], s2[:, :nt], -math.exp(-0.5), None, ALU.mult)
                    P.dma('sp', S['lwT'][z * E + e0:z * E + e0 + 128, n0:n0 + nt], s2[:, :nt])
                    i += 1
        P.barrier()
    if CUT <= 2:
        return
    with ExitStack() as st:
        NCH = T // 64
        NSLOT = 2 if KHP >= 2 else 1
        HB = []
        for s_ in range(NSLOT):
            H = Ctx()
            H.vst = P.sb([128, NCH, 64], BF16, st, "vst")
            H.ybuf = P.sb([128, NCH, 64], F32, st, "ybuf")
            H.bon = P.sb([128, NCH, 1], F32, st, "bon")
            HB.append(H)
        HC = NCH // 2
        gt_ = P.sb([128, NCH, 64], BF16, st, "gt_")
        tmpR = P.sb([128, HC, 64], F32, st, "tmpR")
        stat = P.sb([128, NCH, 2], F32, st, "stat")
        uob = P.sb([128, HC, 128], BF16, st, "uob")
        P.memset(uob[:, :, :], 0.0, eng='pool')
        uT_sb = P.sb([128, T // 2], BF16, st, "uT_sb")
        onesb = P.sb([128, 2], BF16, st, "onesb")
        P.memset(onesb[:, :], 1.0)
        idst = P.sb([128, 64], BF16, st, "idst")
        P.copy(idst[0:64, :], G.idnb[0:64, 0:64], partial=True)
        P.copy(idst[64:128, :], G.idnb[64:128, 64:128], partial=True)
        ones512 = P.sb([128, 128], F32, st, "ones128")
        P.memset(ones512[:, :], 1.0)
        pp = G.psum
        CH = []
        for ci_ in range(2 * NSLOT):
            C = Ctx()
            C.wk = [P.sb([128, 128], F32, st, "wk") for _ in range(14)]
            C.AR = P.sb([128, 2, 256], BF16, st, "AR")
            C.Kt = P.sb([128, 2, 128], BF16, st, "Kt")
            C.Bt = P.sb([128, 2, 128], BF16, st, "Bt")
            C.Pb = P.sb([128, 2, 128], BF16, st, "Pb")
            for b_ in (C.AR, C.Kt, C.Bt, C.Pb):
                P.memset(b_[:, :, :], 0.0, eng='pool')
            C.cend = P.sb([128, 4], F32, st, "cend")
            C.MM = P.sb([128, 512], BF16, st, "MM")
            C.KB = P.sb([128, 256], BF16, st, "KB")
            C.PS = [P.sb([128, 256], BF16, st, "PSb") for _ in range(2)]
            C.PT = [P.sb([128, 128], BF16, st, "PTb") for _ in range(2)]
            C.Wt = P.sb([128, 64], BF16, st, "Wt")
            C.Ut = P.sb([128, 64], BF16, st, "Ut")
            C.Sf = P.sb([128, 64], F32, st, "Sf")
            C.Sb = [P.sb([128, 64], BF16, st, "Sb") for _ in range(2)]
            C.tmpS = P.sb([128, 64], F32, st, "tmpS")
            C.banks = (pp[2 * ci_], pp[2 * ci_ + 1])
            CH.append(C)
        STILES = [(128 * i, 128) for i in range(T // 128)]

        def chain(hp, z, C, H):
            e0 = hp * 128
            vst, ybuf, bon = H.vst, H.ybuf, H.bon
            r, k, a, lw, t1, t2, kk, Gc, Ec, kd, eg, ex, ei, bco = C.wk
            AR, Kt, Bt, Pb, cend = C.AR, C.Kt, C.Bt, C.Pb, C.cend
            MM, KB, PS_, PT_, Wt, Ut, Sf, Sb, tmpS = C.MM, C.KB, C.PS, C.PT, C.Wt, C.Ut, C.Sf, C.Sb, C.tmpS
            bD, b2 = C.banks
            b0 = b1 = bD
            tiles = list(STILES) if z == 0 else [STILES[1], STILES[0]] + list(STILES[:1:-1])
            P.memset(Sf[:, :], 0.0); yield
            P.memset(Sb[0][:, :], 0.0); yield
            sbi = 0
            for (n0, nt) in tiles:
                ncn = nt // 64
                cb = n0 // 64
                P.dma('sp', r[:, :nt], S['rT'][e0:e0 + 128, n0:n0 + nt], partial=False); yield
                P.dma('sp', k[:, :nt], S['kT'][e0:e0 + 128, n0:n0 + nt], partial=False); yield
                P.dma('sp', a[:, :nt], S['aT'][z * E + e0:z * E + e0 + 128, n0:n0 + nt], partial=False); yield
                P.dma('sp', lw[:, :nt], S['lwT'][z * E + e0:z * E + e0 + 128, n0:n0 + nt], partial=False); yield
                P.ts(t1[:, :nt], k[:, :nt], W['k_k'][:, hp:hp + 1], None, ALU.mult); yield
                P.tt(t2[:, :nt], t1[:, :nt], t1[:, :nt], ALU.mult); yield
                P.mm(b1[:, :nt], G.blk[:, :], t2[:, :nt]); yield
                P.ts(kk[:, :nt], b1[:, :nt], 1e-24, None, ALU.add); yield
                P.act(kk[:, :nt], kk[:, :nt], AF.Ln); yield
                P.act(kk[:, :nt], kk[:, :nt], AF.Exp, scale=-0.5); yield
                P.tt(kk[:, :nt], kk[:, :nt], t1[:, :nt], ALU.mult); yield
                P.scan(Gc[:, :nt], ones512[:, :nt], lw[:, :nt], 0.0, ALU.mult, ALU.add); yield
                P.tt(Ec[:, :nt], Gc[:, :nt], lw[:, :nt], ALU.subtract); yield
                v3 = lambda b_: b_[:, :nt].re("p (c s) -> p c s", s=64)
                G3, E3, t13, t23 = v3(Gc), v3(Ec), v3(t1), v3(t2)
                ce3 = cend[:, :ncn].re("p (c o) -> p c o", o=1)
                if z == 0:
                    base = E3[:, :, 0:1].bc([128, ncn, 64])
                    P.tt(t13, G3, base, ALU.subtract); yield
                    P.tt(t23, E3, base, ALU.subtract); yield
                else:
                    base = G3[:, :, 63:64].bc([128, ncn, 64])
                    P.tt(t13, base, E3, ALU.subtract); yield
                    P.tt(t23, base, G3, ALU.subtract); yield
                P.tt(ce3, G3[:, :, 63:64], E3[:, :, 0:1], ALU.subtract); yield
                P.act(cend[:, :ncn], cend[:, :ncn], AF.Exp); yield
                P.ts(kd[:, :nt], a[:, :nt], -1.0, W['k_a'][:, hp:hp + 1], ALU.add, ALU.mult); yield
                P.stt(kd[:, :nt], kd[:, :nt], 1.0, k[:, :nt], ALU.add, ALU.mult); yield
                P.act(eg[:, :nt], t1[:, :nt], AF.Exp); yield
                P.act(ex[:, :nt], t2[:, :nt], AF.Exp); yield
                P.act(ei[:, :nt], t1[:, :nt], AF.Exp, scale=-1.0); yield
                P.tt(bco[:, :nt], kk[:, :nt], a[:, :nt], ALU.mult); yield
                r3, kk3, kd3, eg3, ex3, ei3, bc3 = v3(r), v3(kk), v3(kd), v3(eg), v3(ex), v3(ei), v3(bco)
                for hpar in range(2):
                    ps_ = slice(hpar * 64, hpar * 64 + 64)
                    cs = slice(hpar * 64, hpar * 64 + 64)
                    P.stt(AR[ps_, :ncn, cs], kk3[ps_], -1.0, ex3[ps_], ALU.mult, ALU.mult, partial=True); yield
                    P.tt(AR[ps_, :ncn, 128 + hpar * 64:128 + hpar * 64 + 64], r3[ps_], eg3[ps_], ALU.mult, partial=True); yield
                    P.tt(Kt[ps_, :ncn, cs], kd3[ps_], ei3[ps_], ALU.mult, partial=True); yield
                    P.tt(Bt[ps_, :ncn, cs], bc3[ps_], ei3[ps_], ALU.mult, partial=True); yield
                    P.stt(Pb[ps_, :ncn, cs], r3[ps_], W['r_k'][ps_, hp:hp + 1], kd3[ps_], ALU.mult, ALU.mult, partial=True); yield
                corder = range(ncn) if z == 0 else range(ncn - 1, -1, -1)
                for cl in corder:
                    c = cb + cl
                    ARc, Ktc, Btc = AR[:, cl, :], Kt[:, cl, :], Bt[:, cl, :]
                    P.mm(bD[:, 0:256], Ktc, ARc); yield
                    P.mm(bD[:, 256:512], Btc, ARc); yield
                    P.tt(MM[:, :].re("p (o m) -> p o m", o=2), bD[:, 0:512].re("p (o m) -> p o m", o=2),
                         G.masks[:, z, :].re("p (o m) -> p o m", o=1).bc([128, 2, 256]), ALU.mult); yield
                    P.mm(bD[:, 0:128], AR[:, cl, 0:128], Btc); yield
                    P.tt(PT_[0][:, :], bD[:, 0:128], G.maskt[:, z, :], ALU.mult); yield
                    P.mm(bD[:, 128:256], Ktc, G.idnb[:, :]); yield
                    P.mm(bD[:, 256:384], Btc, G.idnb[:, :]); yield
                    P.mm(b2[:, 0:128], PT_[0][:, :], MM[:, 256:384]); yield
                    P.mm(b2[:, 256:384], MM[:, 256:384], PT_[0][:, :]); yield
                    P.copy(KB[:, :], bD[:, 128:384]); yield
                    P.copy(PS_[1][:, 0:128], b2[:, 0:128], partial=True, eng='act'); yield
                    P.tt(PS_[1][:, 128:256], MM[:, 256:384], G.idnb[:, :], ALU.add, partial=True); yield
                    P.copy(PT_[1][:, :], b2[:, 256:384], eng='act'); yield
                    cur = 1
                    for lev in range(1, 6):
                        nxt = 1 - cur
                        if lev < 5:
                            P.mm(b2[:, 0:128], PT_[cur][:, :], PS_[cur][:, 0:128]); yield
                            P.mm(b2[:, 128:256], PT_[cur][:, :], PS_[cur][:, 128:256], start=True, stop=False); yield
                            P.mm(b2[:, 128:256], G.idnb[:, :], PS_[cur][:, 128:256], start=False, stop=True); yield
                            P.mm(b2[:, 256:384], PS_[cur][:, 0:128], PT_[cur][:, :]); yield
                            P.copy(PS_[nxt][:, :], b2[:, 0:256], eng='act'); yield
                            P.copy(PT_[nxt][:, :], b2[:, 256:384], eng='act'); yield
                        else:
                            P.mm(b2[:, 128:256], PT_[cur][:, :], PS_[cur][:, 128:256], start=True, stop=False); yield
                            P.mm(b2[:, 128:256], G.idnb[:, :], PS_[cur][:, 128:256], start=False, stop=True); yield
                            P.copy(PS_[nxt][:, 128:256], b2[:, 128:256], partial=True, eng='act'); yield
                        cur = nxt
                    X = PS_[cur][:, 128:256]
                    S0 = Sb[sbi]
                    P.mm(b2[:, 384:448], AR[:, cl, 0:128], S0[:, :], start=True, stop=False); yield
                    P.mm(b2[:, 384:448], MM[:, 0:128], vst[:, c, :], start=False, stop=True); yield
                    P.copy(Wt[:, :], b2[:, 384:448], eng='act'); yield
                    P.mm(b2[:, 448:512], X, Wt[:, :]); yield
                    P.copy(Ut[:, :], b2[:, 448:512], eng='act'); yield
                    P.mm(bD[:, 448:512], AR[:, cl, 128:256], S0[:, :], start=True, stop=False); yield
                    P.mm(bD[:, 448:512], MM[:, 128:256], vst[:, c, :], start=False, stop=False); yield
                    P.mm(bD[:, 448:512], MM[:, 384:512], Ut[:, :], start=False, stop=True); yield
                    P.mm(bD[:, 0:2], Pb[:, cl, :], onesb[:, :]); yield
                    P.mm(bD[:, 384:448], KB[:, 0:128], vst[:, c, :], start=True, stop=False); yield
                    P.mm(bD[:, 384:448], KB[:, 128:256], Ut[:, :], start=False, stop=True); yield
                    P.tt(ybuf[:, c, :], ybuf[:, c, :], bD[:, 448:512], ALU.add, partial=True); yield
                    P.tt(bon[:, c, :], bon[:, c, :], bD[:, 0:1], ALU.add, partial=True); yield
                    P.tt(tmpS[:, :], bD[:, 384:448], Sf[:, :], ALU.add); yield
                    P.ts(Sf[:, :], tmpS[:, :], cend[:, cl:cl + 1], None, ALU.mult); yield
                    sbi = 1 - sbi
                    P.act(Sb[sbi][:, :], tmpS[:, :], AF.Identity, scale=cend[:, cl:cl + 1]); yield

        for hp0 in range(0, KHP, NSLOT):
            hps = list(range(hp0, min(KHP, hp0 + NSLOT)))
            gens = []
            for si, hp in enumerate(hps):
                e0 = hp * 128
                H = HB[si]
                for hpar in range(2):
                    for (n0, nt) in TILES:
                        P.dma('pool', H.vst[hpar * 64:(hpar + 1) * 64, n0 // 64:(n0 + nt) // 64, :],
                              S['vtok'][n0:n0 + nt, e0 + hpar * 64:e0 + hpar * 64 + 64].re("(c s) v -> s c v", s=64), partial=(hpar == 1 or n0 > 0))
                P.memset(H.ybuf[:, :, :], 0.0)
                P.memset(H.bon[:, :, :], 0.0)
                for z in range(2):
                    gens.append(chain(hp, z, CH[2 * si + z], H))
            for gi_, g_ in enumerate(gens):
                for _ in range(KOFF[gi_ % 4]):
                    next(g_)
            while gens:
                for g_ in list(gens):
                    try:
                        next(g_)
                    except StopIteration:
                        gens.remove(g_)
            for si, hp in enumerate(hps):
                e0 = hp * 128
                H = HB[si]
                vst, ybuf, bon = H.vst, H.ybuf, H.bon
                for hpar in range(2):
                    for (n0, nt) in TILES:
                        P.dma('pool', gt_[hpar * 64:(hpar + 1) * 64, n0 // 64:(n0 + nt) // 64, :],
                              S['gtok'][n0:n0 + nt, e0 + hpar * 64:e0 + hpar * 64 + 64].re("(c s) v -> s c v", s=64), partial=(hpar == 1 or n0 > 0))
                P.op('dve', lambda e: e.tensor_reduce(out=stat[:, :, 0:1].ap, in_=ybuf[:, :, :].ap, axis=mybir.AxisListType.X, op=ALU.add),
                     [ybuf[:, :, :]], [stat[:, :, 0:1]], partial=True)
                P.ts(stat[:, :, 0:1], stat[:, :, 0:1], 1.0 / 64, None, ALU.mult, partial=True)
                P.tt(ybuf[:, :, :], ybuf[:, :, :], stat[:, :, 0:1].bc([128, NCH, 64]), ALU.subtract)
                for hf_ in range(2):
                    hs = slice(hf_ * HC, (hf_ + 1) * HC)
                    P.tt(tmpR[:, :, :], ybuf[:, hs, :], ybuf[:, hs, :], ALU.mult)
                    P.op('dve', lambda e, hs=hs: e.tensor_reduce(out=stat[:, hs, 1:2].ap, in_=tmpR[:, :, :].ap, axis=mybir.AxisListType.X, op=ALU.add),
                         [tmpR[:, :, :]], [stat[:, hs, 1:2]], partial=True)
                P.ts(stat[:, :, 1:2], stat[:, :, 1:2], 1.0 / 64, None, ALU.mult, partial=True)
                P.rsqrt(stat[:, :, 1:2], stat[:, :, 1:2], 64e-5, partial=True)
                P.tt(ybuf[:, :, :], ybuf[:, :, :], stat[:, :, 1:2].bc([128, NCH, 64]), ALU.mult)
                P.tt(ybuf[:, :, :], ybuf[:, :, :], W['lnwb'][:, hp, :].re("p (o v) -> p o v", o=1).bc([128, NCH, 64]), ALU.mult)
                P.tt(ybuf[:, :, :], ybuf[:, :, :], W['lnbb'][:, hp, :].re("p (o v) -> p o v", o=1).bc([128, NCH, 64]), ALU.add)
                for hf_ in range(2):
                    hs = slice(hf_ * HC, (hf_ + 1) * HC)
                    P.tt(tmpR[:, :, :], vst[:, hs, :], bon[:, hs, :].bc([128, HC, 64]), ALU.mult)
                    P.tt(ybuf[:, hs, :], ybuf[:, hs, :], tmpR[:, :, :], ALU.add, partial=True)
                    for hpar in range(2):
                        ps_ = slice(hpar * 64, hpar * 64 + 64)
                        P.tt(uob[ps_, :, hpar * 64:hpar * 64 + 64], ybuf[ps_, hs, :], gt_[ps_, hs, :], ALU.mult, partial=True)
                    for c in range(HC):
                        pt = pp[6 + c % 2]
                        P.mm(pt[:, 0:64], uob[:, c, :], idst[:, :])
                        P.copy(uT_sb[:, c * 64:(c + 1) * 64], pt[:, 0:64], partial=True, eng=('act' if c % 2 else 'dve'))
                    P.dma('sp', S['uT'][e0:e0 + 128, hf_ * (T // 2):(hf_ + 1) * (T // 2)], uT_sb[:, :])
        P.barrier()


def hgrn_layer(P, G, l, W):
    S = G.scr
    with ExitStack() as st:
        hT = P.sb([128, 8, T], BF16, st, "hT")
        norm_phase(P, G, hT)
        wb = [P.sb([128, 8, 512], BF16, st, "wb") for _ in range(2)]
        stg = [P.sb([128, 512], F32, st, "stg") for _ in range(4)]
        cnt = [0, 0]
        silu_post = lambda s_, pt_, r0_, ne_: P.act(s_, pt_, AF.Silu)
        proj_fm(P, G, hT, W['w_in'], 0, E, S['rT'], 0, wb, stg, cnt, post=silu_post)
        proj_fm(P, G, hT, W['w_in'], E, 2 * E, S['aT'], 0, wb, stg, cnt)
        proj_tm(P, G, hT, W['w_in'], 3 * E, E, S['vtok'], wb, stg, cnt)
        proj_tm(P, G, hT, W['w_in'], 4 * E, E, S['gtok'], wb, stg, cnt, func=AF.Silu)
        P.barrier()
    if CUT <= 1:
        return
    with ExitStack() as st:
        NSC = T // 128
        lg = P.sb([128, 4, 16], F32, st, "lg")
        P.dma('sp', lg[:, :, :], W['lbl'][:, :, :], partial=False)
        P.act(lg[:, :, :], lg[:, :, :], AF.Exp)
        ssum = P.sb([128, 16], F32, st, "ssum")
        lb = P.sb([128, 16], F32, st, "lb")
        oml = P.sb([128, 16], F32, st, "oml")
        P.tt(ssum[:, :], lg[:, 0, :], lg[:, 1, :], ALU.add)
        P.tt(ssum[:, :], ssum[:, :], lg[:, 2, :], ALU.add)
        P.tt(ssum[:, :], ssum[:, :], lg[:, 3, :], ALU.add)
        lo_, hi_ = 1, l
        P.copy(lb[:, :], lg[:, 1, :])
        for i_ in range(2, l + 1):
            P.tt(lb[:, :], lb[:, :], lg[:, i_, :], ALU.add)
        P.op('dve', lambda e: e.reciprocal(out=ssum[:, :].ap, in_=ssum[:, :].ap), [ssum[:, :]], [ssum[:, :]])
        P.tt(lb[:, :], lb[:, :], ssum[:, :], ALU.mult)
        P.ts(oml[:, :], lb[:, :], -1.0, 1.0, ALU.mult, ALU.add)
        vt = P.sb([128, NSC, 128], BF16, st, "vt")
        gt_ = P.sb([128, NSC, 128], BF16, st, "gt_")
        obuf = P.sb([128, NSC, 128], F32, st, "obuf")
        tmpR = P.sb([128, NSC, 128], F32, st, "tmpR")
        stat = P.sb([128, NSC, 1], F32, st, "stat")
        ub = P.sb([128, NSC, 128], BF16, st, "ub")
        uT_sb = P.sb([128, T], BF16, st, "uT_sb")
        ones512 = P.sb([128, 256], F32, st, "ones256")
        P.memset(ones512[:, :], 1.0)
        pp = G.psum
        CH = []
        for z in range(2):
            C = Ctx()
            C.wk = [P.sb([128, 256], F32, st, "wk") for _ in range(10)]
            C.Qe = P.sb([128, 2, 640], BF16, st, "Qe")
            C.Ke = P.sb([128, 2, 640], BF16, st, "Ke")
            C.Qp = P.sb([128, 2, 128], BF16, st, "Qp")
            C.Kp = P.sb([128, 2, 128], BF16, st, "Kp")
            P.memset(C.Qe[:, :, :], 0.0, eng='pool')
            P.memset(C.Ke[:, :, :], 0.0, eng='pool')
            C.cend = P.sb([128, 8], F32, st, "cend")
            C.At = P.sb([128, 128], BF16, st, "At")
            C.KeT = P.sb([128, 512], BF16, st, "KeT")
            C.Sf = P.sb([128, 128], F32, st, "Sf")
            C.Sb = [P.sb([128, 128], BF16, st, "Sb") for _ in range(2)]
            C.tmpS = P.sb([128, 128], F32, st, "tmpS")
            C.banks = (pp[3 * z], pp[3 * z + 1], pp[3 * z + 2])
            CH.append(C)
        STILES = [(256 * i, 256) for i in range(T // 256)]

        def chain(h, z, C):
            e0 = h * 128
            q, fp, f, lf, Gc, Ec, t1, eg, ei, kc = C.wk
            Qe, Ke, Qp, Kp, cend, At, KeT, Sf, Sb, tmpS = C.Qe, C.Ke, C.Qp, C.Kp, C.cend, C.At, C.KeT, C.Sf, C.Sb, C.tmpS
            bD, bA, bS = C.banks
            tiles = list(STILES) if z == 0 else [STILES[0]] + list(STILES[:0:-1])
            P.memset(Sf[:, :], 0.0); yield
            P.memset(Sb[0][:, :], 0.0); yield
            sbi = 0
            for (n0, nt) in tiles:
                nsc = nt // 128
                ncn = nt // 32
                P.dma('sp', q[:, :nt], S['rT'][e0:e0 + 128, n0:n0 + nt], partial=False); yield
                P.dma('sp', fp[:, :nt], S['aT'][z * E + e0:z * E + e0 + 128, n0:n0 + nt], partial=False); yield
                P.act(f[:, :nt], fp[:, :nt], AF.Sigmoid); yield
                P.ts(f[:, :nt], f[:, :nt], oml[:, h:h + 1], lb[:, h:h + 1], ALU.mult, ALU.add); yield
                P.act(lf[:, :nt], f[:, :nt], AF.Ln); yield
                P.ts(kc[:, :nt], f[:, :nt], -1.0, 1.0, ALU.mult, ALU.add); yield
                P.scan(Gc[:, :nt], ones512[:, :nt], lf[:, :nt], 0.0, ALU.mult, ALU.add); yield
                P.tt(Ec[:, :nt], Gc[:, :nt], lf[:, :nt], ALU.subtract); yield
                v3 = lambda b_: b_[:, :nt].re("p (c s) -> p c s", s=32)
                G3, E3, t13 = v3(Gc), v3(Ec), v3(t1)
                if z == 0:
                    P.tt(t13, G3, E3[:, :, 0:1].bc([128, ncn, 32]), ALU.subtract); yield
                else:
                    P.tt(t13, G3[:, :, 31:32].bc([128, ncn, 32]), E3, ALU.subtract); yield
                ce3 = cend[:, :ncn].re("p (c o) -> p c o", o=1)
                P.tt(ce3, G3[:, :, 31:32], E3[:, :, 0:1], ALU.subtract); yield
                P.act(cend[:, :ncn], cend[:, :ncn], AF.Exp); yield
                P.act(eg[:, :nt], t1[:, :nt], AF.Exp); yield
                P.act(ei[:, :nt], t1[:, :nt], AF.Exp, scale=-1.0); yield
                v128 = lambda b_: b_[:, :nt].re("p (a s) -> p a s", s=128)
                P.tt(Qp[:, :nsc, :], v128(q), v128(eg), ALU.mult); yield
                P.tt(Kp[:, :nsc, :], v128(kc), v128(ei), ALU.mult); yield
                v4 = lambda b_: b_[:, :nt].re("p (a j s) -> p a j s", j=4, s=32)
                P.tt(Qe[:, :nsc, :].re("p a (j x) -> p a j x", x=160)[:, :, :, 0:32], v4(q), v4(eg), ALU.mult, partial=True); yield
                P.tt(Ke[:, :nsc, :].re("p a (j x) -> p a j x", x=160)[:, :, :, 0:32], v4(kc), v4(ei), ALU.mult, partial=True); yield
                sc_order = range(nsc) if z == 0 else range(nsc - 1, -1, -1)
                for a_ in sc_order:
                    g = n0 // 128 + a_
                    P.mm(bD[:, 0:128], Kp[:, a_, :], Qp[:, a_, :]); yield
                    for j in range(4):
                        P.mm(bA[:, j * 128:(j + 1) * 128], Ke[:, a_, j * 128:(j + 1) * 128], G.idnb[:, :]); yield
                    P.tt(At[:, :], bD[:, 0:128], G.mask32[:, z, :], ALU.mult); yield
                    P.copy(KeT[:, :], bA[:, :], eng='act'); yield
                    P.mm(bD[:, 128:256], At[:, :], vt[:, g, :], start=True, stop=False); yield
                    jorder = list(range(4)) if z == 0 else [3, 2, 1, 0]
                    for jj, j in enumerate(jorder):
                        P.mm(bD[:, 128:256], Qe[:, a_, j * 128:(j + 1) * 128], Sb[sbi][:, :], start=False, stop=(jj == 3)); yield
                        ps_ = bS[:, 128 * (jj % 2):128 + 128 * (jj % 2)]
                        P.mm(ps_, KeT[:, j * 128:(j + 1) * 128], vt[:, g, :]); yield
                        P.tt(tmpS[:, :], ps_, Sf[:, :], ALU.add); yield
                        cc = cend[:, a_ * 4 + j:a_ * 4 + j + 1]
                        P.ts(Sf[:, :], tmpS[:, :], cc, None, ALU.mult); yield
                        sbi = 1 - sbi
                        P.act(Sb[sbi][:, :], tmpS[:, :], AF.Identity, scale=cc); yield
                    P.tt(obuf[:, g, :], obuf[:, g, :], bD[:, 128:256], ALU.add, partial=True); yield

        for h in range(KHP):
            e0 = h * 128
            for (n0, nt) in TILES:
                P.dma('pool', vt[:, n0 // 128:(n0 + nt) // 128, :], S['vtok'][n0:n0 + nt, e0:e0 + 128].re("(c s) v -> s c v", s=128), partial=(n0 > 0))
                P.dma('pool', gt_[:, n0 // 128:(n0 + nt) // 128, :], S['gtok'][n0:n0 + nt, e0:e0 + 128].re("(c s) v -> s c v", s=128), partial=(n0 > 0))
            P.memset(obuf[:, :, :], 0.0)
            gens = [chain(h, z, CH[z]) for z in range(2)]
            for _ in range(14):
                next(gens[1])
            while gens:
                for g_ in list(gens):
                    try:
                        next(g_)
                    except StopIteration:
                        gens.remove(g_)
            P.tt(tmpR[:, :, :], obuf[:, :, :], obuf[:, :, :], ALU.mult)
            P.op('dve', lambda e: e.tensor_reduce(out=stat[:, :, 0:1].ap, in_=tmpR[:, :, :].ap, axis=mybir.AxisListType.X, op=ALU.add),
                 [tmpR[:, :, :]], [stat[:, :, 0:1]])
            P.ts(stat[:, :, :], stat[:, :, :], 1.0 / 128, None, ALU.mult)
            P.rsqrt(stat[:, :, :], stat[:, :, :], EPS)
            P.tt(obuf[:, :, :], obuf[:, :, :], stat[:, :, 0:1].bc([128, NSC, 128]), ALU.mult)
            P.tt(obuf[:, :, :], obuf[:, :, :], W['gnb'][:, :].re("p (o v) -> p o v", o=1).bc([128, NSC, 128]), ALU.mult)
            P.tt(ub[:, :, :], obuf[:, :, :], gt_[:, :, :], ALU.mult)
            for g in range(NSC):
                pt = pp[6 + g % 2]
                P.mm(pt[:, 0:128], ub[:, g, :], G.idnb[:, :])
                P.copy(uT_sb[:, g * 128:(g + 1) * 128], pt[:, 0:128], partial=True)
            P.dma('sp', S['uT'][e0:e0 + 128, :], uT_sb[:, :])
        P.barrier()


TWO_PI = 2.0 * math.pi


def hyena_tables(P, G, L, cosb, sinb):
    nb = L // 128
    N = 2 * L
    with ExitStack() as st:
        arg = P.sb([128, nb, 128], F32, st, "arg")
        m = P.sb([128, nb, 128], F32, st, "m")
        ob = [P.sb([128, nb, 128], BF16, st, "ob") for _ in range(2)]
        fr = P.sb([128, 128], F32, st, "fr")
        ki = P.sb([128, nb, 128], I32, st, "ki")
        for fb in range(nb):
            P.ts(fr[:, :], G.jrow[:, :], float(128 * fb), None, ALU.add)
            P.tt(arg[:, :, :], G.tcol[:, :nb].re("p (c o) -> p c o", o=1).bc([128, nb, 128]),
                 fr[:, :].re("p (o j) -> p o j", o=1).bc([128, nb, 128]), ALU.mult)
            for kind, (off, dst) in enumerate(((N / 4, cosb), (0.0, sinb))):
                P.ts(m[:, :, :], arg[:, :, :], float(off), 1.0 / N, ALU.add, ALU.mult)
                P.copy(ki[:, :, :], m[:, :, :])
                P.tt(m[:, :, :], m[:, :, :], ki[:, :, :], ALU.subtract)
                P.act(ob[kind][:, :, :], m[:, :, :], AF.Sin, scale=TWO_PI)
                P.dma('sp', dst[fb], ob[kind][:, :, :])
        P.barrier()


def hyena_filters(P, G, W, L, zT_d, tnneg, ksum_d, kdiff_d):
    nb = L // 128
    with ExitStack() as st:
        zT = P.sb([33, L], F32, st, "zT")
        P.dma('sp', zT[:, :], zT_d[:, :], partial=False)
        ha = P.sb([64, L], F32, st, "ha")
        hb_ = P.sb([64, L], F32, st, "hb")
        tmp = P.sb([64, 512], F32, st, "ftmp")
        kif = P.sb([64, 512], I32, st, "kif")
        plan = [(zT, 33, W['f_w1'], 0, ha), (ha, 64, W['f_w2'], 1, hb_), (hb_, 64, W['f_w3'], 2, ha)]
        for (src, kd, wm, bi, dst) in plan:
            for t0 in range(0, L, 512):
                nt = min(512, L - t0)
                pt = G.psum[(t0 // 512) % 2]
                P.mm(pt[0:64, :nt], wm[0:kd, :], src[0:kd, t0:t0 + nt])
                P.ts(tmp[:, :nt], pt[0:64, :nt], W['fb'][:, bi:bi + 1], W['sf'][:, 0:1], ALU.add, ALU.mult)
                P.ts(tmp[:, :nt], tmp[:, :nt], 1.0 / TWO_PI, None, ALU.mult)
                P.copy(kif[:, :nt], tmp[:, :nt])
                P.tt(tmp[:, :nt], tmp[:, :nt], kif[:, :nt], ALU.subtract)
                P.act(dst[:, t0:t0 + nt], tmp[:, :nt], AF.Sin, scale=TWO_PI, partial=True)
        h3 = ha
        w4 = P.sb([64, 2 * E], F32, st, "w4")
        P.dma('sp', w4[:, :], W['f_w4'][:, :], partial=False)
        win = P.sb([128, 512], F32, st, "win")
        hf = P.sb([128, 512], F32, st, "hf")
        hk = P.sb([128, 512], F32, st, "hk")
        sk = [P.sb([128, 512], BF16, st, "sk") for _ in range(2)]
        dk = [P.sb([128, 512], BF16, st, "dk") for _ in range(2)]
        i = 0
        for tb in range(nb):
            for ebk in range(4):
                pf = G.psum[0]
                pb = G.psum[1]
                P.mm(pf[:, :], h3[:, tb * 128:(tb + 1) * 128], w4[:, ebk * 512:(ebk + 1) * 512])
                P.mm(pb[:, :], h3[:, tb * 128:(tb + 1) * 128], w4[:, E + ebk * 512:E + (ebk + 1) * 512])
                P.act(win[:, :], G.deltab[:, ebk * 512:(ebk + 1) * 512], AF.Exp, scale=tnneg[:, tb:tb + 1])
                P.tt(hf[:, :], pf[:, :], win[:, :], ALU.mult)
                P.tt(hk[:, :], pb[:, :], win[:, :], ALU.mult)
                P.tt(sk[i % 2][:, :], hf[:, :], hk[:, :], ALU.add)
                P.tt(dk[i % 2][:, :], hf[:, :], hk[:, :], ALU.subtract)
                P.dma('sp', ksum_d[tb * 128:(tb + 1) * 128, ebk * 512:(ebk + 1) * 512], sk[i % 2][:, :])
                P.dma('sp', kdiff_d[tb * 128:(tb + 1) * 128, ebk * 512:(ebk + 1) * 512], dk[i % 2][:, :])
                i += 1
        P.barrier()


def hyena_dft(P, G, S, L, n_off, cosb, sinb, ksum_d, kdiff_d):
    nb = L // 128
    N = 2 * L
    pp = G.psum
    for unit in range(4):
        c0 = unit * 512
        with ExitStack() as st0:
            KY = P.sb([128, nb, 2, 512], BF16, st0, "KY")
            kn = P.sb([2, 512], F32, st0, "kn")
            ynb = P.sb([2, 512], BF16, st0, "ynb")
            with ExitStack() as st:
                ta = P.sb([128, nb, 512], BF16, st, "ta")
                tb_ = P.sb([128, nb, 512], BF16, st, "tb")
                slab = [[P.sb([128, nb, 128], BF16, st, "slab") for _ in range(2)] for _ in range(2)]
                P.dma('sp', ta[:, :, :], ksum_d[:, c0:c0 + 512].re("(c p) e -> p c e", p=128), partial=False)
                P.dma('sp', tb_[:, :, :], kdiff_d[:, c0:c0 + 512].re("(c p) e -> p c e", p=128), partial=False)
                for fb in range(nb):
                    cs, ss = slab[0][fb % 2], slab[1][fb % 2]
                    P.dma('sp', cs[:, :, :], cosb[fb], partial=False)
                    P.dma('sp', ss[:, :, :], sinb[fb], partial=False)
                    for tc in range(nb):
                        P.mm(pp[0][:, :], cs[:, tc, :], ta[:, tc, :], start=(tc == 0), stop=(tc == nb - 1))
                    for tc in range(nb):
                        P.mm(pp[1][:, :], ss[:, tc, :], tb_[:, tc, :], start=(tc == 0), stop=(tc == nb - 1))
                    P.copy(KY[:, fb, 0, :], pp[0][:, :], partial=True)
                    P.copy(KY[:, fb, 1, :], pp[1][:, :], partial=True)
                for tc in range(nb):
                    P.mm(pp[4][0:2, :], G.altb[:, :], ta[:, tc, :], start=(tc == 0), stop=(tc == nb - 1))
                P.copy(kn[:, :], pp[4][0:2, :])
                P.barrier()
            with ExitStack() as st:
                ta = P.sb([128, nb, 512], BF16, st, "ta")
                slab = [[P.sb([128, nb, 128], BF16, st, "slab") for _ in range(2)] for _ in range(2)]
                t = [P.sb([128, 512], F32, st, "t") for _ in range(4)]
                P.dma('sp', ta[:, :, :], S['utok'][n_off:n_off + L, c0:c0 + 512].re("(c p) e -> p c e", p=128), partial=False)
                for fb in range(nb):
                    cs, ss = slab[0][fb % 2], slab[1][fb % 2]
                    P.dma('sp', cs[:, :, :], cosb[fb], partial=False)
                    P.dma('sp', ss[:, :, :], sinb[fb], partial=False)
                    for tc in range(nb):
                        P.mm(pp[2][:, :], cs[:, tc, :], ta[:, tc, :], start=(tc == 0), stop=(tc == nb - 1))
                    for tc in range(nb):
                        P.mm(pp[3][:, :], ss[:, tc, :], ta[:, tc, :], start=(tc == 0), stop=(tc == nb - 1))
                    P.tt(t[0][:, :], pp[2][:, :], KY[:, fb, 0, :], ALU.mult)
                    P.tt(t[1][:, :], pp[3][:, :], KY[:, fb, 1, :], ALU.mult)
                    P.tt(t[2][:, :], pp[2][:, :], KY[:, fb, 1, :], ALU.mult)
                    P.tt(t[3][:, :], pp[3][:, :], KY[:, fb, 0, :], ALU.mult)
                    P.tt(KY[:, fb, 0, :], t[0][:, :], t[1][:, :], ALU.subtract, partial=True)
                    P.tt(KY[:, fb, 1, :], t[2][:, :], t[3][:, :], ALU.add, partial=True)
                    if fb == 0:
                        P.ts(KY[0:1, 0, 0, :], KY[0:1, 0, 0, :], 0.5, None, ALU.mult, partial=True)
                for tc in range(nb):
                    P.mm(pp[4][0:2, :], G.altb[:, :], ta[:, tc, :], start=(tc == 0), stop=(tc == nb - 1))
                P.stt(ynb[:, :], pp[4][0:2, :], 0.5, kn[:, :], ALU.mult, ALU.mult)
                P.barrier()
            with ExitStack() as st:
                slab = [[P.sb([128, nb, 128], BF16, st, "slab") for _ in range(2)] for _ in range(2)]
                yst = [P.sb([128, 4, 128], F32, st, "yst") for _ in range(2)]
                k = 0
                for tbk in range(nb):
                    cs, ss = slab[0][tbk % 2], slab[1][tbk % 2]
                    P.dma('sp', cs[:, :, :], cosb[tbk], partial=False)
                    P.dma('sp', ss[:, :, :], sinb[tbk], partial=False)
                    ys = yst[tbk % 2]
                    for eb in range(4):
                        po = pp[5 + k % 3]
                        k += 1
                        for fc in range(nb):
                            P.mm(po[:, 0:128], KY[:, fc, 0, eb * 128:(eb + 1) * 128], cs[:, fc, :], start=(fc == 0), stop=False)
                            P.mm(po[:, 0:128], KY[:, fc, 1, eb * 128:(eb + 1) * 128], ss[:, fc, :], start=False, stop=False)
                        P.mm(po[:, 0:128], ynb[0:1, eb * 128:(eb + 1) * 128], G.altrow[0:1, :], start=False, stop=True)
                        P.ts(ys[:, eb, :], po[:, 0:128], 2.0 / N, None, ALU.mult, partial=(eb > 0))
                    P.dma('sp', S['aT'][c0:c0 + 512, n_off + tbk * 128:n_off + (tbk + 1) * 128].re("(a p) t -> p a t", p=128), ys[:, :, :])
                P.barrier()


def hyena_layer(P, G, l, W):
    S = G.scr
    hyena_tables(P, G, LX, S['cosx'], S['sinx'])
    hyena_tables(P, G, LC, S['cosc'], S['sinc'])
    if CUT <= 1:
        return
    hyena_filters(P, G, W, LX, W['zTx'], W['tnx'], S['ksx'], S['kdx'])
    hyena_filters(P, G, W, LC, W['zTc'], W['tnc'], S['ksc'], S['kdc'])
    if CUT <= 2:
        return
    with ExitStack() as st:
        hT = P.sb([128, 8, T], BF16, st, "hT")
        norm_phase(P, G, hT)
        wb = [P.sb([128, 8, 512], BF16, st, "wb") for _ in range(2)]
        stg = [P.sb([128, 512], F32, st, "stg") for _ in range(4)]
        cnt = [0, 0]
        silu_post = lambda s_, pt_, r0_, ne_: P.act(s_, pt_, AF.Silu)
        proj_fm(P, G, hT, W['w_in'], 0, E, S['rT'], 0, wb, stg, cnt)
        proj_fm(P, G, hT, W['w_in'], E, E, S['kT'], 0, wb, stg, cnt)
        proj_fm(P, G, hT, W['w_in'], 2 * E, E, S['lwT'], 0, wb, stg, cnt)
        proj_fm(P, G, hT, W['w_in'], 3 * E, E, S['lwT'], E, wb, stg, cnt, post=silu_post)
        P.barrier()
    if CUT <= 3:
        return
    with ExitStack() as st:
        p = P.sb([128, T], F32, st, "p")
        sa = P.sb([128, T], F32, st, "sa")
        sb_ = P.sb([128, T], F32, st, "sb")
        ubf = P.sb([128, T], BF16, st, "ubf")
        stgb = [P.sb([128, 4, 128], BF16, st, "stgb") for _ in range(2)]
        cw, cb = W['cw'], W['cb']

        def conv(dst, src, blk):
            P.dma('sp', p[:, :], src, partial=False)
            P.ts(dst[:, :], p[:, :], cw[:, 1, blk:blk + 1], cb[:, blk:blk + 1], ALU.mult, ALU.add)
            for (a, b) in ((0, LC), (LC, T)):
                P.stt(dst[:, a + 1:b], p[:, a:b - 1], cw[:, 0, blk:blk + 1], dst[:, a + 1:b], ALU.mult, ALU.add, partial=True)
                P.stt(dst[:, a:b - 1], p[:, a + 1:b], cw[:, 2, blk:blk + 1], dst[:, a:b - 1], ALU.mult, ALU.add, partial=True)
        for eb in range(16):
            e0 = eb * 128
            conv(sa, S['kT'][e0:e0 + 128, :], 16 + eb)
            conv(sb_, S['lwT'][e0:e0 + 128, :], 32 + eb)
            P.tt(sa[:, :], sa[:, :], sb_[:, :], ALU.mult)
            P.dma('sp', S['kT'][e0:e0 + 128, :], sa[:, :])
            P.copy(ubf[:, :], sa[:, :], eng='act')
            for gi, g4 in enumerate(range(0, T // 128, 4)):
                n4 = min(4, T // 128 - g4)
                pt = G.psum[gi % 2]
                for j in range(n4):
                    P.mm(pt[:, j * 128:(j + 1) * 128], ubf[:, (g4 + j) * 128:(g4 + j + 1) * 128], G.idnb[:, :])
                sg = stgb[gi % 2]
                P.copy(sg[:, :n4, :], pt[:, :n4 * 128].re("p (a e) -> p a e", e=128))
                P.dma('sp', S['utok'][g4 * 128:(g4 + n4) * 128, e0:e0 + 128].re("(a p) e -> p a e", p=128), sg[:, :n4, :])
            conv(sa, S['rT'][e0:e0 + 128, :], eb)
            P.dma('sp', p[:, :], S['lwT'][E + e0:E + e0 + 128, :], partial=False)
            P.tt(sa[:, :], sa[:, :], p[:, :], ALU.mult)
            P.dma('sp', S['rT'][e0:e0 + 128, :], sa[:, :])
        P.barrier()
    if CUT <= 4:
        return
    hyena_dft(P, G, S, LC, 0, S['cosc'], S['sinc'], S['ksc'], S['kdc'])
    if CUT <= 5:
        return
    hyena_dft(P, G, S, LX, LC, S['cosx'], S['sinx'], S['ksx'], S['kdx'])
    with ExitStack() as st:
        y = P.sb([128, T], F32, st, "y")
        u = P.sb([128, T], F32, st, "u")
        gx = P.sb([128, T], F32, st, "gx")
        ob = [P.sb([128, T], BF16, st, "ob") for _ in range(2)]
        for eb in range(16):
            e0 = eb * 128
            P.dma('sp', y[:, :], S['aT'][e0:e0 + 128, :], partial=False)
            P.dma('sp', u[:, :], S['kT'][e0:e0 + 128, :], partial=False)
            P.dma('sp', gx[:, :], S['rT'][e0:e0 + 128, :], partial=False)
            P.stt(y[:, :], u[:, :], W['fbias'][:, eb:eb + 1], y[:, :], ALU.mult, ALU.add)
            P.tt(ob[eb % 2][:, :], y[:, :], gx[:, :], ALU.mult)
            P.dma('sp', S['uT'][e0:e0 + 128, :], ob[eb % 2][:, :])
        P.barrier()


def build(layers=(0, 1, 2, 3)):
    if isinstance(layers, int):
        layers = tuple(range(layers))
    nc = bass.Bass("TRN2", target_bir_lowering=False)
    P = Prog(nc)
    G = Ctx()
    G.xs_in = P.dram("xT0", [D, T], F32, "ExternalInput")
    G.cond = P.dram("cond", [128, 8, 2], F32, "ExternalInput")
    G.ada_w = P.dram("ada_w", [4, D, 3 * D], F32, "ExternalInput")
    G.w_out = P.dram("w_out", [4, E, D], F32, "ExternalInput")
    G.out = P.dram("outT", [D, LX], F32, "ExternalOutput")
    G.xs = P.dram("xs", [D, T], F32)
    S = {}
    S['rT'] = P.dram("s_rT", [E, T], F32)
    S['kT'] = P.dram("s_kT", [E, T], F32)
    S['vtok'] = P.dram("s_vtok", [T, E], F32)
    if 3 in layers and 0 not in layers:
        S['vfirst'] = P.dram("vfirst_in", [T, E], F32, "ExternalInput")
    else:
        S['vfirst'] = P.dram("s_vfirst", [T, E], F32)
    S['gtok'] = P.dram("s_gtok", [T, E], F32)
    S['loT'] = P.dram("s_loT", [256, T], F32)
    S['lovT'] = P.dram("s_lovT", [32, T], F32)
    S['uT'] = P.dram("s_uT", [E, T], BF16)
    S['aT'] = P.dram("s_aT", [2 * E, T], F32)
    S['lwT'] = P.dram("s_lwT", [2 * E, T], F32)
    G.scr = S
    st = P.stack

    def cin(name, shape, dt=F32):
        d = P.dram(name, shape, dt, "ExternalInput")
        b = P.sb(shape, dt, st, name)
        if len(shape) == 2:
            P.dma('sp', b[:, :], d[:, :], partial=False)
        else:
            P.dma('sp', b[:, :, :], d[:, :, :], partial=False)
        return b
    G.masks = cin("masks", [128, 2, 256])
    G.maskt = cin("maskt", [128, 2, 128])
    idn = cin("idn", [128, 128])
    G.blk = cin("blk", [128, 128])
    G.adab = cin("adab", [128, 96])
    G.npre = cin("npre", [128, 32])
    G.npost = cin("npost", [128, 32])
    G.idnb = P.sb([128, 128], BF16, st, "idnb")
    P.copy(G.idnb[:, :], idn[:, :])
    G.ones = P.sb([128, 128], F32, st, "ones")
    P.memset(G.ones[:, :], 1.0)
    G.psum = [P.ps([128, 512], F32, st, "ps") for _ in range(8)]
    G.A = P.sb([128, 8, 2], F32, st, "A")
    G.B = P.sb([128, 8, 2], F32, st, "B")
    G.Gt = P.sb([128, 8, 2], F32, st, "Gt")
    G.scT = P.sb([128, 8, 2], F32, st, "scT")
    P.dma('sp', G.scT[:, :, :], G.cond[:, :, :], partial=False)
    P.act(G.scT[:, :, :], G.scT[:, :, :], AF.Silu)
    for (n0, nt) in TILES:
        P.dma('sp', G.xs[:, n0:n0 + nt], G.xs_in[:, n0:n0 + nt])
    LW = {}
    for l in (0, 3):
        if l not in layers:
            continue
        W = {}
        pre = "l%d_" % l
        ncols = 4 * E + 256 + (32 if l == 3 else 0)
        W['w_in'] = P.dram(pre + "w_in", [D, ncols], F32, "ExternalInput")
        W['mu'] = cin(pre + "mu", [128, 48])
        W['om'] = P.sb([128, 48], F32, st, pre + "om")
        P.ts(W['om'][:, :], W['mu'][:, :], -1.0, 1.0, ALU.mult, ALU.add)
        W['w0'] = cin(pre + "w0", [128, 32])
        W['a0'] = cin(pre + "a0", [128, 32])
        W['w2'] = P.dram(pre + "w2", [128, E], F32, "ExternalInput")
        W['a2'] = P.dram(pre + "a2", [128, E], F32, "ExternalInput")
        W['k_k'] = cin(pre + "k_k", [128, 16])
        W['k_a'] = cin(pre + "k_a", [128, 16])
        W['r_k'] = cin(pre + "r_k", [128, 16])
        W['lnwb'] = cin(pre + "lnw", [128, 16, 64])
        W['lnbb'] = cin(pre + "lnb", [128, 16, 64])
        if l == 3:
            W['v0'] = P.dram(pre + "v0", [1, E], F32, "ExternalInput")
            W['v2'] = P.dram(pre + "v2", [32, E], F32, "ExternalInput")
        LW[l] = W
    if 1 in layers:
        W = {}
        W['w_in'] = P.dram("l1_w_in", [D, 4 * E], F32, "ExternalInput")
        W['cw'] = cin("l1_cw", [128, 3, 48])
        W['cb'] = cin("l1_cb", [128, 48])
        W['fbias'] = cin("l1_fbias", [128, 16])
        W['f_w1'] = cin("l1_f_w1", [33, 64])
        W['f_w2'] = cin("l1_f_w2", [64, 64])
        W['f_w3'] = cin("l1_f_w3", [64, 64])
        W['f_w4'] = P.dram("l1_f_w4", [64, 2 * E], F32, "ExternalInput")
        W['fb'] = cin("l1_fb", [64, 3])
        W['sf'] = cin("l1_sf", [64, 1])
        W['zTx'] = P.dram("zTx", [33, LX], F32, "ExternalInput")
        W['zTc'] = P.dram("zTc", [33, LC], F32, "ExternalInput")
        W['tnx'] = cin("tnx", [128, LX // 128])
        W['tnc'] = cin("tnc", [128, LC // 128])
        G.deltab = cin("deltab", [128, E])
        G.jrow = cin("jrow", [128, 128])
        G.tcol = cin("tcol", [128, 32])
        altf = cin("altf", [128, 2])
        G.altb = P.sb([128, 2], BF16, st, "altb")
        P.copy(G.altb[:, :], altf[:, :])
        altrf = cin("altrf", [1, 128])
        G.altrow = P.sb([1, 128], BF16, st, "altrow")
        P.copy(G.altrow[:, :], altrf[:, :])
        S['cosx'] = P.dram("s_cosx", [LX // 128, 128, LX // 128, 128], BF16)
        S['sinx'] = P.dram("s_sinx", [LX // 128, 128, LX // 128, 128], BF16)
        S['cosc'] = P.dram("s_cosc", [LC // 128, 128, LC // 128, 128], BF16)
        S['sinc'] = P.dram("s_sinc", [LC // 128, 128, LC // 128, 128], BF16)
        S['ksx'] = P.dram("s_ksx", [LX, E], BF16)
        S['kdx'] = P.dram("s_kdx", [LX, E], BF16)
        S['ksc'] = P.dram("s_ksc", [LC, E], BF16)
        S['kdc'] = P.dram("s_kdc", [LC, E], BF16)
        S['utok'] = P.dram("s_utok", [T, E], BF16)
        LW[1] = W
    if 2 in layers:
        W = {}
        W['w_in'] = P.dram("l2_w_in", [D, 5 * E], F32, "ExternalInput")
        W['lbl'] = P.dram("l2_lbl", [128, 4, 16], F32, "ExternalInput")
        W['gnb'] = cin("l2_gnb", [128, 128])
        G.mask32 = cin("mask32", [128, 2, 128])
        LW[2] = W
    P.barrier()
    for l in layers:
        adaln_phase(P, G, l)
        if l == 2:
            hgrn_layer(P, G, l, LW[l])
        if l == 1:
            hyena_layer(P, G, l, LW[l])
        if l in (0, 3):
            rwkv_layer(P, G, l, LW[l], vres=(l == 3))
            if l == 0:
                for tb in range(T // 512 + 1):
                    a0_, a1_ = tb * 512, min(T, tb * 512 + 512)
                    P.dma('sp', S['vfirst'][a0_:a1_, :], S['vtok'][a0_:a1_, :])
                P.barrier()
        outproj_phase(P, G, l, S['uT'])
    if KDBG:
        dbg = P.dram("dbg_uT", [E, T], BF16, "ExternalOutput")
        for i in range(16):
            P.dma('sp', dbg[i * 128:(i + 1) * 128, :], S['uT'][i * 128:(i + 1) * 128, :])
    for i in range(8):
        P.dma('sp', G.out[:, i * 512:(i + 1) * 512], G.xs[:, LC + i * 512:LC + (i + 1) * 512])
    P.barrier()
    return nc, P


def prep_inputs(inp, b, layers=(0, 1, 2, 3)):
    m = {}
    m['xT0'] = np.ascontiguousarray(np.concatenate([inp['ctx'][b], inp['x'][b]], axis=0).T)
    cond = np.stack([inp['c'][b], inp['c_ctx']], axis=-1)
    m['cond'] = np.ascontiguousarray(cond.reshape(8, 128, 2).transpose(1, 0, 2))
    m['ada_w'] = inp['ada_w']
    m['w_out'] = inp['w_out']
    m['adab'] = col_layout(inp['ada_b'].reshape(-1))
    m['npre'] = col_layout(inp['norm_pre'].reshape(-1))
    m['npost'] = col_layout(inp['norm_post'].reshape(-1))
    c = make_consts()
    m['masks'] = np.ascontiguousarray(c['masks'].transpose(1, 0, 2))
    m['maskt'] = np.ascontiguousarray(c['maskt'].transpose(1, 0, 2))
    m['idn'] = c['idn']
    m['blk'] = c['blk']
    if 1 in layers:
        m['l1_w_in'] = inp['l1_w_in']
        m['l1_cw'] = np.ascontiguousarray(inp['l1_conv_w'].reshape(3, 48, 128).transpose(2, 0, 1))
        m['l1_cb'] = col_layout(inp['l1_conv_b'])
        m['l1_fbias'] = col_layout(inp['l1_filter_bias'])
        m['l1_f_w1'] = inp['l1_f_w1']
        m['l1_f_w2'] = inp['l1_f_w2']
        m['l1_f_w3'] = inp['l1_f_w3']
        m['l1_f_w4'] = inp['l1_f_w4']
        m['l1_fb'] = np.ascontiguousarray(np.stack([inp['l1_f_b1'], inp['l1_f_b2'], inp['l1_f_b3']], axis=1))
        m['l1_sf'] = np.ascontiguousarray(inp['l1_sin_freq'].reshape(64, 1))
        m.update(hyena_consts())
    if 2 in layers:
        m['l2_w_in'] = inp['l2_w_in']
        m['l2_lbl'] = np.ascontiguousarray(inp['hgrn_lb_logits'].reshape(4, 16, 128).transpose(2, 0, 1))
        m['l2_gnb'] = np.ascontiguousarray(np.tile(inp['l2_g_norm'][None, :], (128, 1)))
        s_ = np.arange(128)
        same = (s_[:, None] // 32) == (s_[None, :] // 32)
        m32 = np.zeros((128, 2, 128), np.float32)
        m32[:, 0, :] = same & (s_[:, None] <= s_[None, :])
        m32[:, 1, :] = same & (s_[:, None] >= s_[None, :])
        m['mask32'] = m32
    for l in (0, 3):
        if l not in layers:
            continue
        pre = "l%d_" % l
        m[pre + 'w_in'] = inp[pre + 'w_in']
        m[pre + 'mu'] = col_layout(inp[pre + 'mu'].reshape(-1))
        m[pre + 'w0'] = col_layout(inp[pre + 'w0'].reshape(-1))
        m[pre + 'a0'] = col_layout(inp[pre + 'a0'].reshape(-1))
        m[pre + 'w2'] = np.ascontiguousarray(inp[pre + 'w2'].reshape(128, E))
        m[pre + 'a2'] = np.ascontiguousarray(inp[pre + 'a2'].reshape(128, E))
        m[pre + 'k_k'] = col_layout(inp[pre + 'k_k'])
        m[pre + 'k_a'] = col_layout(inp[pre + 'k_a'])
        m[pre + 'r_k'] = col_layout(inp[pre + 'r_k'].reshape(-1))
        lw = inp[pre + 'ln_w'].reshape(16, 2, 64)
        lb = inp[pre + 'ln_b'].reshape(16, 2, 64)
        m[pre + 'lnw'] = np.ascontiguousarray(np.repeat(lw.transpose(1, 0, 2), 64, axis=0))
        m[pre + 'lnb'] = np.ascontiguousarray(np.repeat(lb.transpose(1, 0, 2), 64, axis=0))
        if l == 3:
            m[pre + 'v0'] = inp[pre + 'v0'].reshape(1, E)
            m[pre + 'v2'] = inp[pre + 'v2']
    return m


_CACHE = {}


def kernel(**inputs):
    inp = {k: np.asarray(v) for k, v in inputs.items()}
    if 'nc' not in _CACHE:
        _CACHE['nc'] = build((0, 1, 2, 3))[0]
    nc = _CACHE['nc']
    in_maps = [prep_inputs(inp, b) for b in range(NC8)]
    res = run_bass_kernel_spmd(nc, in_maps, core_ids=list(range(NC8)))
    out = np.stack([np.ascontiguousarray(res.results[b]["outT"].T) for b in range(NC8)], axis=0)
    return out.astype(np.float32)
```
